# Optimizing a Trainium2 kernel written in Bass

```python
import math
import jax, jax.numpy as jnp
from jax import lax
import numpy as np

D_MODEL = 1024
BATCH = 2
SEQ = 16384
DEPTH = 4

GRID_W = 64
CTX_LEN = 256
HEAD_DIM = 64
ROPE_THETA = 10000.0
Q_BLOCK = 128
EPS = 1e-6
N_MOD = 6
DIFF_HEADS = 4
DIFF_QK_DIM = HEAD_DIM
DIFF_V_DIM = 2 * HEAD_DIM
GQA_Q_HEADS = 8
GQA_KV_HEADS = 2
GQA_GROUP = GQA_Q_HEADS // GQA_KV_HEADS
ATTN_IN = DIFF_HEADS * (4 * DIFF_QK_DIM + DIFF_V_DIM) + (GQA_Q_HEADS + 2 * GQA_KV_HEADS) * HEAD_DIM
ATTN_OUT = DIFF_HEADS * DIFF_V_DIM + GQA_Q_HEADS * HEAD_DIM
SGU_DIM = D_MODEL
SGU_GROUPS = 4
SGU_CHUNK = 128
FFN_DIM = 2816
CONV_W = 3
N_EVEN = (DEPTH + 1) // 2
N_ODD = DEPTH // 2

kernel_name = "hybrid_diffattn_gqa_sgu_convffn_dit"


def rms_norm(x, g):
    xf = x.astype(jnp.float32)
    y = xf * lax.rsqrt(jnp.mean(xf * xf, axis=-1, keepdims=True) + EPS)
    return (y * g.astype(jnp.float32)).astype(x.dtype)


def modulate(h, shift, scale):
    return h * (1 + scale) + shift


def axial_rope_tables(rows):
    n_freq = HEAD_DIM // 4
    inv = ROPE_THETA ** (-jnp.arange(n_freq, dtype=jnp.float32) / n_freq)
    row = jnp.repeat(jnp.arange(rows, dtype=jnp.float32), GRID_W)
    col = jnp.tile(jnp.arange(GRID_W, dtype=jnp.float32), rows)
    ang = jnp.concatenate([row[:, None] * inv, col[:, None] * inv], axis=-1)
    return jnp.cos(ang), jnp.sin(ang)


def apply_rope(x, cos, sin):
    xf = x.astype(jnp.float32).reshape(x.shape[:-1] + (x.shape[-1] // 2, 2))
    x0, x1 = xf[..., 0], xf[..., 1]
    c = cos[None, :, None, :]
    s = sin[None, :, None, :]
    out = jnp.stack([x0 * c - x1 * s, x0 * s + x1 * c], axis=-1).reshape(x.shape)
    return out.astype(x.dtype)


def attn_project(h, w_in, q_norm, k_norm, rope):
    B, S, _ = h.shape
    qk_a = DIFF_HEADS * DIFF_QK_DIM
    sizes = [qk_a, qk_a, qk_a, qk_a, DIFF_HEADS * DIFF_V_DIM,
             GQA_Q_HEADS * HEAD_DIM, GQA_KV_HEADS * HEAD_DIM, GQA_KV_HEADS * HEAD_DIM]
    splits = [int(v) for v in np.cumsum(sizes)[:-1]]
    q1, q2, k1, k2, va, qb, kb, vb = jnp.split(h @ w_in, splits, axis=-1)
    heads = lambda t, n: t.reshape(B, S, n, -1)
    q1, q2, k1, k2, va = (heads(t, DIFF_HEADS) for t in (q1, q2, k1, k2, va))
    qb = rms_norm(heads(qb, GQA_Q_HEADS), q_norm)
    kb = rms_norm(heads(kb, GQA_KV_HEADS), k_norm)
    vb = heads(vb, GQA_KV_HEADS)
    if rope is not None:
        cos, sin = rope
        q1, q2, k1, k2, qb, kb = (apply_rope(t, cos, sin) for t in (q1, q2, k1, k2, qb, kb))
    return q1, q2, qb, k1, k2, va, kb, vb


def diff_attend(q1, q2, k1, k2, v, lam):
    scale = DIFF_QK_DIM ** -0.5
    p1 = jax.nn.softmax(jnp.einsum('bqhd,bkhd->bhqk', q1, k1).astype(jnp.float32) * scale, axis=-1)
    p2 = jax.nn.softmax(jnp.einsum('bqhd,bkhd->bhqk', q2, k2).astype(jnp.float32) * scale, axis=-1)
    p = (p1 - lam * p2).astype(v.dtype)
    return jnp.einsum('bhqk,bkhv->bqhv', p, v)


def gqa_attend(q, k, v):
    B, Q, H, d = q.shape
    qg = q.reshape(B, Q, GQA_KV_HEADS, GQA_GROUP, d)
    s = jnp.einsum('bqgrd,bkgd->bgrqk', qg, k).astype(jnp.float32) * (d ** -0.5)
    p = jax.nn.softmax(s, axis=-1).astype(v.dtype)
    return jnp.einsum('bgrqk,bkgd->bqgrd', p, v).reshape(B, Q, H, d)


def attn_heads(q1, q2, qb, k1, k2, va, kb, vb, lam, lam_init, subln):
    B, Q = q1.shape[:2]
    oa = rms_norm(diff_attend(q1, q2, k1, k2, va, lam), subln) * (1 - lam_init)
    ob = gqa_attend(qb, kb, vb)
    return jnp.concatenate([oa.reshape(B, Q, -1), ob.reshape(B, Q, -1)], axis=-1)


def sweep_query_blocks(fn, qs):
    B, S = qs[0].shape[:2]
    nb = S // Q_BLOCK
    blocks = tuple(jnp.moveaxis(q.reshape((B, nb, Q_BLOCK) + q.shape[2:]), 1, 0) for q in qs)
    out = lax.map(lambda qb: fn(*qb), blocks)
    return jnp.moveaxis(out, 0, 1).reshape(B, S, out.shape[-1])


def attention_mixer(h_lat, h_ctx, w_in, w_out, lq1, lk1, lq2, lk2, subln, q_norm, k_norm,
                    lam_init, rope, with_ctx_out):
    lat = attn_project(h_lat, w_in, q_norm, k_norm, rope)
    ctx = attn_project(h_ctx, w_in, q_norm, k_norm, None)
    lam = (jnp.exp(jnp.sum(lq1.astype(jnp.float32) * lk1.astype(jnp.float32)))
           - jnp.exp(jnp.sum(lq2.astype(jnp.float32) * lk2.astype(jnp.float32))) + lam_init)
    K1, K2, VA, KB, VB = (jnp.concatenate([ctx[i], lat[i]], axis=1) for i in range(3, 8))
    lat_fn = lambda q1b, q2b, qbb: attn_heads(q1b, q2b, qbb, K1, K2, VA, KB, VB, lam, lam_init, subln)
    y_lat = sweep_query_blocks(lat_fn, lat[:3]) @ w_out
    y_ctx = None
    if with_ctx_out:
        y_ctx = attn_heads(*ctx, lam, lam_init, subln) @ w_out
    return y_lat, y_ctx


def sgu_mixer(h, w_in, v_norm, w_s, b_s, w_out):
    B, S, _ = h.shape
    z = jax.nn.gelu(h @ w_in, approximate=False)
    u, v = jnp.split(z, 2, axis=-1)
    v = rms_norm(v, v_norm)
    n = S // SGU_CHUNK
    vg = v.reshape(B, n, SGU_CHUNK, SGU_GROUPS, SGU_DIM // SGU_GROUPS)
    mixed = jnp.einsum('gpq,bnqgc->bnpgc', w_s, vg) + b_s.T[:, :, None]
    return (u * mixed.reshape(B, S, SGU_DIM)) @ w_out


def conv_ffn(h, w_up, conv_w, conv_b, w_down):
    S = h.shape[1]
    z = h @ w_up
    zp = jnp.pad(z, ((0, 0), (1, 1), (0, 0)))
    z = conv_w[0] * zp[:, :S] + conv_w[1] * zp[:, 1:S + 1] + conv_w[2] * zp[:, 2:] + conv_b
    g, u = jnp.split(z, 2, axis=-1)
    return (jax.nn.silu(g) * u) @ w_down


def setup_inputs(seed: int = 0) -> dict:
    key = jax.random.key(seed)
    ks = iter(jax.random.split(key, 32))
    D = D_MODEL
    nrm = lambda shape, s: jax.random.normal(next(ks), shape, jnp.float32) * s
    return {
        "x": nrm((BATCH, SEQ, D), 1.0),
        "c": nrm((BATCH, D), 1.0),
        "ctx": nrm((BATCH, CTX_LEN, D), 1.0),
        "c_ctx": nrm((D,), 1.0),
        "ada_w": nrm((DEPTH, D, N_MOD * D), 0.5 * D ** -0.5),
        "ada_b": nrm((DEPTH, N_MOD * D), 0.02),
        "mix_norm": 1.0 + nrm((DEPTH, D), 0.1),
        "ffn_norm": 1.0 + nrm((DEPTH, D), 0.1),
        "final_norm": 1.0 + nrm((D,), 0.1),
        "attn_w_in": nrm((N_EVEN, D, ATTN_IN), D ** -0.5),
        "attn_w_out": nrm((N_EVEN, ATTN_OUT, D), ATTN_OUT ** -0.5),
        "diff_lq1": nrm((N_EVEN, DIFF_QK_DIM), 0.1),
        "diff_lk1": nrm((N_EVEN, DIFF_QK_DIM), 0.1),
        "diff_lq2": nrm((N_EVEN, DIFF_QK_DIM), 0.1),
        "diff_lk2": nrm((N_EVEN, DIFF_QK_DIM), 0.1),
        "diff_subln": 1.0 + nrm((N_EVEN, DIFF_V_DIM), 0.1),
        "gqa_q_norm": 1.0 + nrm((N_EVEN, HEAD_DIM), 0.1),
        "gqa_k_norm": 1.0 + nrm((N_EVEN, HEAD_DIM), 0.1),
        "sgu_w_in": nrm((N_ODD, D, 2 * SGU_DIM), D ** -0.5),
        "sgu_v_norm": 1.0 + nrm((N_ODD, SGU_DIM), 0.1),
        "sgu_w_s": nrm((N_ODD, SGU_GROUPS, SGU_CHUNK, SGU_CHUNK), SGU_CHUNK ** -0.5),
        "sgu_b_s": 1.0 + nrm((N_ODD, SGU_GROUPS, SGU_CHUNK), 0.1),
        "sgu_w_out": nrm((N_ODD, SGU_DIM, D), SGU_DIM ** -0.5),
        "ffn_w_up": nrm((DEPTH, D, 2 * FFN_DIM), D ** -0.5),
        "ffn_conv_w": nrm((DEPTH, CONV_W, 2 * FFN_DIM), CONV_W ** -0.5),
        "ffn_conv_b": nrm((DEPTH, 2 * FFN_DIM), 0.02),
        "ffn_w_down": nrm((DEPTH, FFN_DIM, D), FFN_DIM ** -0.5),
    }


def reference(x, c, ctx, c_ctx, ada_w, ada_b, mix_norm, ffn_norm, final_norm,
              attn_w_in, attn_w_out, diff_lq1, diff_lk1, diff_lq2, diff_lk2, diff_subln,
              gqa_q_norm, gqa_k_norm, sgu_w_in, sgu_v_norm, sgu_w_s, sgu_b_s, sgu_w_out,
              ffn_w_up, ffn_conv_w, ffn_conv_b, ffn_w_down):
    rows = x.shape[1] // GRID_W
    rope = axial_rope_tables(rows)
    last_attn = (DEPTH - 1) // 2 * 2
    s_c = jax.nn.silu(c)
    s_cc = jax.nn.silu(c_ctx)
    for l in range(DEPTH):
        i = l // 2
        is_attn = l % 2 == 0
        update_ctx = l < last_attn
        m_lat = (s_c @ ada_w[l] + ada_b[l])[:, None, :]
        sh1, sc1, g1, sh2, sc2, g2 = jnp.split(m_lat, N_MOD, axis=-1)
        h_lat = modulate(rms_norm(x, mix_norm[l]), sh1, sc1)
        if is_attn or update_ctx:
            m_ctx = s_cc @ ada_w[l] + ada_b[l]
            csh1, csc1, cg1, csh2, csc2, cg2 = jnp.split(m_ctx, N_MOD, axis=-1)
            h_ctx = modulate(rms_norm(ctx, mix_norm[l]), csh1, csc1)
        if is_attn:
            lam_init = 0.8 - 0.6 * math.exp(-0.3 * l)
            y_lat, y_ctx = attention_mixer(h_lat, h_ctx, attn_w_in[i], attn_w_out[i],
                                           diff_lq1[i], diff_lk1[i], diff_lq2[i], diff_lk2[i],
                                           diff_subln[i], gqa_q_norm[i], gqa_k_norm[i],
                                           lam_init, rope, update_ctx)
        else:
            y_lat = sgu_mixer(h_lat, sgu_w_in[i], sgu_v_norm[i], sgu_w_s[i], sgu_b_s[i], sgu_w_out[i])
            if update_ctx:
                y_ctx = sgu_mixer(h_ctx, sgu_w_in[i], sgu_v_norm[i], sgu_w_s[i], sgu_b_s[i], sgu_w_out[i])
        x = x + g1 * y_lat
        x = x + g2 * conv_ffn(modulate(rms_norm(x, ffn_norm[l]), sh2, sc2),
                              ffn_w_up[l], ffn_conv_w[l], ffn_conv_b[l], ffn_w_down[l])
        if update_ctx:
            ctx = ctx + cg1 * y_ctx
            ctx = ctx + cg2 * conv_ffn(modulate(rms_norm(ctx, ffn_norm[l]), csh2, csc2),
                                       ffn_w_up[l], ffn_conv_w[l], ffn_conv_b[l], ffn_w_down[l])
    return rms_norm(x, final_norm)
```

```python
import math
from contextlib import ExitStack

import numpy as np
import concourse.bass as bass
import concourse.mybir as mybir
from concourse.bass_utils import run_bass_kernel_spmd

F32 = mybir.dt.float32
BF16 = mybir.dt.bfloat16
AF = mybir.ActivationFunctionType
ALU = mybir.AluOpType

D = 1024
NCH = 8
CTX = 256
FF = 2816
NPAIR = 22
EPS = 1e-6
DEPTH = 4
ENGS = ("pe", "act", "dve", "pool", "sp")


class Res:
    __slots__ = ("w", "rc", "rd")

    def __init__(self):
        self.w = None
        self.rc = {}
        self.rd = []


class Op:
    __slots__ = ("eng", "kind", "fn", "deps", "idx", "needed", "count", "sem", "target")


class Prog:
    NS = 8

    def __init__(self, nc, es):
        self.nc = nc
        self.ops = {e: [] for e in ENGS}
        self.bar = {e: [] for e in ENGS}
        self.dma_n = {q: 0 for q in ("sp", "act", "pool")}
        self.dsem = {q: [es.enter_context(nc.semaphore(f"d_{q}_{i}")) for i in range(self.NS)]
                     for q in ("sp", "act", "pool")}
        self.csem = {e: es.enter_context(nc.semaphore(f"c_{e}")) for e in ("pe", "act", "dve", "pool")}
        self.es = es
        self.live_dma = []
        self.ncc = 0

    def _add(self, eng, kind, fn, r, w):
        o = Op()
        o.eng, o.kind, o.fn, o.needed, o.count, o.sem, o.target = eng, kind, fn, False, 0, None, 0
        o.idx = len(self.ops[eng])
        deps = {}

        def dep(x, why):
            if x is None or x is o:
                return
            k = id(x)
            if k in deps:
                if why == "raw":
                    deps[k] = (x, why)
                return
            deps[k] = (x, why)

        for x in r:
            dep(x.w, "raw")
        for x in w:
            dep(x.w, "waw")
            for rd in x.rc.values():
                dep(rd, "war")
            for rd in x.rd:
                dep(rd, "war")
        for x in self.bar[eng]:
            dep(x, "raw")
        self.bar[eng] = []
        o.deps = list(deps.values())
        for (x, why) in o.deps:
            if x.kind == "c":
                if x.eng != eng or kind != "c":
                    x.needed = True
                elif eng != "pe" and why == "raw" and (o.idx - x.idx) <= 2:
                    x.needed = True
        self.ops[eng].append(o)
        for x in r:
            if kind == "c":
                x.rc[eng] = o
            else:
                x.rd.append(o)
        for x in w:
            x.w = o
            x.rc = {}
            x.rd = []
        return o

    def op(self, eng, fn, r=(), w=()):
        return self._add(eng, "c", fn, r, w)

    def dma(self, q, fn, r=(), w=()):
        o = self._add(q, "d", fn, r, w)
        n = self.dma_n[q]
        self.dma_n[q] = n + 1
        o.sem = self.dsem[q][n % self.NS]
        o.target = 16 * (n // self.NS + 1)
        self.live_dma.append(o)
        return o

    def cc(self, fn, r=(), w=()):
        import os
        if os.environ.get("NO_CC"):
            return self._add("pool", "c", lambda e: e.engine_nop(), r, w)
        if not hasattr(self, "cc_chain"):
            self.cc_chain = Res()
        o = self._add("pool", "cc", fn, r, list(w) + [self.cc_chain])
        o.sem = self.es.enter_context(self.nc.semaphore(f"cc_{self.ncc}"))
        self.ncc += 1
        o.target = 1
        self.live_dma.append(o)
        return o

    def barrier(self):
        last = {}
        for e in ("pe", "act", "dve", "pool"):
            for o in reversed(self.ops[e]):
                if o.kind == "c":
                    last[e] = o
                    break
        for e in ENGS:
            self.bar[e] = [o for (k, o) in last.items() if k != e] + list(self.live_dma)
        self.live_dma = []

    def finish(self):
        self.barrier()
        self._add("sp", "nop", None, (), ())

    def emit(self):
        nc = self.nc
        for e in ("pe", "act", "dve", "pool"):
            c = 0
            for o in self.ops[e]:
                if o.kind == "c" and o.needed:
                    c += 1
                o.count = c if o.kind == "c" else 0
        prog = self

        def run(E, eng):
            seen = {}

            def wait(sem, val):
                k = id(sem)
                if seen.get(k, 0) >= val:
                    return
                eng.wait_ge(sem, val)
                seen[k] = val

            for o in prog.ops[E]:
                if o.kind == "d" and o.target > 16:
                    wait(o.sem, o.target - 16)
                for (x, why) in o.deps:
                    if x.kind == "c":
                        if x.eng == E and o.kind == "c":
                            if E == "pe" or why != "raw" or (o.idx - x.idx) > 2:
                                continue
                        wait(prog.csem[x.eng], x.count)
                    elif x.kind in ("d", "cc"):
                        wait(x.sem, x.target)
                if o.fn is None:
                    continue
                ins = o.fn(eng)
                if o.kind == "c":
                    if o.needed:
                        ins.then_inc(prog.csem[E], 1)
                elif o.kind == "d":
                    ins.then_inc(o.sem, 16)
                elif o.kind == "cc":
                    ins.then_inc(o.sem)

        with nc.Block() as block:
            @block.tensor
            def _(e):
                run("pe", e)

            @block.scalar
            def _(e):
                run("act", e)

            @block.vector
            def _(e):
                run("dve", e)

            @block.gpsimd
            def _(e):
                run("pool", e)

            @block.sync
            def _(e):
                run("sp", e)


def MM(out, lhsT, rhs, start=True, stop=True):
    return lambda e: e.matmul(out, lhsT, rhs, start=start, stop=stop)


def ACT(out, in_, func, bias=0.0, scale=1.0):
    return lambda e: e.activation(out, in_, func, bias=bias, scale=scale)


def TT(out, a, b, op):
    return lambda e: e.tensor_tensor(out, a, b, op)


def TS(out, a, s1, s2, op0, op1=None):
    if op1 is None:
        return lambda e: e.tensor_scalar(out, a, s1, None, op0)
    return lambda e: e.tensor_scalar(out, a, s1, s2, op0, op1)


def STT(out, a, s, b, op0, op1):
    return lambda e: e.scalar_tensor_tensor(out, a, s, b, op0, op1)


def CP(out, a):
    return lambda e: e.tensor_copy(out, a)


def RCP(out, a):
    return lambda e: e.reciprocal(out, a)


def MS(ap, v):
    return lambda e: e.memset(ap, v)


def DMA(out, in_, **kw):
    return lambda e: e.dma_start(out=out, in_=in_, **kw)


class Ring:
    def __init__(self, aps):
        self.aps = aps
        self.res = [Res() for _ in aps]
        self.i = 0

    def next(self):
        k = self.i % len(self.aps)
        self.i += 1
        return self.aps[k], self.res[k]


def build(TPC, NL=DEPTH):
    S = 4 * TPC
    NB = TPC // 512
    NKT = (CTX + S) // 128
    NLT = TPC // 128
    nc = bass.Bass("TRN2", target_bir_lowering=False)
    es = ExitStack()

    def din(name, shape, dt=F32):
        return nc.dram_tensor(name, list(shape), dt, kind="ExternalInput")

    def dscr(name, shape, dt):
        return nc.dram_tensor(name, list(shape), dt)

    xT_in = din("xT", [D, TPC])
    cT_in = din("cT", [D, CTX])
    cvec_in = din("cvec", [128, NCH, 2])
    adaw_in = din("ada_w", [DEPTH, D, 6 * D])
    adab_in = din("ada_b", [128, DEPTH, 48])
    mixn_in = din("mixn", [128, DEPTH, NCH])
    ffnn_in = din("ffnn", [128, DEPTH, NCH])
    finn_in = din("finn", [128, NCH])
    wqk_in = din("wqk", [2, D, 1664])
    wqs_in = din("wqs", [2, D, 1664])
    wv_in = din("wv", [2, D, 640])
    ggain_in = din("ggain", [128, 2, 5, 2])
    wod_in = din("wod", [2, 512, D])
    wog_in = din("wog", [2, 64, 8, D])
    lqk_in = din("lqk", [1, 2, 4, 64])
    subln_in = din("subln", [128, 2])
    ropeC_in = din("ropeC", [128, TPC])
    ropeS_in = din("ropeS", [128, TPC])
    swu_in = din("swu", [2, D, D])
    swv_in = din("swv", [2, D, D])
    svn_in = din("svn", [128, 2, NCH])
    swsT_in = din("swsT", [2, 128, 4, 128])
    sbsb_in = din("sbsb", [128, 2, 4, 128])
    swo_in = din("swo", [2, D, D])
    wup_in = din("wup", [DEPTH, D, NPAIR * 256])
    convp_in = din("convp", [128, DEPTH, 44, 4])
    wdn_in = din("wdn", [DEPTH, FF, D])
    flags_in = din("flags", [128, 2])
    sel_in = din("sel", [8, 2])
    yT_out = nc.dram_tensor("yT", [D, TPC], F32, kind="ExternalOutput")

    xw0 = dscr("xw0", [D, TPC + 2], F32)
    xw1 = dscr("xw1", [D, TPC + 2], F32)
    cw = dscr("cw", [D, CTX], F32)
    wqk_b = dscr("wqk_b", [2, D, 1664], BF16)
    wqs_b = dscr("wqs_b", [2, D, 1664], BF16)
    wv_b = dscr("wv_b", [2, D, 640], BF16)
    wod_b = dscr("wod_b", [2, 512, D], BF16)
    wog_b = dscr("wog_b", [2, 64, 8, D], BF16)
    swu_b = dscr("swu_b", [2, D, D], BF16)
    swv_b = dscr("swv_b", [2, D, D], BF16)
    swsT_b = dscr("swsT_b", [2, 128, 4, 128], BF16)
    swo_b = dscr("swo_b", [2, D, D], BF16)
    wup_b = dscr("wup_b", [DEPTH, D, NPAIR * 256], BF16)
    wdn_b = dscr("wdn_b", [DEPTH, FF, D], BF16)
    Qs = dscr("Qs", [8, 128, TPC], BF16)
    Qc = dscr("Qc", [8, 128, CTX], BF16)
    Klp = [dscr(f"Kl{p}", [64, TPC], BF16) for p in range(10)]
    Vlp = [dscr(f"Vl{p}", [64, TPC], BF16) for p in range(10)]
    Kap = [dscr(f"Ka{p}", [4 * 64, TPC], BF16) for p in range(10)]
    Vap = [dscr(f"Va{p}", [4 * 64, TPC], BF16) for p in range(10)]
    Kc = dscr("Kc", [640, CTX], BF16)
    Vc = dscr("Vc", [640, CTX], BF16)
    AT = dscr("AT", [12, 128, TPC], BF16)
    ATc = dscr("ATc", [12, 128, CTX], BF16)
    HLl = dscr("HLl", [2, D], F32)
    HLa = dscr("HLa", [8, D], F32)

    P = Prog(nc, es)

    R_xws = [[Res() for _ in range(NB)] for _ in range(2)]
    R_xhs = [Res(), Res()]
    cur = {"i": 0}
    R_cw = Res()
    R_w = {}

    uid = [0]

    def un(name):
        uid[0] += 1
        return f"{name}_u{uid[0]}"

    def sb(name, shape, dt):
        return es.enter_context(nc.sbuf_tensor(un(name), list(shape), dt))

    ones_bf = sb("ones_bf", [128, 128], BF16)
    bd_bf = sb("bd_bf", [128, 128], BF16)
    ones_f = sb("ones_f", [128, 128], F32)
    cvec = sb("cvec_s", [128, NCH, 2], F32)
    modraw = sb("modraw", [128, DEPTH, 48, 2], F32)
    adab = sb("adab_s", [128, DEPTH, 48], F32)
    mixn = sb("mixn_s", [128, DEPTH, NCH], F32)
    ffnn = sb("ffnn_s", [128, DEPTH, NCH], F32)
    finn = sb("finn_s", [128, NCH], F32)
    modv = sb("modv", [128, DEPTH, 6, 2, NCH], F32)
    ggain = sb("ggain_s", [128, 2, 5, 2], F32)
    subln = sb("subln_s", [128, 2], F32)
    lqk = sb("lqk_s", [1, 2, 4, 64], F32)
    lamt = sb("lamt", [1, 2, 8], F32)
    svn = sb("svn_s", [128, 2, NCH], F32)
    sbsb = sb("sbsb_s", [128, 2, 4, 128], F32)
    convp = sb("convp_s", [128, DEPTH, 44, 4], F32)
    flags = sb("flags_s", [128, 2], F32)
    sel = sb("sel_s", [8, 2], F32)
    R_const = Res()

    consts = [(cvec, cvec_in), (adab, adab_in), (mixn, mixn_in), (ffnn, ffnn_in), (finn, finn_in),
              (ggain, ggain_in), (subln, subln_in), (lqk, lqk_in), (svn, svn_in), (sbsb, sbsb_in),
              (convp, convp_in), (flags, flags_in), (sel, sel_in)]
    R_cs = []
    for (t, src) in consts:
        r_ = Res()
        R_cs.append(r_)
        P.dma("sp", DMA(t[:], src.ap()), w=[r_])
    R_ones = Res()
    P.op("pool", MS(ones_bf[:], 1.0), w=[R_ones])
    P.op("pool", MS(ones_f[:], 1.0), w=[R_ones])
    P.op("pool", MS(bd_bf[:], 0.0), w=[R_ones])
    P.op("pool", MS(bd_bf[0:64, 0:64], 1.0), w=[R_ones])
    P.op("pool", MS(bd_bf[64:128, 64:128], 1.0), w=[R_ones])

    def conv_w(key, src_ap, dst_ap, n_el):
        r_ = Res()
        R_w[key] = r_
        rows = n_el // 1024
        s2 = src_ap
        d2 = dst_ap
        step = 8192
        for r0 in range(0, rows, step):
            r1 = min(rows, r0 + step)
            P.dma("pool", DMA(d2[r0:r1, :], s2[r0:r1, :]), w=[r_])

    def flat2(h, idx, n_el):
        ap = h.ap()[idx]
        names = " ".join(f"d{i}" for i in range(len(ap.shape)))
        ap = ap.rearrange(f"{names} -> ({names})")
        return ap.rearrange("(r c) -> r c", c=1024)

    def conv_layer_weights(l):
        i = l // 2
        if l % 2 == 0:
            for (nm, src, dst, n) in (("wqk", wqk_in, wqk_b, D * 1664), ("wqs", wqs_in, wqs_b, D * 1664),
                                      ("wv", wv_in, wv_b, D * 640), ("wod", wod_in, wod_b, 512 * D),
                                      ("wog", wog_in, wog_b, 64 * 8 * D)):
                conv_w((nm, i), flat2(src, i, n), flat2(dst, i, n), n)
        else:
            for (nm, src, dst, n) in (("swu", swu_in, swu_b, D * D), ("swv", swv_in, swv_b, D * D),
                                      ("swsT", swsT_in, swsT_b, 128 * 4 * 128), ("swo", swo_in, swo_b, D * D)):
                conv_w((nm, i), flat2(src, i, n), flat2(dst, i, n), n)
        conv_w(("wup", l), flat2(wup_in, l, D * NPAIR * 256), flat2(wup_b, l, D * NPAIR * 256), D * NPAIR * 256)
        conv_w(("wdn", l), flat2(wdn_in, l, FF * D), flat2(wdn_b, l, FF * D), FF * D)

    for l in range(NL):
        conv_layer_weights(l)

    xw_fms = [xw0.ap().rearrange("(c p) t -> p c t", p=128), xw1.ap().rearrange("(c p) t -> p c t", p=128)]
    cw_fm = cw.ap().rearrange("(c p) t -> p c t", p=128)
    for b in range(NB):
        P.dma("sp", DMA(xw0.ap()[:, 1 + b * 512:1 + (b + 1) * 512], xT_in.ap()[:, b * 512:(b + 1) * 512]),
              w=[R_xws[0][b]])
    P.dma("sp", DMA(cw.ap(), cT_in.ap()), w=[R_cw])

    with ExitStack() as ph:
        def sbp(name, shape, dt):
            return ph.enter_context(nc.sbuf_tensor(un(name), list(shape), dt))

        def psp(name, shape, dt=F32):
            return ph.enter_context(nc.psum_tensor(un(name), list(shape), dt))

        scv = sbp("scv", [128, NCH, 2], F32)
        R_scv = Res()
        P.op("act", ACT(scv[:], cvec[:], AF.Silu), r=[R_cs[0]], w=[R_scv])
        wst = Ring([sbp(f"adaw{i}", [128, NCH, 768], F32) for i in range(2)])
        pm = Ring([psp(f"pm{i}", [128, 512]) for i in range(2)])
        for l in range(NL):
            for cg in range(8):
                wt, wr = wst.next()
                P.dma("sp", DMA(wt[:], adaw_in.ap()[l].rearrange("(k p) n -> p k n", p=128)[:, :, cg * 768:(cg + 1) * 768]),
                      w=[wr])
                pt, pr = pm.next()
                for j in range(6):
                    for k in range(NCH):
                        P.op("pe", MM(pt[:, 2 * j:2 * j + 2], wt[:, k, j * 128:(j + 1) * 128], scv[:, k, :],
                                      start=(k == 0), stop=(k == NCH - 1)), r=[wr, R_scv], w=[pr])
                for s_ in range(2):
                    P.op("dve", TT(modraw[:, l, cg * 6:(cg + 1) * 6, s_], pt[:, s_:12:2],
                                   adab[:, l, cg * 6:(cg + 1) * 6], ALU.add), r=[pr, R_cs[1]], w=[R_const])
        for l in range(NL):
            for s_ in range(2):
                mr = lambda m: modraw[:, l, m * 8:(m + 1) * 8, s_]
                P.op("dve", STT(modv[:, l, 0, s_, :], mr(1), 1.0, mixn[:, l, :], ALU.add, ALU.mult),
                     r=[R_const, R_cs[2]], w=[R_const])
                P.op("dve", CP(modv[:, l, 1, s_, :], mr(0)), r=[R_const], w=[R_const])
                P.op("dve", CP(modv[:, l, 2, s_, :], mr(2)), r=[R_const], w=[R_const])
                P.op("dve", STT(modv[:, l, 3, s_, :], mr(4), 1.0, ffnn[:, l, :], ALU.add, ALU.mult),
                     r=[R_const, R_cs[3]], w=[R_const])
                P.op("dve", CP(modv[:, l, 4, s_, :], mr(3)), r=[R_const], w=[R_const])
                P.op("dve", CP(modv[:, l, 5, s_, :], mr(5)), r=[R_const], w=[R_const])
        P.barrier()

    def norm_mod(env, xt, rx, N, A, Bv, h, rh, flag=None):
        pt, pr = env["ps"].next()
        for c in range(NCH):
            sq, rs = env["sq"].next()
            P.op("act", ACT(sq[:, :N], xt[:, c, :N], AF.Square), r=[rx], w=[rs])
            P.op("pe", MM(pt[:, :N], ones_bf[:], sq[:, :N], start=(c == 0), stop=(c == NCH - 1)),
                 r=[rs, R_ones], w=[pr])
        rt, rr = env["rstd"].next()
        P.op("act", ACT(rt[:, :N], pt[:, :N], AF.Sqrt, bias=env["eps"][:, 0:1], scale=1.0 / D), r=[pr, env["reps"]], w=[rr])
        P.op("dve", RCP(rt[:, :N], rt[:, :N]), r=[rr], w=[rr])
        for c in range(NCH):
            tt, tr = env["tmp"].next()
            P.op("dve", TT(tt[:, :N], xt[:, c, :N], rt[:, :N], ALU.mult), r=[rx, rr], w=[tr])
            if Bv is not None:
                P.op("act", ACT(h[:, c, :N], tt[:, :N], AF.Identity, bias=Bv[:, c:c + 1], scale=A[:, c:c + 1]),
                     r=[tr, R_const], w=[rh])
            else:
                P.op("act", ACT(h[:, c, :N], tt[:, :N], AF.Identity, scale=A[:, c:c + 1]), r=[tr, R_const], w=[rh])
            if flag is not None:
                P.op("dve", TS(h[:, c, :N], h[:, c, :N], flag, None, ALU.mult), r=[rh, R_cs[11]], w=[rh])

    epsT = sb("epsT", [128, 1], F32)
    R_eps = Res()
    P.op("pool", MS(epsT[:], EPS), w=[R_eps])

    def make_env(ph, tag, ps_n=1):
        def sbp(name, shape, dt):
            return ph.enter_context(nc.sbuf_tensor(un(name), list(shape), dt))
        return {
            "sq": Ring([sbp(f"{tag}_sq{i}", [128, 512], BF16) for i in range(3)]),
            "tmp": Ring([sbp(f"{tag}_tmp{i}", [128, 512], F32) for i in range(3)]),
            "rstd": Ring([sbp(f"{tag}_rstd{i}", [128, 512], F32) for i in range(2)]),
            "ps": Ring([ph.enter_context(nc.psum_tensor(un(f"{tag}_nps{i}"), [128, 512], F32)) for i in range(ps_n)]),
            "eps": epsT, "reps": R_eps,
        }

    def attention_layer(l):
        i = l // 2
        xw_fm, R_xw = xw_fms[cur["i"]], R_xws[cur["i"]]
        with_ctx_out = l < 2
        lam_init = 0.8 - 0.6 * math.exp(-0.3 * l)
        R_Qs = [[Res() for _ in range(NB)] for _ in range(8)]
        R_Qc = [Res() for _ in range(8)]
        R_Kl = [Res() for _ in range(10)]
        R_Vl = [Res() for _ in range(10)]
        R_Ka = [Res() for _ in range(10)]
        R_Va = [Res() for _ in range(10)]
        R_Kc, R_Vc = Res(), Res()
        R_AT = [[Res() for _ in range(NB)] for _ in range(12)]
        R_ATc = [Res() for _ in range(12)]

        R_lam = Res()
        with ExitStack() as ph:
            prod = ph.enter_context(nc.sbuf_tensor(un("lprod"), [1, 2, 64], F32))
            rp = Res()
            P.op("dve", TT(prod[:, 0, :], lqk[:, i, 0, :], lqk[:, i, 1, :], ALU.mult), r=[R_cs[7]], w=[rp])
            P.op("dve", TT(prod[:, 1, :], lqk[:, i, 2, :], lqk[:, i, 3, :], ALU.mult), r=[R_cs[7]], w=[rp])
            P.op("dve", lambda e: e.reduce_sum(lamt[:, i, 0:2], prod[:], mybir.AxisListType.X), r=[rp], w=[R_lam])
            P.op("act", ACT(lamt[:, i, 2:4], lamt[:, i, 0:2], AF.Exp), r=[R_lam], w=[R_lam])
            P.op("dve", TT(lamt[:, i, 4:5], lamt[:, i, 3:4], lamt[:, i, 2:3], ALU.subtract), r=[R_lam], w=[R_lam])
            P.op("dve", TS(lamt[:, i, 5:6], lamt[:, i, 4:5], -lam_init, None, ALU.add), r=[R_lam], w=[R_lam])
            P.barrier()
        nlam = lamt[0:1, i, 5:6]

        with ExitStack() as ph:
            def sbp(name, shape, dt):
                return ph.enter_context(nc.sbuf_tensor(un(name), list(shape), dt))

            def psp(name, shape, dt=F32):
                return ph.enter_context(nc.psum_tensor(un(name), list(shape), dt))

            env = make_env(ph, f"a1_{l}")
            wqk = sbp("wqk_s", [128, NCH, 1664], BF16)
            wqs = sbp("wqs_s", [128, NCH, 1664], BF16)
            wv = sbp("wv_s", [128, NCH, 640], BF16)
            R_wq, R_wqs_, R_wv_ = Res(), Res(), Res()
            P.dma("sp", DMA(wqk[:], wqk_b.ap()[i].rearrange("(k p) n -> p k n", p=128)), r=[R_w[("wqk", i)]], w=[R_wq])
            P.dma("sp", DMA(wqs[:], wqs_b.ap()[i].rearrange("(k p) n -> p k n", p=128)), r=[R_w[("wqs", i)]], w=[R_wqs_])
            P.dma("sp", DMA(wv[:], wv_b.ap()[i].rearrange("(k p) n -> p k n", p=128)), r=[R_w[("wv", i)]], w=[R_wv_])
            xring = Ring([sbp(f"a1x{j}", [128, NCH, 512], F32) for j in range(2)])
            hring = Ring([sbp(f"a1h{j}", [128, NCH, 512], BF16) for j in range(2)])
            ropeC = Ring([sbp(f"rC{j}", [128, 512], F32) for j in range(2)])
            ropeS = Ring([sbp(f"rS{j}", [128, 512], F32) for j in range(2)])
            pA = Ring([psp(f"pA{j}", [128, 512]) for j in range(2)])
            pB = Ring([psp(f"pB{j}", [128, 512]) for j in range(2)])
            pG = Ring([psp("pG0", [128, 512])])
            pV = Ring([psp(f"pV{j}", [128, 640]) for j in range(1)])
            t1r = Ring([sbp(f"t1_{j}", [128, 512], F32) for j in range(2)])
            t2r = Ring([sbp(f"t2_{j}", [128, 512], F32) for j in range(2)])
            gsq = Ring([sbp(f"gsq{j}", [128, 512], BF16) for j in range(2)])
            grs = Ring([sbp(f"grs{j}", [128, 512], F32) for j in range(2)])
            outr = Ring([sbp(f"qko{j}", [128, 512], BF16) for j in range(3)])
            vrow = Ring([sbp(f"vrow{j}", [128, 640], BF16) for j in range(2)])

            def project(N, src_fm, rsrc, col0, stream, b_idx):
                lat = stream == 0
                xt, rx = xring.next()
                P.dma("sp", DMA(xt[:, :, :N], src_fm[:, :, col0:col0 + N]), r=[rsrc], w=[rx])
                ht, rh = hring.next()
                norm_mod(env, xt, rx, N, modv[:, l, 0, stream, :], modv[:, l, 1, stream, :], ht, rh)
                if lat:
                    rc, rrc = ropeC.next()
                    rs_, rrs = ropeS.next()
                    P.dma("sp", DMA(rc[:], ropeC_in.ap()[:, b_idx * 512:(b_idx + 1) * 512]), w=[rrc])
                    P.dma("sp", DMA(rs_[:], ropeS_in.ap()[:, b_idx * 512:(b_idx + 1) * 512]), w=[rrs])
                for t in range(13):
                    is_q = t < 4 or 8 <= t < 12
                    if is_q and (not lat) and (not with_ctx_out):
                        continue
                    gq = t >= 8
                    pa, rpa = pA.next()
                    for k in range(NCH):
                        P.op("pe", MM(pa[:, :N], wqk[:, k, t * 128:(t + 1) * 128], ht[:, k, :N],
                                      start=(k == 0), stop=(k == NCH - 1)), r=[R_wq, rh], w=[rpa])
                    if lat:
                        pb, rpb = pB.next()
                        for k in range(NCH):
                            P.op("pe", MM(pb[:, :N], wqs[:, k, t * 128:(t + 1) * 128], ht[:, k, :N],
                                          start=(k == 0), stop=(k == NCH - 1)), r=[R_wqs_, rh], w=[rpb])
                    if gq:
                        g_idx = t - 8
                        sqt, rsq = gsq.next()
                        P.op("act", ACT(sqt[:, :N], pa[:, :N], AF.Square), r=[rpa], w=[rsq])
                        pg, rpg = pG.next()
                        P.op("pe", MM(pg[:, :N], bd_bf[:], sqt[:, :N]), r=[rsq, R_ones], w=[rpg])
                        rst, rrst = grs.next()
                        P.op("act", ACT(rst[:, :N], pg[:, :N], AF.Sqrt, bias=epsT[:, 0:1], scale=1.0 / 64), r=[rpg, R_eps], w=[rrst])
                        P.op("dve", RCP(rst[:, :N], rst[:, :N]), r=[rrst], w=[rrst])
                    ot, rot = outr.next()
                    if lat:
                        t1, rt1 = t1r.next()
                        t2, rt2 = t2r.next()
                        if gq:
                            P.op("dve", STT(t1[:, :N], pa[:, :N], ggain[:, i, g_idx, 0:1], rc[:, :N], ALU.mult, ALU.mult),
                                 r=[rpa, rrc, R_cs[5]], w=[rt1])
                            P.op("dve", STT(t2[:, :N], pb[:, :N], ggain[:, i, g_idx, 1:2], rs_[:, :N], ALU.mult, ALU.mult),
                                 r=[rpb, rrs, R_cs[5]], w=[rt2])
                            P.op("pool", TT(t1[:, :N], t1[:, :N], t2[:, :N], ALU.add), r=[rt1, rt2], w=[rt1])
                            P.op("dve", TT(ot[:, :N], t1[:, :N], rst[:, :N], ALU.mult), r=[rt1, rrst], w=[rot])
                        else:
                            P.op("dve", TT(t1[:, :N], pa[:, :N], rc[:, :N], ALU.mult), r=[rpa, rrc], w=[rt1])
                            P.op("dve", TT(t2[:, :N], pb[:, :N], rs_[:, :N], ALU.mult), r=[rpb, rrs], w=[rt2])
                            P.op("pool", TT(ot[:, :N], t1[:, :N], t2[:, :N], ALU.add), r=[rt1, rt2], w=[rot])
                    else:
                        if gq:
                            P.op("dve", STT(ot[:, :N], pa[:, :N], ggain[:, i, g_idx, 0:1], rst[:, :N], ALU.mult, ALU.mult),
                                 r=[rpa, rrst, R_cs[5]], w=[rot])
                        else:
                            P.op("act", ACT(ot[:, :N], pa[:, :N], AF.Identity), r=[rpa], w=[rot])
                    if is_q:
                        qi = t if t < 4 else t - 4
                        if lat:
                            P.dma("sp", DMA(Qs.ap()[qi, :, col0 - 1:col0 - 1 + N], ot[:, :N]), r=[rot], w=[R_Qs[qi][b_idx]])
                        else:
                            P.dma("sp", DMA(Qc.ap()[qi], ot[:, :N]), r=[rot], w=[R_Qc[qi]])
                    else:
                        ki = t - 4 if t < 8 else 4
                        if lat:
                            for hf in range(2):
                                P.dma("sp", DMA(Klp[2 * ki + hf].ap()[:, col0 - 1:col0 - 1 + N], ot[hf * 64:(hf + 1) * 64, :N]),
                                      r=[rot], w=[R_Kl[2 * ki + hf]])
                        else:
                            P.dma("sp", DMA(Kc.ap()[ki * 128:(ki + 1) * 128, :], ot[:, :N]), r=[rot], w=[R_Kc])
                for tt_ in range(N // 128):
                    pv, rpv = pV.next()
                    for k in range(NCH):
                        P.op("pe", MM(pv[:, 0:512], ht[:, k, tt_ * 128:(tt_ + 1) * 128], wv[:, k, 0:512],
                                      start=(k == 0), stop=(k == NCH - 1)), r=[R_wv_, rh], w=[rpv])
                    for k in range(NCH):
                        P.op("pe", MM(pv[:, 512:640], ht[:, k, tt_ * 128:(tt_ + 1) * 128], wv[:, k, 512:640],
                                      start=(k == 0), stop=(k == NCH - 1)), r=[R_wv_, rh], w=[rpv])
                    vr, rvr = vrow.next()
                    P.op("act", ACT(vr[:], pv[:], AF.Identity), r=[rpv], w=[rvr])
                    if lat:
                        tile_idx = (col0 - 1) // 128 + tt_
                        for gi_ in range(5):
                            for hf in range(2):
                                P.dma("sp", DMA(Vlp[2 * gi_ + hf].ap()[:, tile_idx * 128:(tile_idx + 1) * 128],
                                                vr[hf * 64:(hf + 1) * 64, gi_ * 128:(gi_ + 1) * 128]), r=[rvr], w=[R_Vl[2 * gi_ + hf]])
                    else:
                        dst = Vc.ap().rearrange("(g p) (t c) -> p g t c", p=128, c=128)[:, :, tt_, :]
                        P.dma("sp", DMA(dst, vr[:].rearrange("p (g c) -> p g c", c=128)), r=[rvr], w=[R_Vc])

            project(CTX, cw_fm, R_cw, 0, 1, 0)
            for b in range(NB):
                project(512, xw_fm, R_xw[b], 1 + b * 512, 0, b)
            P.barrier()

        groups = [[0, 1, 2, 3], [4, 5, 6, 7]]
        for p_ in range(10):
            P.cc(lambda e, p_=p_: e.collective_compute("AllGather", ALU.bypass, replica_groups=groups,
                                                       ins=[Klp[p_].ap().opt()], outs=[Kap[p_].ap().opt()]),
                 r=[R_Kl[p_]], w=[R_Ka[p_]])
            P.cc(lambda e, p_=p_: e.collective_compute("AllGather", ALU.bypass, replica_groups=groups,
                                                       ins=[Vlp[p_].ap().opt()], outs=[Vap[p_].ap().opt()]),
                 r=[R_Vl[p_]], w=[R_Va[p_]])

        with ExitStack() as ph:
            def sbp(name, shape, dt):
                return ph.enter_context(nc.sbuf_tensor(un(name), list(shape), dt))

            def psp(name, shape, dt=F32):
                return ph.enter_context(nc.psum_tensor(un(name), list(shape), dt))

            KB = [sbp(f"KB{j}", [128, CTX + S], BF16) for j in range(2)]
            VB = [sbp(f"VB{j}", [128, NKT * 130], BF16) for j in range(2)]
            R_KB = [[Res() for _ in range(9)] for _ in range(2)]
            R_VB = [[Res() for _ in range(9)] for _ in range(2)]
            qring = Ring([sbp(f"qt{j}", [128, 512], BF16) for j in range(3)])
            pring = Ring([sbp(f"pt{j}", [128, 512], BF16) for j in range(4)])
            psS = Ring([psp(f"psS{j}", [128, 512]) for j in range(3)])
            psO = Ring([psp(f"psO{j}", [128, 512]) for j in range(2)])
            psL = Ring([psp(f"psL{j}", [128, 512]) for j in range(2)])
            psE = Ring([psp("psE0", [128, 512])])
            osb = Ring([sbp(f"osb{j}", [128, 512], F32) for j in range(4)])
            lsb = Ring([sbp(f"lsb{j}", [128, 512], F32) for j in range(4)])
            bcs = Ring([sbp(f"bcs{j}", [128, 512], F32) for j in range(2)])
            ework = Ring([sbp(f"ew{j}", [128, 512], F32) for j in range(3)])
            esq = Ring([sbp("esq0", [128, 512], BF16)])
            aout = Ring([sbp(f"aout{j}", [128, 512], BF16) for j in range(2)])

            def load_kv(g, slot):
                kb, vb = KB[slot], VB[slot]
                rk, rv = R_KB[slot], R_VB[slot]
                P.dma("sp", DMA(kb[:, 0:CTX], Kc.ap()[g * 128:(g + 1) * 128, :]), r=[R_Kc], w=[rk[0]])
                for r_ in range(4):
                    for hf in range(2):
                        P.dma("sp", DMA(kb[hf * 64:(hf + 1) * 64, CTX + r_ * TPC:CTX + (r_ + 1) * TPC],
                                        Kap[2 * g + hf].ap()[r_ * 64:(r_ + 1) * 64, :]), r=[R_Ka[2 * g + hf]], w=[rk[1 + 2 * r_ + hf]])
                if g < 4:
                    v3 = vb[:, 0:NKT * 128].rearrange("p (t c) -> p t c", c=128)
                    P.dma("sp", DMA(v3[:, 0:2, :], Vc.ap()[g * 128:(g + 1) * 128, :].rearrange("p (t c) -> p t c", c=128)),
                          r=[R_Vc], w=[rv[0]])
                    for r_ in range(4):
                        for hf in range(2):
                            P.dma("sp", DMA(v3[hf * 64:(hf + 1) * 64, 2 + r_ * NLT:2 + (r_ + 1) * NLT, :],
                                            Vap[2 * g + hf].ap()[r_ * 64:(r_ + 1) * 64, :].rearrange("p (t c) -> p t c", c=128)),
                                  r=[R_Va[2 * g + hf]], w=[rv[1 + 2 * r_ + hf]])
                else:
                    v3 = vb[:].rearrange("p (t c) -> p t c", c=130)
                    for hh in range(2):
                        P.dma("sp", DMA(v3[:, 0:2, hh * 65:hh * 65 + 64],
                                        Vc.ap()[g * 128:(g + 1) * 128, :].rearrange("p (t c) -> p t c", c=128)[:, :, hh * 64:(hh + 1) * 64]),
                              r=[R_Vc], w=[rv[0]])
                        for r_ in range(4):
                            for hf in range(2):
                                P.dma("sp", DMA(v3[hf * 64:(hf + 1) * 64, 2 + r_ * NLT:2 + (r_ + 1) * NLT, hh * 65:hh * 65 + 64],
                                                Vap[2 * g + hf].ap()[r_ * 64:(r_ + 1) * 64, :].rearrange("p (t c) -> p t c", c=128)[:, :, hh * 64:(hh + 1) * 64]),
                                      r=[R_Va[2 * g + hf]], w=[rv[1 + 2 * r_ + hf]])
                    for hh in range(2):
                        P.op("pool", MS(v3[:, :, hh * 65 + 64:hh * 65 + 65], 1.0), r=[], w=rv)

            def unit(g, slot, m, qt, rq, N, nkt, kt0):
                kb, vb = KB[slot], VB[slot]
                rk, rv = R_KB[slot], R_VB[slot]
                diff = g < 4
                if diff:
                    v3 = vb[:, 0:NKT * 128].rearrange("p (t c) -> p t c", c=128)
                    c0, dvp = 0, 128
                else:
                    v3 = vb[:].rearrange("p (t c) -> p t c", c=130)
                    c0, dvp = m * 65, 65
                po, rpo = psO.next()
                if diff:
                    pl, rpl = psL.next()
                ps_list = {}

                def issue_S(kt):
                    st, rst_ = psS.next()
                    ps_list[kt] = (st, rst_)
                    P.op("pe", MM(st[:, :N], kb[m * 64:(m + 1) * 64, (kt0 + kt) * 128:(kt0 + kt + 1) * 128],
                                  qt[m * 64:(m + 1) * 64, :N]), r=rk + [rq], w=[rst_])

                for kt in range(min(2, nkt)):
                    issue_S(kt)
                for kt in range(nkt):
                    st, rst_ = ps_list.pop(kt)
                    pt_, rpt = pring.next()
                    P.op("act", ACT(pt_[:, :N], st[:, :N], AF.Exp, scale=0.125), r=[rst_], w=[rpt])
                    P.op("pe", MM(po[0:dvp, :N], v3[:, kt0 + kt, c0:c0 + dvp], pt_[:, :N],
                                  start=(kt == 0), stop=(kt == nkt - 1)), r=rv + [rpt], w=[rpo])
                    if diff:
                        P.op("pe", MM(pl[0:1, :N], ones_bf[:, 0:1], pt_[:, :N],
                                      start=(kt == 0), stop=(kt == nkt - 1)), r=[rpt, R_ones], w=[rpl])
                    if kt + 2 < nkt:
                        issue_S(kt + 2)
                ot, rot = osb.next()
                P.op("dve", CP(ot[0:dvp, :N], po[0:dvp, :N]), r=[rpo], w=[rot])
                if diff:
                    lt, rlt = lsb.next()
                    P.op("dve", CP(lt[0:1, :N], pl[0:1, :N]), r=[rpl], w=[rlt])
                    return ot, rot, lt, rlt
                return ot, rot, None, None

            def bcast_row(row_ap, rrow, base, nparts, N):
                pe_, rpe = psE.next()
                P.op("pe", MM(pe_[0:nparts, :N], ones_f[base:base + 1, 0:nparts], row_ap), r=[rrow, R_ones], w=[rpe])
                bt, rbt = bcs.next()
                P.op("act", ACT(bt[0:nparts, :N], pe_[0:nparts, :N], AF.Identity), r=[rpe], w=[rbt])
                return bt, rbt

            def epilogue_gqa(ot, rot, N, dst_ap, rdst):
                P.op("dve", RCP(ot[64:65, :N], ot[64:65, :N]), r=[rot], w=[rot])
                bt, rbt = bcast_row(ot[64:65, :N], rot, 64, 64, N)
                at, rat = aout.next()
                P.op("dve", TT(at[0:64, :N], ot[0:64, :N], bt[0:64, :N], ALU.mult), r=[rot, rbt], w=[rat])
                P.dma("sp", DMA(dst_ap, at[0:64, :N]), r=[rat], w=[rdst])

            def epilogue_diff(o1, ro1, l1, rl1, o2, ro2, l2, rl2, N, dst_ap, rdst):
                P.op("dve", RCP(l1[0:1, :N], l1[0:1, :N]), r=[rl1], w=[rl1])
                P.op("dve", RCP(l2[0:1, :N], l2[0:1, :N]), r=[rl2], w=[rl2])
                P.op("dve", TS(l2[0:1, :N], l2[0:1, :N], nlam, None, ALU.mult), r=[rl2, R_lam], w=[rl2])
                b1, rb1 = bcast_row(l1[0:1, :N], rl1, 0, 128, N)
                w1, rw1 = ework.next()
                P.op("dve", TT(w1[:, :N], o1[:, :N], b1[:, :N], ALU.mult), r=[ro1, rb1], w=[rw1])
                b2, rb2 = bcast_row(l2[0:1, :N], rl2, 0, 128, N)
                w2, rw2 = ework.next()
                P.op("dve", TT(w2[:, :N], o2[:, :N], b2[:, :N], ALU.mult), r=[ro2, rb2], w=[rw2])
                P.op("dve", TT(w1[:, :N], w1[:, :N], w2[:, :N], ALU.add), r=[rw1, rw2], w=[rw1])
                sq, rsq = esq.next()
                P.op("act", ACT(sq[:, :N], w1[:, :N], AF.Square), r=[rw1], w=[rsq])
                pe_, rpe = psE.next()
                P.op("pe", MM(pe_[:, :N], ones_bf[:], sq[:, :N]), r=[rsq, R_ones], w=[rpe])
                w3, rw3 = ework.next()
                P.op("act", ACT(w3[:, :N], pe_[:, :N], AF.Sqrt, bias=epsT[:, 0:1], scale=1.0 / 128), r=[rpe, R_eps], w=[rw3])
                P.op("dve", RCP(w3[:, :N], w3[:, :N]), r=[rw3], w=[rw3])
                P.op("dve", TT(w1[:, :N], w1[:, :N], w3[:, :N], ALU.mult), r=[rw1, rw3], w=[rw1])
                at, rat = aout.next()
                P.op("dve", TS(at[:, :N], w1[:, :N], subln[:, i:i + 1], 1.0 - lam_init, ALU.mult, ALU.mult),
                     r=[rw1, R_cs[6]], w=[rat])
                P.dma("sp", DMA(dst_ap, at[:, :N]), r=[rat], w=[rdst])

            def run_group(g, slot, lat_chunks=True):
                qtiles = [g] if g < 4 else [4, 5, 6, 7]
                chunks = []
                if with_ctx_out:
                    chunks.append(("ctx", 0))
                chunks += [("lat", b) for b in range(NB)]
                for qi in qtiles:
                    for (kind, b) in chunks:
                        qt, rq = qring.next()
                        if kind == "ctx":
                            N, nkt = CTX, CTX // 128
                            P.dma("sp", DMA(qt[:, :N], Qc.ap()[qi]), r=[R_Qc[qi]], w=[rq])
                        else:
                            N, nkt = 512, NKT
                            P.dma("sp", DMA(qt[:, :N], Qs.ap()[qi, :, b * 512:(b + 1) * 512]), r=[R_Qs[qi][b]], w=[rq])
                        res = [unit(g, slot, m, qt, rq, N, nkt, 0) for m in range(2)]
                        if g < 4:
                            if kind == "ctx":
                                dst, rdst = ATc.ap()[g], R_ATc[g]
                            else:
                                dst, rdst = AT.ap()[g, :, b * 512:(b + 1) * 512], R_AT[g][b]
                            epilogue_diff(res[0][0], res[0][1], res[0][2], res[0][3],
                                          res[1][0], res[1][1], res[1][2], res[1][3], N, dst, rdst)
                        else:
                            j = qi - 4
                            for m in range(2):
                                hidx = 4 + j + 4 * m
                                if kind == "ctx":
                                    dst, rdst = ATc.ap()[hidx, 0:64, :], R_ATc[hidx]
                                else:
                                    dst, rdst = AT.ap()[hidx, 0:64, b * 512:(b + 1) * 512], R_AT[hidx][b]
                                epilogue_gqa(res[m][0], res[m][1], N, dst, rdst)

            load_kv(0, 0)
            for g in range(5):
                if g + 1 < 5:
                    load_kv(g + 1, (g + 1) % 2)
                run_group(g, g % 2)
            P.barrier()

        with ExitStack() as ph:
            def sbp(name, shape, dt):
                return ph.enter_context(nc.sbuf_tensor(un(name), list(shape), dt))

            def psp(name, shape, dt=F32):
                return ph.enter_context(nc.psum_tensor(un(name), list(shape), dt))

            wod = sbp("wod_s", [128, 4, D], BF16)
            wog = sbp("wog_s", [64, 8, D], BF16)
            R_wod, R_wog = Res(), Res()
            P.dma("sp", DMA(wod[:], wod_b.ap()[i].rearrange("(k p) n -> p k n", p=128)), r=[R_w[("wod", i)]], w=[R_wod])
            P.dma("sp", DMA(wog[:], wog_b.ap()[i]), r=[R_w[("wog", i)]], w=[R_wog])
            atr = Ring([sbp(f"atb{j}", [128, 12, 512], BF16) for j in range(2)])
            xr = Ring([sbp(f"a4x{j}", [128, NCH, 512], F32) for j in range(2)])
            py = Ring([psp(f"py{j}", [128, 512]) for j in range(3)])

            def outproj(N, at_src, rat_list, x_fm, rx_dram, col0, stream):
                at_, rat_ = atr.next()
                P.dma("sp", DMA(at_[:, 0:4, :N], at_src[0:4].rearrange("a p t -> p a t")), r=rat_list[0:4], w=[rat_])
                P.dma("sp", DMA(at_[0:64, 4:12, :N], at_src[4:12, 0:64, :].rearrange("a p t -> p a t")), r=rat_list[4:12], w=[rat_])
                xt, rx = xr.next()
                P.dma("sp", DMA(xt[:, :, :N], x_fm[:, :, col0:col0 + N]), r=[rx_dram], w=[rx])
                for fo in range(NCH):
                    pt_, rp = py.next()
                    for a in range(4):
                        P.op("pe", MM(pt_[:, :N], wod[:, a, fo * 128:(fo + 1) * 128], at_[:, a, :N],
                                      start=(a == 0), stop=False), r=[R_wod, rat_], w=[rp])
                    for a in range(8):
                        P.op("pe", MM(pt_[:, :N], wog[:, a, fo * 128:(fo + 1) * 128], at_[0:64, 4 + a, :N],
                                      start=False, stop=(a == 7)), r=[R_wog, rat_], w=[rp])
                    P.op("dve", STT(xt[:, fo, :N], pt_[:, :N], modv[:, l, 2, stream, fo:fo + 1], xt[:, fo, :N], ALU.mult, ALU.add),
                         r=[rp, rx, R_const], w=[rx])
                P.dma("sp", DMA(x_fm[:, :, col0:col0 + N], xt[:, :, :N]), r=[rx], w=[rx_dram])

            if with_ctx_out:
                outproj(CTX, ATc.ap(), R_ATc, cw_fm, R_cw, 0, 1)
            for b in range(NB):
                outproj(512, AT.ap()[:, :, b * 512:(b + 1) * 512], [R_AT[a][b] for a in range(12)], xw_fm, R_xw[b], 1 + b * 512, 0)
            P.barrier()

    def sgu_layer(l):
        i = l // 2
        xw_fm, R_xw = xw_fms[cur["i"]], R_xws[cur["i"]]
        with_ctx = l < 2
        with ExitStack() as ph:
            def sbp(name, shape, dt):
                return ph.enter_context(nc.sbuf_tensor(un(name), list(shape), dt))

            def psp(name, shape, dt=F32):
                return ph.enter_context(nc.psum_tensor(un(name), list(shape), dt))

            env = make_env(ph, f"sg_{l}")
            wu = sbp("swu_s", [128, NCH, D], BF16)
            wvv = sbp("swv_s", [128, NCH, D], BF16)
            wo = sbp("swo_s", [128, NCH, D], BF16)
            wsT = sbp("swsT_s", [128, 4, 128], BF16)
            R_wu, R_wvv, R_wo, R_wsT = Res(), Res(), Res(), Res()
            P.dma("sp", DMA(wu[:], swu_b.ap()[i].rearrange("(k p) n -> p k n", p=128)), r=[R_w[("swu", i)]], w=[R_wu])
            P.dma("sp", DMA(wvv[:], swv_b.ap()[i].rearrange("(k p) n -> p k n", p=128)), r=[R_w[("swv", i)]], w=[R_wvv])
            P.dma("sp", DMA(wo[:], swo_b.ap()[i].rearrange("(k p) n -> p k n", p=128)), r=[R_w[("swo", i)]], w=[R_wo])
            P.dma("sp", DMA(wsT[:], swsT_b.ap()[i]), r=[R_w[("swsT", i)]], w=[R_wsT])
            xr = Ring([sbp(f"sgx{j}", [128, NCH, 512], F32) for j in range(2)])
            hr = Ring([sbp(f"sgh{j}", [128, NCH, 512], BF16) for j in range(2)])
            uT = Ring([sbp(f"sgu{j}", [128, NCH, 512], BF16) for j in range(1)])
            suT = Ring([sbp(f"sgsu{j}", [128, NCH, 512], BF16) for j in range(1)])
            vg = Ring([sbp(f"sgv{j}", [128, D], F32) for j in range(2)])
            vn = Ring([sbp(f"sgvn{j}", [128, D], BF16) for j in range(2)])
            vjunk = Ring([sbp("sgjunk", [128, D], BF16)])
            vss = Ring([sbp(f"sgss{j}", [128, 2], F32) for j in range(2)])
            mt = Ring([sbp(f"sgm{j}", [128, 128], F32) for j in range(3)])
            pU = Ring([psp(f"pU{j}", [128, 512]) for j in range(2)])
            pVv = Ring([psp(f"pVv{j}", [128, D]) for j in range(1)])
            pM = Ring([psp(f"pM{j}", [128, 512]) for j in range(2)])
            pY = Ring([psp(f"pY{j}", [128, 512]) for j in range(1)])

            def sgu_block(N, x_fm, rx_dram, col0, stream):
                xt, rx = xr.next()
                P.dma("sp", DMA(xt[:, :, :N], x_fm[:, :, col0:col0 + N]), r=[rx_dram], w=[rx])
                ht, rh = hr.next()
                norm_mod(env, xt, rx, N, modv[:, l, 0, stream, :], modv[:, l, 1, stream, :], ht, rh)
                ut, rut = uT.next()
                for fc in range(NCH):
                    pu, rpu = pU.next()
                    for k in range(NCH):
                        P.op("pe", MM(pu[:, :N], wu[:, k, fc * 128:(fc + 1) * 128], ht[:, k, :N],
                                      start=(k == 0), stop=(k == NCH - 1)), r=[R_wu, rh], w=[rpu])
                    P.op("act", ACT(ut[:, fc, :N], pu[:, :N], AF.Gelu), r=[rpu], w=[rut])
                sut, rsut = suT.next()
                for tt_ in range(N // 128):
                    pv, rpv = pVv.next()
                    for half in range(2):
                        for k in range(NCH):
                            P.op("pe", MM(pv[:, half * 512:(half + 1) * 512], ht[:, k, tt_ * 128:(tt_ + 1) * 128],
                                          wvv[:, k, half * 512:(half + 1) * 512], start=(k == 0), stop=(k == NCH - 1)),
                                 r=[R_wvv, rh], w=[rpv])
                    vt, rvt = vg.next()
                    P.op("act", ACT(vt[:], pv[:], AF.Gelu), r=[rpv], w=[rvt])
                    sst, rss = vss.next()
                    jk, rjk = vjunk.next()
                    P.op("dve", lambda e, jk=jk, vt=vt, sst=sst: e.tensor_tensor(jk[:], vt[:], vt[:], ALU.mult), r=[rvt], w=[rjk])
                    P.op("dve", lambda e, jk=jk, sst=sst: e.reduce_sum(sst[:, 0:1], jk[:], mybir.AxisListType.X), r=[rjk], w=[rss])
                    P.op("act", ACT(sst[:, 1:2], sst[:, 0:1], AF.Sqrt, bias=epsT[:, 0:1], scale=1.0 / D), r=[rss, R_eps], w=[rss])
                    P.op("dve", RCP(sst[:, 1:2], sst[:, 1:2]), r=[rss], w=[rss])
                    vnt, rvn = vn.next()
                    P.op("dve", TS(vnt[:], vt[:], sst[:, 1:2], None, ALU.mult), r=[rvt, rss], w=[rvn])
                    pm_, rpm = pM.next()
                    for gi in range(4):
                        for cc_ in range(2):
                            fc = gi * 2 + cc_
                            sl = pm_[:, (fc % 4) * 128:(fc % 4 + 1) * 128]
                            P.op("pe", MM(sl, vnt[:, fc * 128:(fc + 1) * 128], wsT[:, gi, :]), r=[rvn, R_wsT], w=[rpm])
                            mtt, rmt = mt.next()
                            P.op("dve", STT(mtt[:], sl, svn[:, i, fc:fc + 1], sbsb[:, i, gi, :], ALU.mult, ALU.add),
                                 r=[rpm, R_cs[8], R_cs[9]], w=[rmt])
                            P.op("pool", TT(sut[:, fc, tt_ * 128:(tt_ + 1) * 128], mtt[:], ut[:, fc, tt_ * 128:(tt_ + 1) * 128], ALU.mult),
                                 r=[rmt, rut], w=[rsut])
                            if fc == 3:
                                pm_, rpm = pM.next()
                for fo in range(NCH):
                    pt_, rp = pY.next()
                    for k in range(NCH):
                        P.op("pe", MM(pt_[:, :N], wo[:, k, fo * 128:(fo + 1) * 128], sut[:, k, :N],
                                      start=(k == 0), stop=(k == NCH - 1)), r=[R_wo, rsut], w=[rp])
                    P.op("dve", STT(xt[:, fo, :N], pt_[:, :N], modv[:, l, 2, stream, fo:fo + 1], xt[:, fo, :N], ALU.mult, ALU.add),
                         r=[rp, rx, R_const], w=[rx])
                P.dma("sp", DMA(x_fm[:, :, col0:col0 + N], xt[:, :, :N]), r=[rx], w=[rx_dram])

            if with_ctx:
                sgu_block(CTX, cw_fm, R_cw, 0, 1)
            for b in range(NB):
                sgu_block(512, xw_fm, R_xw[b], 1 + b * 512, 0)
            P.barrier()

    def halo_exchange(l):
        xw_fm, R_xw, R_xh = xw_fms[cur["i"]], R_xws[cur["i"]], R_xhs[cur["i"]]
        R_hl, R_ha = Res(), Res()
        with ExitStack() as ph:
            hs = ph.enter_context(nc.sbuf_tensor(un("hs"), [8, D], F32))
            ho = ph.enter_context(nc.sbuf_tensor(un("ho"), [128, 2, NCH, 1], F32))
            hp = ph.enter_context(nc.psum_tensor(un("hp"), [128, 2, NCH], F32))
            hb = ph.enter_context(nc.sbuf_tensor(un("hb"), [128, 2, NCH, 1], F32))
            rhs_, rho, rhp, rhb = Res(), Res(), Res(), Res()
            P.dma("sp", DMA(hb[:, 0, :, :], xw_fm[:, :, 1:2], allow_slow_non_contiguous=True), r=[R_xw[0]], w=[rhb])
            P.dma("sp", DMA(hb[:, 1, :, :], xw_fm[:, :, TPC:TPC + 1], allow_slow_non_contiguous=True), r=[R_xw[NB - 1]], w=[rhb])
            for s_ in range(2):
                P.dma("sp", DMA(HLl.ap()[s_].rearrange("(p c) -> p c", p=128), hb[:, s_, :, 0], allow_slow_non_contiguous=True), r=[rhb], w=[R_hl])
            P.cc(lambda e: e.collective_compute("AllGather", ALU.bypass, replica_groups=[[0, 1, 2, 3], [4, 5, 6, 7]],
                                                ins=[HLl.ap().opt()], outs=[HLa.ap().opt()]), r=[R_hl], w=[R_ha])
            P.dma("sp", DMA(hs[:], HLa.ap()), r=[R_ha], w=[rhs_])
            for s_ in range(2):
                for c in range(NCH):
                    P.op("pe", MM(hp[:, s_, c:c + 1], hs[:, c:D:NCH], sel[:, s_:s_ + 1]), r=[rhs_, R_cs[12]], w=[rhp])
            P.op("dve", CP(ho[:, :, :, 0], hp[:]), r=[rhp], w=[rho])
            P.dma("sp", DMA(xw_fm[:, :, 0:1], ho[:, 0, :, :], allow_slow_non_contiguous=True), r=[rho], w=[R_xh])
            P.dma("sp", DMA(xw_fm[:, :, TPC + 1:TPC + 2], ho[:, 1, :, :], allow_slow_non_contiguous=True), r=[rho], w=[R_xh])
            P.barrier()

    def ffn_layer(l):
        with_ctx = l < 2
        ci = cur["i"]
        xin_fm, R_xin, R_xh = xw_fms[ci], R_xws[ci], R_xhs[ci]
        xout_fm, R_xout = xw_fms[1 - ci], R_xws[1 - ci]
        with ExitStack() as ph:
            def sbp(name, shape, dt):
                return ph.enter_context(nc.sbuf_tensor(un(name), list(shape), dt))

            def psp(name, shape, dt=F32):
                return ph.enter_context(nc.psum_tensor(un(name), list(shape), dt))

            env = make_env(ph, f"ff_{l}")
            xt = sbp("ffx", [128, NCH, 1024], F32)
            xh = sbp("ffxh", [128, NCH, 2], F32)
            ht = sbp("ffh", [128, NCH, 1024], BF16)
            hh2 = sbp("ffhh", [128, NCH, 2], BF16)
            hzt = sbp("ffhz", [128, 88], F32)
            rx, rh, rxh, rhh, rhz = Res(), Res(), Res(), Res(), Res()
            wupr = Ring([sbp(f"wup{j}", [128, NCH, 512], BF16) for j in range(2)])
            wdnr = Ring([sbp(f"wdn{j}", [128, NPAIR, 128], BF16) for j in range(2)])
            aT = sbp("ffa", [128, NPAIR, 1024], BF16)
            ra = Res()
            zr = Ring([sbp(f"ffz{j}", [128, 1026], F32) for j in range(3)])
            tr_ = Ring([sbp(f"fft{j}", [128, 1024], F32) for j in range(3)])
            sg = Ring([sbp(f"ffs{j}", [128, 1024], F32) for j in range(2)])
            pH = Ring([psp("pH0", [128, 512])])
            pZ = Ring([psp(f"pZ{j}", [128, 512]) for j in range(5)])
            pYr = Ring([psp(f"pYf{j}", [128, 512]) for j in range(1)])

            def ffn_block(N, src_fm, r_src, dst_fm, r_dst, colL, colM, stream, zero_halo, first_blk=False, last_blk=False):
                nh = (N + 511) // 512
                hw = [min(512, N - hh * 512) for hh in range(nh)]
                P.dma("sp", DMA(xt[:, :, 0:N], src_fm[:, :, colM:colM + N]), r=r_src, w=[rx])
                A2, B2 = modv[:, l, 3, stream, :], modv[:, l, 4, stream, :]
                for hh in range(nh):
                    c0 = hh * 512
                    norm_mod(env, xt[:, :, c0:c0 + hw[hh]], rx, hw[hh], A2, B2, ht[:, :, c0:c0 + hw[hh]], rh)
                if zero_halo:
                    P.op("pool", MS(hh2[:], 0.0), w=[rhh])
                else:
                    P.dma("sp", DMA(xh[:, :, 0:1], src_fm[:, :, colL:colL + 1], allow_slow_non_contiguous=True), r=r_src + [R_xh], w=[rxh])
                    P.dma("sp", DMA(xh[:, :, 1:2], src_fm[:, :, colM + N:colM + N + 1], allow_slow_non_contiguous=True), r=r_src + [R_xh], w=[rxh])
                    norm_mod(env, xh, rxh, 2, A2, B2, hh2, rhh)
                    for c in range(NCH):
                        if first_blk:
                            P.op("dve", TS(hh2[:, c, 0:1], hh2[:, c, 0:1], flags[:, 0:1], None, ALU.mult), r=[rhh, R_cs[11]], w=[rhh])
                        if last_blk:
                            P.op("dve", TS(hh2[:, c, 1:2], hh2[:, c, 1:2], flags[:, 1:2], None, ALU.mult), r=[rhh, R_cs[11]], w=[rhh])
                wts = {}

                def get_w(j):
                    jj = j // 2
                    if jj not in wts:
                        wt, rw = wupr.next()
                        P.dma("sp", DMA(wt[:], wup_b.ap()[l].rearrange("(k p) n -> p k n", p=128)[:, :, jj * 512:(jj + 1) * 512]),
                              r=[R_w[("wup", l)]], w=[rw])
                        wts[jj] = (wt, rw)
                    wt, rw = wts[jj]
                    return wt, rw, (j % 2) * 256

                ph_, rph = pH.next()
                for j in range(NPAIR):
                    wt, rw, wc = get_w(j)
                    for s_ in range(2):
                        ti = j * 2 + s_
                        for k in range(NCH):
                            P.op("pe", MM(ph_[:, ti * 2:ti * 2 + 2], wt[:, k, wc + s_ * 128:wc + (s_ + 1) * 128], hh2[:, k, :],
                                          start=(k == 0), stop=(k == NCH - 1)), r=[rw, rhh], w=[rph])
                wts.clear()
                P.op("act", ACT(hzt[:, 0:88], ph_[:, 0:88], AF.Identity), r=[rph], w=[rhz])
                for j in range(NPAIR):
                    wt, rw, wc = get_w(j)
                    outs = []
                    for s_ in range(2):
                        zt, rz = zr.next()
                        for hh in range(nh):
                            pz, rpz = pZ.next()
                            c0 = hh * 512
                            for k in range(NCH):
                                P.op("pe", MM(pz[:, :hw[hh]], wt[:, k, wc + s_ * 128:wc + (s_ + 1) * 128], ht[:, k, c0:c0 + hw[hh]],
                                              start=(k == 0), stop=(k == NCH - 1)), r=[rw, rh], w=[rpz])
                            P.op("act", ACT(zt[:, 1 + c0:1 + c0 + hw[hh]], pz[:, :hw[hh]], AF.Identity), r=[rpz], w=[rz])
                        ti = j * 2 + s_
                        P.op("pool", CP(zt[:, 0:1], hzt[:, ti * 2:ti * 2 + 1]), r=[rhz], w=[rz])
                        P.op("pool", CP(zt[:, N + 1:N + 2], hzt[:, ti * 2 + 1:ti * 2 + 2]), r=[rhz], w=[rz])
                        tt, rt = tr_.next()
                        P.op("dve", TS(tt[:, :N], zt[:, 1:N + 1], convp[:, l, ti, 1:2], convp[:, l, ti, 3:4], ALU.mult, ALU.add),
                             r=[rz, R_cs[10]], w=[rt])
                        P.op("dve", STT(tt[:, :N], zt[:, 0:N], convp[:, l, ti, 0:1], tt[:, :N], ALU.mult, ALU.add),
                             r=[rz, rt, R_cs[10]], w=[rt])
                        P.op("dve", STT(tt[:, :N], zt[:, 2:N + 2], convp[:, l, ti, 2:3], tt[:, :N], ALU.mult, ALU.add),
                             r=[rz, rt, R_cs[10]], w=[rt])
                        outs.append((tt, rt))
                    st, rs = sg.next()
                    P.op("act", ACT(st[:, :N], outs[0][0][:, :N], AF.Silu), r=[outs[0][1]], w=[rs])
                    P.op("pool", TT(aT[:, j, :N], st[:, :N], outs[1][0][:, :N], ALU.mult), r=[rs, outs[1][1]], w=[ra])
                for fo in range(NCH):
                    wd, rwd = wdnr.next()
                    P.dma("sp", DMA(wd[:], wdn_b.ap()[l].rearrange("(j p) n -> p j n", p=128)[:, :, fo * 128:(fo + 1) * 128]),
                          r=[R_w[("wdn", l)]], w=[rwd])
                    for hh in range(nh):
                        c0 = hh * 512
                        py_, rpy = pYr.next()
                        for j in range(NPAIR):
                            P.op("pe", MM(py_[:, :hw[hh]], wd[:, j, :], aT[:, j, c0:c0 + hw[hh]],
                                          start=(j == 0), stop=(j == NPAIR - 1)), r=[rwd, ra], w=[rpy])
                        P.op("dve", STT(xt[:, fo, c0:c0 + hw[hh]], py_[:, :hw[hh]], modv[:, l, 5, stream, fo:fo + 1],
                                        xt[:, fo, c0:c0 + hw[hh]], ALU.mult, ALU.add), r=[rpy, rx, R_const], w=[rx])
                P.dma("sp", DMA(dst_fm[:, :, colM:colM + N], xt[:, :, 0:N]), r=[rx], w=r_dst)

            if with_ctx:
                ffn_block(CTX, cw_fm, [R_cw], cw_fm, [R_cw], 0, 0, 1, True)
            NBLK = max(1, TPC // 1024)
            BN = TPC // NBLK
            for b in range(NBLK):
                ks = [k for k in range(NB) if not ((k + 1) * 512 <= b * BN - 1 or k * 512 >= (b + 1) * BN + 1)]
                km = [k for k in range(NB) if b * BN <= k * 512 < (b + 1) * BN]
                ffn_block(BN, xin_fm, [R_xin[k] for k in ks], xout_fm, [R_xout[k] for k in km], b * BN, 1 + b * BN, 0, False,
                          first_blk=(b == 0), last_blk=(b == NBLK - 1))
            P.barrier()
        cur["i"] = 1 - ci

    def final_norm():
        with ExitStack() as ph:
            def sbp(name, shape, dt):
                return ph.enter_context(nc.sbuf_tensor(un(name), list(shape), dt))
            env = make_env(ph, "fin")
            xr = Ring([sbp(f"fnx{j}", [128, NCH, 512], F32) for j in range(2)])
            yr = Ring([sbp(f"fny{j}", [128, NCH, 512], F32) for j in range(2)])
            y_fm = yT_out.ap().rearrange("(c p) t -> p c t", p=128)
            xw_fm, R_xw = xw_fms[cur["i"]], R_xws[cur["i"]]
            for b in range(NB):
                xt, rx = xr.next()
                P.dma("sp", DMA(xt[:], xw_fm[:, :, 1 + b * 512:1 + (b + 1) * 512]), r=[R_xw[b]], w=[rx])
                yt, ry = yr.next()
                norm_mod(env, xt, rx, 512, finn, None, yt, ry)
                P.dma("sp", DMA(y_fm[:, :, b * 512:(b + 1) * 512], yt[:]), r=[ry], w=[Res()])

    import os
    dbg = os.environ.get("DBG_XMID")
    for l in range(NL):
        if l % 2 == 0:
            attention_layer(l)
        else:
            sgu_layer(l)
        if dbg:
            break
        halo_exchange(l)
        ffn_layer(l)
    if dbg:
        for b in range(NB):
            P.dma("sp", DMA(yT_out.ap()[:, b * 512:(b + 1) * 512], xw0.ap()[:, 1 + b * 512:1 + (b + 1) * 512]),
                  r=[R_xws[0][b]], w=[Res()])
    else:
        final_norm()
    P.finish()
    P.emit()
    es.close()
    return nc


def _fm(v):
    v = np.asarray(v, np.float32)
    return np.ascontiguousarray(v.reshape(-1, 128).T)


def prep_inputs(inp, TPC):
    f32 = lambda a: np.ascontiguousarray(np.asarray(a, dtype=np.float32))
    x = f32(inp["x"]); ctx = f32(inp["ctx"]); c = f32(inp["c"]); c_ctx = f32(inp["c_ctx"])
    S = x.shape[1]
    assert S == 4 * TPC
    shared = {}
    shared["ada_w"] = f32(inp["ada_w"])
    shared["ada_b"] = np.ascontiguousarray(np.stack([_fm(inp["ada_b"][l]) for l in range(DEPTH)], 1))
    shared["mixn"] = np.ascontiguousarray(np.stack([_fm(inp["mix_norm"][l]) for l in range(DEPTH)], 1))
    shared["ffnn"] = np.ascontiguousarray(np.stack([_fm(inp["ffn_norm"][l]) for l in range(DEPTH)], 1))
    shared["finn"] = _fm(inp["final_norm"])
    w_in = f32(inp["attn_w_in"])
    de = np.concatenate([np.arange(0, 64, 2), np.arange(1, 64, 2)])
    sw = np.concatenate([np.arange(1, 64, 2), np.arange(0, 64, 2)])
    off = {"q1": 0, "q2": 256, "k1": 512, "k2": 768, "va": 1024, "qb": 1536, "kb": 2048, "vb": 2176}

    def blk(name, h, perm):
        return off[name] + h * 64 + perm

    def tiles(perm):
        cols = []
        for h in range(4):
            cols += [blk("q1", h, perm), blk("q2", h, perm)]
        for h in range(4):
            cols += [blk("k1", h, perm), blk("k2", h, perm)]
        for j in range(4):
            cols += [blk("qb", j, perm), blk("qb", 4 + j, perm)]
        cols += [blk("kb", 0, perm), blk("kb", 1, perm)]
        return np.concatenate(cols)

    cn, cs = tiles(de), tiles(sw)
    shared["wqk"] = np.ascontiguousarray(w_in[:, :, cn])
    shared["wqs"] = np.ascontiguousarray(w_in[:, :, cs])
    vcols = np.concatenate([np.arange(1024, 1536), np.arange(2176, 2304)])
    shared["wv"] = np.ascontiguousarray(w_in[:, :, vcols])
    qn = f32(inp["gqa_q_norm"]); kn = f32(inp["gqa_k_norm"])
    gg = np.zeros((128, 2, 5, 2), np.float32)
    for i in range(2):
        for t in range(5):
            g = qn[i] if t < 4 else kn[i]
            gg[:, i, t, 0] = np.concatenate([g[de], g[de]])
            gg[:, i, t, 1] = np.concatenate([g[sw], g[sw]])
    shared["ggain"] = gg
    w_out = f32(inp["attn_w_out"])
    shared["wod"] = np.ascontiguousarray(w_out[:, 0:512, :])
    shared["wog"] = np.ascontiguousarray(w_out[:, 512:1024, :].reshape(2, 8, 64, D).transpose(0, 2, 1, 3))
    lq = np.stack([f32(inp["diff_lq1"]), f32(inp["diff_lk1"]), f32(inp["diff_lq2"]), f32(inp["diff_lk2"])], 1)
    shared["lqk"] = np.ascontiguousarray(lq[None])
    shared["subln"] = np.ascontiguousarray(f32(inp["diff_subln"]).T)
    swin = f32(inp["sgu_w_in"])
    shared["swu"] = np.ascontiguousarray(swin[:, :, 0:D])
    shared["swv"] = np.ascontiguousarray(swin[:, :, D:2 * D])
    shared["svn"] = np.ascontiguousarray(np.stack([_fm(inp["sgu_v_norm"][i]) for i in range(2)], 1))
    shared["swsT"] = np.ascontiguousarray(f32(inp["sgu_w_s"]).transpose(0, 3, 1, 2))
    shared["sbsb"] = np.ascontiguousarray(np.broadcast_to(f32(inp["sgu_b_s"])[None], (128, 2, 4, 128)))
    shared["swo"] = f32(inp["sgu_w_out"])
    wup = f32(inp["ffn_w_up"])
    shared["wup"] = np.ascontiguousarray(
        np.stack([wup[:, :, 0:FF].reshape(DEPTH, D, NPAIR, 128), wup[:, :, FF:2 * FF].reshape(DEPTH, D, NPAIR, 128)], 3)
        .reshape(DEPTH, D, NPAIR * 256))
    cwt = f32(inp["ffn_conv_w"]); cb = f32(inp["ffn_conv_b"])
    cp = np.zeros((128, DEPTH, 44, 4), np.float32)
    for l in range(DEPTH):
        for j in range(NPAIR):
            for s_ in range(2):
                f0 = s_ * FF + j * 128
                for k in range(3):
                    cp[:, l, j * 2 + s_, k] = cwt[l, k, f0:f0 + 128]
                cp[:, l, j * 2 + s_, 3] = cb[l, f0:f0 + 128]
    shared["convp"] = cp
    shared["wdn"] = f32(inp["ffn_w_down"])
    inv = (10000.0 ** (-np.arange(16, dtype=np.float32) / 16)).astype(np.float32)
    maps = []
    for core in range(8):
        b, r = core // 4, core % 4
        m = dict(shared)
        t0 = r * TPC
        m["xT"] = np.ascontiguousarray(x[b, t0:t0 + TPC, :].T)
        m["cT"] = np.ascontiguousarray(ctx[b].T)
        cv = np.zeros((128, NCH, 2), np.float32)
        cv[:, :, 0] = _fm(c[b])
        cv[:, :, 1] = _fm(c_ctx)
        m["cvec"] = cv
        t = np.arange(t0, t0 + TPC)
        row = (t // 64).astype(np.float32)
        col = (t % 64).astype(np.float32)
        ang = np.concatenate([row[None, :] * inv[:, None], col[None, :] * inv[:, None]], 0).astype(np.float32)
        cs_, sn_ = np.cos(ang).astype(np.float32), np.sin(ang).astype(np.float32)
        C64 = np.concatenate([cs_, cs_], 0)
        S64 = np.concatenate([-sn_, sn_], 0)
        m["ropeC"] = np.ascontiguousarray(np.concatenate([C64, C64], 0))
        m["ropeS"] = np.ascontiguousarray(np.concatenate([S64, S64], 0))
        fl = np.zeros((128, 2), np.float32)
        fl[:, 0] = 1.0 if r > 0 else 0.0
        fl[:, 1] = 1.0 if r < 3 else 0.0
        m["flags"] = fl
        sl = np.zeros((8, 2), np.float32)
        if r > 0:
            sl[2 * (r - 1) + 1, 0] = 1.0
        if r < 3:
            sl[2 * (r + 1), 1] = 1.0
        m["sel"] = sl
        maps.append(m)
    return maps


_NC_CACHE = {}


def run(inp, TPC, NL=DEPTH):
    key = (TPC, NL)
    if key not in _NC_CACHE:
        _NC_CACHE[key] = build(TPC, NL)
    nc = _NC_CACHE[key]
    maps = prep_inputs(inp, TPC)
    res = run_bass_kernel_spmd(nc, maps, core_ids=list(range(8)))
    B = 2
    out = np.zeros((B, 4 * TPC, D), np.float32)
    for core in range(8):
        b, r = core // 4, core % 4
        out[b, r * TPC:(r + 1) * TPC, :] = np.asarray(res.results[core]["yT"], np.float32).T
    return out


def kernel(**inputs):
    TPC = np.asarray(inputs["x"]).shape[1] // 4
    return run(inputs, TPC, DEPTH)
```

```python
import math
from contextlib import ExitStack

import numpy as np
import concourse.bass as bass
import concourse.mybir as mybir
from concourse.bass_utils import run_bass_kernel_spmd

F32 = mybir.dt.float32
BF16 = mybir.dt.bfloat16
AF = mybir.ActivationFunctionType
ALU = mybir.AluOpType

D = 1024
NCH = 8
CTX = 256
FF = 2816
NPAIR = 22
EPS = 1e-6
DEPTH = 4
ENGS = ("pe", "act", "dve", "pool", "sp")


class Res:
    __slots__ = ("w", "rc", "rd")

    def __init__(self):
        self.w = None
        self.rc = {}
        self.rd = []


class Op:
    __slots__ = ("eng", "kind", "fn", "deps", "idx", "needed", "count", "sem", "target")


class Prog:
    NS = 8

    def __init__(self, nc, es):
        self.nc = nc
        self.ops = {e: [] for e in ENGS}
        self.bar = {e: [] for e in ENGS}
        self.dma_n = {q: 0 for q in ("sp", "act", "pool")}
        self.dsem = {q: [es.enter_context(nc.semaphore(f"d_{q}_{i}")) for i in range(self.NS)]
                     for q in ("sp", "act", "pool")}
        self.csem = {e: es.enter_context(nc.semaphore(f"c_{e}")) for e in ("pe", "act", "dve", "pool")}
        self.es = es
        self.live_dma = []
        self.ncc = 0

    def _add(self, eng, kind, fn, r, w):
        o = Op()
        o.eng, o.kind, o.fn, o.needed, o.count, o.sem, o.target = eng, kind, fn, False, 0, None, 0
        o.idx = len(self.ops[eng])
        deps = {}

        def dep(x, why):
            if x is None or x is o:
                return
            k = id(x)
            if k in deps:
                if why == "raw":
                    deps[k] = (x, why)
                return
            deps[k] = (x, why)

        for x in r:
            dep(x.w, "raw")
        for x in w:
            dep(x.w, "waw")
            for rd in x.rc.values():
                dep(rd, "war")
            for rd in x.rd:
                dep(rd, "war")
        for x in self.bar[eng]:
            dep(x, "raw")
        self.bar[eng] = []
        o.deps = list(deps.values())
        for (x, why) in o.deps:
            if x.kind == "c":
                if x.eng != eng or kind != "c":
                    x.needed = True
                elif eng != "pe" and why == "raw" and (o.idx - x.idx) <= 2:
                    x.needed = True
        self.ops[eng].append(o)
        for x in r:
            if kind == "c":
                x.rc[eng] = o
            else:
                x.rd.append(o)
        for x in w:
            x.w = o
            x.rc = {}
            x.rd = []
        return o

    def op(self, eng, fn, r=(), w=()):
        return self._add(eng, "c", fn, r, w)

    def dma(self, q, fn, r=(), w=()):
        o = self._add(q, "d", fn, r, w)
        n = self.dma_n[q]
        self.dma_n[q] = n + 1
        o.sem = self.dsem[q][n % self.NS]
        o.target = 16 * (n // self.NS + 1)
        self.live_dma.append(o)
        return o

    def cc(self, fn, r=(), w=()):
        import os
        if os.environ.get("NO_CC"):
            return self._add("pool", "c", lambda e: e.engine_nop(), r, w)
        if not hasattr(self, "cc_chain"):
            self.cc_chain = Res()
        o = self._add("pool", "cc", fn, r, list(w) + [self.cc_chain])
        o.sem = self.es.enter_context(self.nc.semaphore(f"cc_{self.ncc}"))
        self.ncc += 1
        o.target = 1
        self.live_dma.append(o)
        return o

    def barrier(self):
        last = {}
        for e in ("pe", "act", "dve", "pool"):
            for o in reversed(self.ops[e]):
                if o.kind == "c":
                    last[e] = o
                    break
        for e in ENGS:
            self.bar[e] = [o for (k, o) in last.items() if k != e] + list(self.live_dma)
        self.live_dma = []

    def finish(self):
        self.barrier()
        self._add("sp", "nop", None, (), ())

    def emit(self):
        nc = self.nc
        for e in ("pe", "act", "dve", "pool"):
            c = 0
            for o in self.ops[e]:
                if o.kind == "c" and o.needed:
                    c += 1
                o.count = c if o.kind == "c" else 0
        prog = self

        def run(E, eng):
            seen = {}

            def wait(sem, val):
                k = id(sem)
                if seen.get(k, 0) >= val:
                    return
                eng.wait_ge(sem, val)
                seen[k] = val

            for o in prog.ops[E]:
                if o.kind == "d" and o.target > 16:
                    wait(o.sem, o.target - 16)
                for (x, why) in o.deps:
                    if x.kind == "c":
                        if x.eng == E and o.kind == "c":
                            if E == "pe" or why != "raw" or (o.idx - x.idx) > 2:
                                continue
                        wait(prog.csem[x.eng], x.count)
                    elif x.kind in ("d", "cc"):
                        wait(x.sem, x.target)
                if o.fn is None:
                    continue
                ins = o.fn(eng)
                if o.kind == "c":
                    if o.needed:
                        ins.then_inc(prog.csem[E], 1)
                elif o.kind == "d":
                    ins.then_inc(o.sem, 16)
                elif o.kind == "cc":
                    ins.then_inc(o.sem)

        with nc.Block() as block:
            @block.tensor
            def _(e):
                run("pe", e)

            @block.scalar
            def _(e):
                run("act", e)

            @block.vector
            def _(e):
                run("dve", e)

            @block.gpsimd
            def _(e):
                run("pool", e)

            @block.sync
            def _(e):
                run("sp", e)


def MM(out, lhsT, rhs, start=True, stop=True):
    return lambda e: e.matmul(out, lhsT, rhs, start=start, stop=stop)


def ACT(out, in_, func, bias=0.0, scale=1.0):
    return lambda e: e.activation(out, in_, func, bias=bias, scale=scale)


def TT(out, a, b, op):
    return lambda e: e.tensor_tensor(out, a, b, op)


def TS(out, a, s1, s2, op0, op1=None):
    if op1 is None:
        return lambda e: e.tensor_scalar(out, a, s1, None, op0)
    return lambda e: e.tensor_scalar(out, a, s1, s2, op0, op1)


def STT(out, a, s, b, op0, op1):
    return lambda e: e.scalar_tensor_tensor(out, a, s, b, op0, op1)


def CP(out, a):
    return lambda e: e.tensor_copy(out, a)


def RCP(out, a):
    return lambda e: e.reciprocal(out, a)


def MS(ap, v):
    return lambda e: e.memset(ap, v)


def DMA(out, in_, **kw):
    return lambda e: e.dma_start(out=out, in_=in_, **kw)


class Ring:
    def __init__(self, aps):
        self.aps = aps
        self.res = [Res() for _ in aps]
        self.i = 0

    def next(self):
        k = self.i % len(self.aps)
        self.i += 1
        return self.aps[k], self.res[k]


def build(TPC, NL=DEPTH):
    S = 4 * TPC
    NB = TPC // 512
    NKT = (CTX + S) // 128
    NLT = TPC // 128
    nc = bass.Bass("TRN2", target_bir_lowering=False)
    es = ExitStack()

    def din(name, shape, dt=F32):
        return nc.dram_tensor(name, list(shape), dt, kind="ExternalInput")

    def dscr(name, shape, dt):
        return nc.dram_tensor(name, list(shape), dt)

    xT_in = din("xT", [D, TPC])
    cT_in = din("cT", [D, CTX])
    cvec_in = din("cvec", [128, NCH, 2])
    adaw_in = din("ada_w", [DEPTH, D, 6 * D])
    adab_in = din("ada_b", [128, DEPTH, 48])
    mixn_in = din("mixn", [128, DEPTH, NCH])
    ffnn_in = din("ffnn", [128, DEPTH, NCH])
    finn_in = din("finn", [128, NCH])
    wqk_in = din("wqk", [2, D, 1664])
    wqs_in = din("wqs", [2, D, 1664])
    wv_in = din("wv", [2, D, 640])
    ggain_in = din("ggain", [128, 2, 5, 2])
    wod_in = din("wod", [2, 512, D])
    wog_in = din("wog", [2, 64, 8, D])
    lqk_in = din("lqk", [1, 2, 4, 64])
    subln_in = din("subln", [128, 2])
    ropeC_in = din("ropeC", [128, TPC])
    ropeS_in = din("ropeS", [128, TPC])
    swu_in = din("swu", [2, D, D])
    swv_in = din("swv", [2, D, D])
    svn_in = din("svn", [128, 2, NCH])
    swsT_in = din("swsT", [2, 128, 4, 128])
    sbsb_in = din("sbsb", [128, 2, 4, 128])
    swo_in = din("swo", [2, D, D])
    wup_in = din("wup", [DEPTH, D, NPAIR * 256])
    convp_in = din("convp", [128, DEPTH, 44, 4])
    wdn_in = din("wdn", [DEPTH, FF, D])
    flags_in = din("flags", [128, 2])
    sel_in = din("sel", [8, 2])
    yT_out = nc.dram_tensor("yT", [D, TPC], F32, kind="ExternalOutput")

    xw0 = dscr("xw0", [D, TPC + 2], F32)
    xw1 = dscr("xw1", [D, TPC + 2], F32)
    cw = dscr("cw", [D, CTX], F32)
    wqk_b = dscr("wqk_b", [2, D, 1664], BF16)
    wqs_b = dscr("wqs_b", [2, D, 1664], BF16)
    wv_b = dscr("wv_b", [2, D, 640], BF16)
    wod_b = dscr("wod_b", [2, 512, D], BF16)
    wog_b = dscr("wog_b", [2, 64, 8, D], BF16)
    swu_b = dscr("swu_b", [2, D, D], BF16)
    swv_b = dscr("swv_b", [2, D, D], BF16)
    swsT_b = dscr("swsT_b", [2, 128, 4, 128], BF16)
    swo_b = dscr("swo_b", [2, D, D], BF16)
    wup_b = dscr("wup_b", [DEPTH, D, NPAIR * 256], BF16)
    wdn_b = dscr("wdn_b", [DEPTH, FF, D], BF16)
    Qs = dscr("Qs", [8, 128, TPC], BF16)
    Qc = dscr("Qc", [8, 128, CTX], BF16)
    Klp = [dscr(f"Kl{p}", [64, TPC], BF16) for p in range(10)]
    Vlp = [dscr(f"Vl{p}", [64, TPC], BF16) for p in range(10)]
    Kap = [dscr(f"Ka{p}", [4 * 64, TPC], BF16) for p in range(10)]
    Vap = [dscr(f"Va{p}", [4 * 64, TPC], BF16) for p in range(10)]
    Kc = dscr("Kc", [640, CTX], BF16)
    Vc = dscr("Vc", [640, CTX], BF16)
    AT = dscr("AT", [12, 128, TPC], BF16)
    ATc = dscr("ATc", [12, 128, CTX], BF16)
    HLl = dscr("HLl", [2, D], F32)
    HLa = dscr("HLa", [8, D], F32)

    P = Prog(nc, es)

    R_xws = [[Res() for _ in range(NB)] for _ in range(2)]
    R_xhs = [Res(), Res()]
    cur = {"i": 0}
    R_cw = Res()
    R_w = {}

    uid = [0]

    def un(name):
        uid[0] += 1
        return f"{name}_u{uid[0]}"

    def sb(name, shape, dt):
        return es.enter_context(nc.sbuf_tensor(un(name), list(shape), dt))

    ones_bf = sb("ones_bf", [128, 128], BF16)
    bd_bf = sb("bd_bf", [128, 128], BF16)
    ones_f = sb("ones_f", [128, 128], F32)
    cvec = sb("cvec_s", [128, NCH, 2], F32)
    modraw = sb("modraw", [128, DEPTH, 48, 2], F32)
    adab = sb("adab_s", [128, DEPTH, 48], F32)
    mixn = sb("mixn_s", [128, DEPTH, NCH], F32)
    ffnn = sb("ffnn_s", [128, DEPTH, NCH], F32)
    finn = sb("finn_s", [128, NCH], F32)
    modv = sb("modv", [128, DEPTH, 6, 2, NCH], F32)
    ggain = sb("ggain_s", [128, 2, 5, 2], F32)
    subln = sb("subln_s", [128, 2], F32)
    lqk = sb("lqk_s", [1, 2, 4, 64], F32)
    lamt = sb("lamt", [1, 2, 8], F32)
    svn = sb("svn_s", [128, 2, NCH], F32)
    sbsb = sb("sbsb_s", [128, 2, 4, 128], F32)
    convp = sb("convp_s", [128, DEPTH, 44, 4], F32)
    flags = sb("flags_s", [128, 2], F32)
    sel = sb("sel_s", [8, 2], F32)
    R_const = Res()

    consts = [(cvec, cvec_in), (adab, adab_in), (mixn, mixn_in), (ffnn, ffnn_in), (finn, finn_in),
              (ggain, ggain_in), (subln, subln_in), (lqk, lqk_in), (svn, svn_in), (sbsb, sbsb_in),
              (convp, convp_in), (flags, flags_in), (sel, sel_in)]
    R_cs = []
    for (t, src) in consts:
        r_ = Res()
        R_cs.append(r_)
        P.dma("sp", DMA(t[:], src.ap()), w=[r_])
    R_ones = Res()
    P.op("pool", MS(ones_bf[:], 1.0), w=[R_ones])
    P.op("pool", MS(ones_f[:], 1.0), w=[R_ones])
    P.op("pool", MS(bd_bf[:], 0.0), w=[R_ones])
    P.op("pool", MS(bd_bf[0:64, 0:64], 1.0), w=[R_ones])
    P.op("pool", MS(bd_bf[64:128, 64:128], 1.0), w=[R_ones])

    def conv_w(key, src_ap, dst_ap, n_el):
        r_ = Res()
        R_w[key] = r_
        rows = n_el // 1024
        s2 = src_ap
        d2 = dst_ap
        step = 8192
        for r0 in range(0, rows, step):
            r1 = min(rows, r0 + step)
            P.dma("pool", DMA(d2[r0:r1, :], s2[r0:r1, :]), w=[r_])

    def flat2(h, idx, n_el):
        ap = h.ap()[idx]
        names = " ".join(f"d{i}" for i in range(len(ap.shape)))
        ap = ap.rearrange(f"{names} -> ({names})")
        return ap.rearrange("(r c) -> r c", c=1024)

    def conv_layer_weights(l):
        i = l // 2
        if l % 2 == 0:
            for (nm, src, dst, n) in (("wqk", wqk_in, wqk_b, D * 1664), ("wqs", wqs_in, wqs_b, D * 1664),
                                      ("wv", wv_in, wv_b, D * 640), ("wod", wod_in, wod_b, 512 * D),
                                      ("wog", wog_in, wog_b, 64 * 8 * D)):
                conv_w((nm, i), flat2(src, i, n), flat2(dst, i, n), n)
        else:
            for (nm, src, dst, n) in (("swu", swu_in, swu_b, D * D), ("swv", swv_in, swv_b, D * D),
                                      ("swsT", swsT_in, swsT_b, 128 * 4 * 128), ("swo", swo_in, swo_b, D * D)):
                conv_w((nm, i), flat2(src, i, n), flat2(dst, i, n), n)
        conv_w(("wup", l), flat2(wup_in, l, D * NPAIR * 256), flat2(wup_b, l, D * NPAIR * 256), D * NPAIR * 256)
        conv_w(("wdn", l), flat2(wdn_in, l, FF * D), flat2(wdn_b, l, FF * D), FF * D)

    for l in range(NL):
        conv_layer_weights(l)

    xw_fms = [xw0.ap().rearrange("(c p) t -> p c t", p=128), xw1.ap().rearrange("(c p) t -> p c t", p=128)]
    cw_fm = cw.ap().rearrange("(c p) t -> p c t", p=128)
    for b in range(NB):
        P.dma("sp", DMA(xw0.ap()[:, 1 + b * 512:1 + (b + 1) * 512], xT_in.ap()[:, b * 512:(b + 1) * 512]),
              w=[R_xws[0][b]])
    P.dma("sp", DMA(cw.ap(), cT_in.ap()), w=[R_cw])

    with ExitStack() as ph:
        def sbp(name, shape, dt):
            return ph.enter_context(nc.sbuf_tensor(un(name), list(shape), dt))

        def psp(name, shape, dt=F32):
            return ph.enter_context(nc.psum_tensor(un(name), list(shape), dt))

        scv = sbp("scv", [128, NCH, 2], F32)
        R_scv = Res()
        P.op("act", ACT(scv[:], cvec[:], AF.Silu), r=[R_cs[0]], w=[R_scv])
        wst = Ring([sbp(f"adaw{i}", [128, NCH, 768], F32) for i in range(2)])
        pm = Ring([psp(f"pm{i}", [128, 512]) for i in range(2)])
        for l in range(NL):
            for cg in range(8):
                wt, wr = wst.next()
                P.dma("sp", DMA(wt[:], adaw_in.ap()[l].rearrange("(k p) n -> p k n", p=128)[:, :, cg * 768:(cg + 1) * 768]),
                      w=[wr])
                pt, pr = pm.next()
                for j in range(6):
                    for k in range(NCH):
                        P.op("pe", MM(pt[:, 2 * j:2 * j + 2], wt[:, k, j * 128:(j + 1) * 128], scv[:, k, :],
                                      start=(k == 0), stop=(k == NCH - 1)), r=[wr, R_scv], w=[pr])
                for s_ in range(2):
                    P.op("dve", TT(modraw[:, l, cg * 6:(cg + 1) * 6, s_], pt[:, s_:12:2],
                                   adab[:, l, cg * 6:(cg + 1) * 6], ALU.add), r=[pr, R_cs[1]], w=[R_const])
        for l in range(NL):
            for s_ in range(2):
                mr = lambda m: modraw[:, l, m * 8:(m + 1) * 8, s_]
                P.op("dve", STT(modv[:, l, 0, s_, :], mr(1), 1.0, mixn[:, l, :], ALU.add, ALU.mult),
                     r=[R_const, R_cs[2]], w=[R_const])
                P.op("dve", CP(modv[:, l, 1, s_, :], mr(0)), r=[R_const], w=[R_const])
                P.op("dve", CP(modv[:, l, 2, s_, :], mr(2)), r=[R_const], w=[R_const])
                P.op("dve", STT(modv[:, l, 3, s_, :], mr(4), 1.0, ffnn[:, l, :], ALU.add, ALU.mult),
                     r=[R_const, R_cs[3]], w=[R_const])
                P.op("dve", CP(modv[:, l, 4, s_, :], mr(3)), r=[R_const], w=[R_const])
                P.op("dve", CP(modv[:, l, 5, s_, :], mr(5)), r=[R_const], w=[R_const])
        P.barrier()

    def norm_mod(env, xt, rx, N, A, Bv, h, rh, flag=None):
        pt, pr = env["ps"].next()
        for c in range(NCH):
            sq, rs = env["sq"].next()
            P.op("act", ACT(sq[:, :N], xt[:, c, :N], AF.Square), r=[rx], w=[rs])
            P.op("pe", MM(pt[:, :N], ones_bf[:], sq[:, :N], start=(c == 0), stop=(c == NCH - 1)),
                 r=[rs, R_ones], w=[pr])
        rt, rr = env["rstd"].next()
        P.op("act", ACT(rt[:, :N], pt[:, :N], AF.Sqrt, bias=env["eps"][:, 0:1], scale=1.0 / D), r=[pr, env["reps"]], w=[rr])
        P.op("dve", RCP(rt[:, :N], rt[:, :N]), r=[rr], w=[rr])
        for c in range(NCH):
            tt, tr = env["tmp"].next()
            P.op("dve", TT(tt[:, :N], xt[:, c, :N], rt[:, :N], ALU.mult), r=[rx, rr], w=[tr])
            if Bv is not None:
                P.op("act", ACT(h[:, c, :N], tt[:, :N], AF.Identity, bias=Bv[:, c:c + 1], scale=A[:, c:c + 1]),
                     r=[tr, R_const], w=[rh])
            else:
                P.op("act", ACT(h[:, c, :N], tt[:, :N], AF.Identity, scale=A[:, c:c + 1]), r=[tr, R_const], w=[rh])
            if flag is not None:
                P.op("dve", TS(h[:, c, :N], h[:, c, :N], flag, None, ALU.mult), r=[rh, R_cs[11]], w=[rh])

    epsT = sb("epsT", [128, 1], F32)
    R_eps = Res()
    P.op("pool", MS(epsT[:], EPS), w=[R_eps])

    def make_env(ph, tag, ps_n=1):
        def sbp(name, shape, dt):
            return ph.enter_context(nc.sbuf_tensor(un(name), list(shape), dt))
        return {
            "sq": Ring([sbp(f"{tag}_sq{i}", [128, 512], BF16) for i in range(3)]),
            "tmp": Ring([sbp(f"{tag}_tmp{i}", [128, 512], F32) for i in range(3)]),
            "rstd": Ring([sbp(f"{tag}_rstd{i}", [128, 512], F32) for i in range(2)]),
            "ps": Ring([ph.enter_context(nc.psum_tensor(un(f"{tag}_nps{i}"), [128, 512], F32)) for i in range(ps_n)]),
            "eps": epsT, "reps": R_eps,
        }

    def attention_layer(l):
        i = l // 2
        xw_fm, R_xw = xw_fms[cur["i"]], R_xws[cur["i"]]
        with_ctx_out = l < 2
        lam_init = 0.8 - 0.6 * math.exp(-0.3 * l)
        R_Qs = [[Res() for _ in range(NB)] for _ in range(8)]
        R_Qc = [Res() for _ in range(8)]
        R_Kl = [Res() for _ in range(10)]
        R_Vl = [Res() for _ in range(10)]
        R_Ka = [Res() for _ in range(10)]
        R_Va = [Res() for _ in range(10)]
        R_Kc, R_Vc = Res(), Res()
        R_AT = [[Res() for _ in range(NB)] for _ in range(12)]
        R_ATc = [Res() for _ in range(12)]

        R_lam = Res()
        with ExitStack() as ph:
            prod = ph.enter_context(nc.sbuf_tensor(un("lprod"), [1, 2, 64], F32))
            rp = Res()
            P.op("dve", TT(prod[:, 0, :], lqk[:, i, 0, :], lqk[:, i, 1, :], ALU.mult), r=[R_cs[7]], w=[rp])
            P.op("dve", TT(prod[:, 1, :], lqk[:, i, 2, :], lqk[:, i, 3, :], ALU.mult), r=[R_cs[7]], w=[rp])
            P.op("dve", lambda e: e.reduce_sum(lamt[:, i, 0:2], prod[:], mybir.AxisListType.X), r=[rp], w=[R_lam])
            P.op("act", ACT(lamt[:, i, 2:4], lamt[:, i, 0:2], AF.Exp), r=[R_lam], w=[R_lam])
            P.op("dve", TT(lamt[:, i, 4:5], lamt[:, i, 3:4], lamt[:, i, 2:3], ALU.subtract), r=[R_lam], w=[R_lam])
            P.op("dve", TS(lamt[:, i, 5:6], lamt[:, i, 4:5], -lam_init, None, ALU.add), r=[R_lam], w=[R_lam])
            P.barrier()
        nlam = lamt[0:1, i, 5:6]

        with ExitStack() as ph:
            def sbp(name, shape, dt):
                return ph.enter_context(nc.sbuf_tensor(un(name), list(shape), dt))

            def psp(name, shape, dt=F32):
                return ph.enter_context(nc.psum_tensor(un(name), list(shape), dt))

            env = make_env(ph, f"a1_{l}")
            wqk = sbp("wqk_s", [128, NCH, 1664], BF16)
            wqs = sbp("wqs_s", [128, NCH, 1664], BF16)
            wv = sbp("wv_s", [128, NCH, 640], BF16)
            R_wq, R_wqs_, R_wv_ = Res(), Res(), Res()
            P.dma("sp", DMA(wqk[:], wqk_b.ap()[i].rearrange("(k p) n -> p k n", p=128)), r=[R_w[("wqk", i)]], w=[R_wq])
            P.dma("sp", DMA(wqs[:], wqs_b.ap()[i].rearrange("(k p) n -> p k n", p=128)), r=[R_w[("wqs", i)]], w=[R_wqs_])
            P.dma("sp", DMA(wv[:], wv_b.ap()[i].rearrange("(k p) n -> p k n", p=128)), r=[R_w[("wv", i)]], w=[R_wv_])
            xring = Ring([sbp(f"a1x{j}", [128, NCH, 512], F32) for j in range(2)])
            hring = Ring([sbp(f"a1h{j}", [128, NCH, 512], BF16) for j in range(2)])
            ropeC = Ring([sbp(f"rC{j}", [128, 512], F32) for j in range(2)])
            ropeS = Ring([sbp(f"rS{j}", [128, 512], F32) for j in range(2)])
            pA = Ring([psp(f"pA{j}", [128, 512]) for j in range(2)])
            pB = Ring([psp(f"pB{j}", [128, 512]) for j in range(2)])
            pG = Ring([psp("pG0", [128, 512])])
            pV = Ring([psp(f"pV{j}", [128, 640]) for j in range(1)])
            t1r = Ring([sbp(f"t1_{j}", [128, 512], F32) for j in range(2)])
            t2r = Ring([sbp(f"t2_{j}", [128, 512], F32) for j in range(2)])
            gsq = Ring([sbp(f"gsq{j}", [128, 512], BF16) for j in range(2)])
            grs = Ring([sbp(f"grs{j}", [128, 512], F32) for j in range(2)])
            outr = Ring([sbp(f"qko{j}", [128, 512], BF16) for j in range(3)])
            vrow = Ring([sbp(f"vrow{j}", [128, 640], BF16) for j in range(2)])

            def project(N, src_fm, rsrc, col0, stream, b_idx):
                lat = stream == 0
                xt, rx = xring.next()
                P.dma("sp", DMA(xt[:, :, :N], src_fm[:, :, col0:col0 + N]), r=[rsrc], w=[rx])
                ht, rh = hring.next()
                norm_mod(env, xt, rx, N, modv[:, l, 0, stream, :], modv[:, l, 1, stream, :], ht, rh)
                if lat:
                    rc, rrc = ropeC.next()
                    rs_, rrs = ropeS.next()
                    P.dma("sp", DMA(rc[:], ropeC_in.ap()[:, b_idx * 512:(b_idx + 1) * 512]), w=[rrc])
                    P.dma("sp", DMA(rs_[:], ropeS_in.ap()[:, b_idx * 512:(b_idx + 1) * 512]), w=[rrs])
                for t in range(13):
                    is_q = t < 4 or 8 <= t < 12
                    if is_q and (not lat) and (not with_ctx_out):
                        continue
                    gq = t >= 8
                    pa, rpa = pA.next()
                    for k in range(NCH):
                        P.op("pe", MM(pa[:, :N], wqk[:, k, t * 128:(t + 1) * 128], ht[:, k, :N],
                                      start=(k == 0), stop=(k == NCH - 1)), r=[R_wq, rh], w=[rpa])
                    if lat:
                        pb, rpb = pB.next()
                        for k in range(NCH):
                            P.op("pe", MM(pb[:, :N], wqs[:, k, t * 128:(t + 1) * 128], ht[:, k, :N],
                                          start=(k == 0), stop=(k == NCH - 1)), r=[R_wqs_, rh], w=[rpb])
                    if gq:
                        g_idx = t - 8
                        sqt, rsq = gsq.next()
                        P.op("act", ACT(sqt[:, :N], pa[:, :N], AF.Square), r=[rpa], w=[rsq])
                        pg, rpg = pG.next()
                        P.op("pe", MM(pg[:, :N], bd_bf[:], sqt[:, :N]), r=[rsq, R_ones], w=[rpg])
                        rst, rrst = grs.next()
                        P.op("act", ACT(rst[:, :N], pg[:, :N], AF.Sqrt, bias=epsT[:, 0:1], scale=1.0 / 64), r=[rpg, R_eps], w=[rrst])
                        P.op("dve", RCP(rst[:, :N], rst[:, :N]), r=[rrst], w=[rrst])
                    ot, rot = outr.next()
                    if lat:
                        t1, rt1 = t1r.next()
                        t2, rt2 = t2r.next()
                        if gq:
                            P.op("dve", STT(t1[:, :N], pa[:, :N], ggain[:, i, g_idx, 0:1], rc[:, :N], ALU.mult, ALU.mult),
                                 r=[rpa, rrc, R_cs[5]], w=[rt1])
                            P.op("dve", STT(t2[:, :N], pb[:, :N], ggain[:, i, g_idx, 1:2], rs_[:, :N], ALU.mult, ALU.mult),
                                 r=[rpb, rrs, R_cs[5]], w=[rt2])
                            P.op("pool", TT(t1[:, :N], t1[:, :N], t2[:, :N], ALU.add), r=[rt1, rt2], w=[rt1])
                            P.op("dve", TT(ot[:, :N], t1[:, :N], rst[:, :N], ALU.mult), r=[rt1, rrst], w=[rot])
                        else:
                            P.op("dve", TT(t1[:, :N], pa[:, :N], rc[:, :N], ALU.mult), r=[rpa, rrc], w=[rt1])
                            P.op("dve", TT(t2[:, :N], pb[:, :N], rs_[:, :N], ALU.mult), r=[rpb, rrs], w=[rt2])
                            P.op("pool", TT(ot[:, :N], t1[:, :N], t2[:, :N], ALU.add), r=[rt1, rt2], w=[rot])
                    else:
                        if gq:
                            P.op("dve", STT(ot[:, :N], pa[:, :N], ggain[:, i, g_idx, 0:1], rst[:, :N], ALU.mult, ALU.mult),
                                 r=[rpa, rrst, R_cs[5]], w=[rot])
                        else:
                            P.op("act", ACT(ot[:, :N], pa[:, :N], AF.Identity), r=[rpa], w=[rot])
                    if is_q:
                        qi = t if t < 4 else t - 4
                        if lat:
                            P.dma("sp", DMA(Qs.ap()[qi, :, col0 - 1:col0 - 1 + N], ot[:, :N]), r=[rot], w=[R_Qs[qi][b_idx]])
                        else:
                            P.dma("sp", DMA(Qc.ap()[qi], ot[:, :N]), r=[rot], w=[R_Qc[qi]])
                    else:
                        ki = t - 4 if t < 8 else 4
                        if lat:
                            for hf in range(2):
                                P.dma("sp", DMA(Klp[2 * ki + hf].ap()[:, col0 - 1:col0 - 1 + N], ot[hf * 64:(hf + 1) * 64, :N]),
                                      r=[rot], w=[R_Kl[2 * ki + hf]])
                        else:
                            P.dma("sp", DMA(Kc.ap()[ki * 128:(ki + 1) * 128, :], ot[:, :N]), r=[rot], w=[R_Kc])
                for tt_ in range(N // 128):
                    pv, rpv = pV.next()
                    for k in range(NCH):
                        P.op("pe", MM(pv[:, 0:512], ht[:, k, tt_ * 128:(tt_ + 1) * 128], wv[:, k, 0:512],
                                      start=(k == 0), stop=(k == NCH - 1)), r=[R_wv_, rh], w=[rpv])
                    for k in range(NCH):
                        P.op("pe", MM(pv[:, 512:640], ht[:, k, tt_ * 128:(tt_ + 1) * 128], wv[:, k, 512:640],
                                      start=(k == 0), stop=(k == NCH - 1)), r=[R_wv_, rh], w=[rpv])
                    vr, rvr = vrow.next()
                    P.op("act", ACT(vr[:], pv[:], AF.Identity), r=[rpv], w=[rvr])
                    if lat:
                        tile_idx = (col0 - 1) // 128 + tt_
                        for gi_ in range(5):
                            for hf in range(2):
                                P.dma("sp", DMA(Vlp[2 * gi_ + hf].ap()[:, tile_idx * 128:(tile_idx + 1) * 128],
                                                vr[hf * 64:(hf + 1) * 64, gi_ * 128:(gi_ + 1) * 128]), r=[rvr], w=[R_Vl[2 * gi_ + hf]])
                    else:
                        dst = Vc.ap().rearrange("(g p) (t c) -> p g t c", p=128, c=128)[:, :, tt_, :]
                        P.dma("sp", DMA(dst, vr[:].rearrange("p (g c) -> p g c", c=128)), r=[rvr], w=[R_Vc])

            project(CTX, cw_fm, R_cw, 0, 1, 0)
            for b in range(NB):
                project(512, xw_fm, R_xw[b], 1 + b * 512, 0, b)
            P.barrier()

        groups = [[0, 1, 2, 3], [4, 5, 6, 7]]
        for p_ in range(10):
            P.cc(lambda e, p_=p_: e.collective_compute("AllGather", ALU.bypass, replica_groups=groups,
                                                       ins=[Klp[p_].ap().opt()], outs=[Kap[p_].ap().opt()]),
                 r=[R_Kl[p_]], w=[R_Ka[p_]])
            P.cc(lambda e, p_=p_: e.collective_compute("AllGather", ALU.bypass, replica_groups=groups,
                                                       ins=[Vlp[p_].ap().opt()], outs=[Vap[p_].ap().opt()]),
                 r=[R_Vl[p_]], w=[R_Va[p_]])

        with ExitStack() as ph:
            def sbp(name, shape, dt):
                return ph.enter_context(nc.sbuf_tensor(un(name), list(shape), dt))

            def psp(name, shape, dt=F32):
                return ph.enter_context(nc.psum_tensor(un(name), list(shape), dt))

            KB = [sbp(f"KB{j}", [128, CTX + S], BF16) for j in range(2)]
            VB = [sbp(f"VB{j}", [128, NKT * 130], BF16) for j in range(2)]
            R_KB = [[Res() for _ in range(9)] for _ in range(2)]
            R_VB = [[Res() for _ in range(9)] for _ in range(2)]
            qring = Ring([sbp(f"qt{j}", [128, 512], BF16) for j in range(2)])
            pring = Ring([sbp(f"pt{j}", [128, 512], BF16) for j in range(8)])
            accr = Ring([sbp(f"acc{j}", [128, 512], F32) for j in range(4)])
            psS = Ring([psp(f"psS{j}", [128, 512]) for j in range(4)])
            psO = Ring([psp(f"psO{j}", [128, 512]) for j in range(3)])
            psE = Ring([psp("psE0", [128, 512])])
            osb = Ring([sbp(f"osb{j}", [128, 512], F32) for j in range(4)])
            lsb = Ring([sbp(f"lsb{j}", [128, 512], F32) for j in range(2)])
            bcs = Ring([sbp(f"bcs{j}", [128, 512], F32) for j in range(2)])
            ework = Ring([sbp(f"ew{j}", [128, 512], F32) for j in range(3)])
            esq = Ring([sbp("esq0", [128, 512], BF16)])
            aout = Ring([sbp(f"aout{j}", [128, 512], BF16) for j in range(2)])

            def load_kv(g, slot):
                kb, vb = KB[slot], VB[slot]
                rk, rv = R_KB[slot], R_VB[slot]
                P.dma("sp", DMA(kb[:, 0:CTX], Kc.ap()[g * 128:(g + 1) * 128, :]), r=[R_Kc], w=[rk[0]])
                for r_ in range(4):
                    for hf in range(2):
                        P.dma("sp", DMA(kb[hf * 64:(hf + 1) * 64, CTX + r_ * TPC:CTX + (r_ + 1) * TPC],
                                        Kap[2 * g + hf].ap()[r_ * 64:(r_ + 1) * 64, :]), r=[R_Ka[2 * g + hf]], w=[rk[1 + 2 * r_ + hf]])
                if g < 4:
                    v3 = vb[:, 0:NKT * 128].rearrange("p (t c) -> p t c", c=128)
                    P.dma("sp", DMA(v3[:, 0:2, :], Vc.ap()[g * 128:(g + 1) * 128, :].rearrange("p (t c) -> p t c", c=128)),
                          r=[R_Vc], w=[rv[0]])
                    for r_ in range(4):
                        for hf in range(2):
                            P.dma("sp", DMA(v3[hf * 64:(hf + 1) * 64, 2 + r_ * NLT:2 + (r_ + 1) * NLT, :],
                                            Vap[2 * g + hf].ap()[r_ * 64:(r_ + 1) * 64, :].rearrange("p (t c) -> p t c", c=128)),
                                  r=[R_Va[2 * g + hf]], w=[rv[1 + 2 * r_ + hf]])
                else:
                    v3 = vb[:].rearrange("p (t c) -> p t c", c=130)
                    for hh in range(2):
                        P.dma("sp", DMA(v3[:, 0:2, hh * 65:hh * 65 + 64],
                                        Vc.ap()[g * 128:(g + 1) * 128, :].rearrange("p (t c) -> p t c", c=128)[:, :, hh * 64:(hh + 1) * 64]),
                              r=[R_Vc], w=[rv[0]])
                        for r_ in range(4):
                            for hf in range(2):
                                P.dma("sp", DMA(v3[hf * 64:(hf + 1) * 64, 2 + r_ * NLT:2 + (r_ + 1) * NLT, hh * 65:hh * 65 + 64],
                                                Vap[2 * g + hf].ap()[r_ * 64:(r_ + 1) * 64, :].rearrange("p (t c) -> p t c", c=128)[:, :, hh * 64:(hh + 1) * 64]),
                                      r=[R_Va[2 * g + hf]], w=[rv[1 + 2 * r_ + hf]])
                    for hh in range(2):
                        P.op("pool", MS(v3[:, :, hh * 65 + 64:hh * 65 + 65], 1.0), r=[], w=rv)

            def unit_pair(g, slot, qt, rq, N, nkt):
                kb, vb = KB[slot], VB[slot]
                rk, rv = R_KB[slot], R_VB[slot]
                diff = g < 4
                if diff:
                    v3 = vb[:, 0:NKT * 128].rearrange("p (t c) -> p t c", c=128)
                    cd = [(0, 128), (0, 128)]
                else:
                    v3 = vb[:].rearrange("p (t c) -> p t c", c=130)
                    cd = [(0, 65), (65, 65)]
                po = [psO.next() for _ in range(2)]
                acc = [accr.next() for _ in range(2)] if diff else None
                ps_list = {}

                def issue_S(kt):
                    for m in range(2):
                        st, rst_ = psS.next()
                        ps_list[(kt, m)] = (st, rst_)
                        P.op("pe", MM(st[:, :N], kb[m * 64:(m + 1) * 64, kt * 128:(kt + 1) * 128],
                                      qt[m * 64:(m + 1) * 64, :N]), r=rk + [rq], w=[rst_])

                for kt in range(min(2, nkt)):
                    issue_S(kt)
                for kt in range(nkt):
                    pts = []
                    for m in range(2):
                        st, rst_ = ps_list.pop((kt, m))
                        pt_, rpt = pring.next()
                        P.op("act", ACT(pt_[:, :N], st[:, :N], AF.Exp, scale=0.125), r=[rst_], w=[rpt])
                        pts.append((pt_, rpt))
                    for m in range(2):
                        pt_, rpt = pts[m]
                        c0, dvp = cd[m]
                        P.op("pe", MM(po[m][0][0:dvp, :N], v3[:, kt, c0:c0 + dvp], pt_[:, :N],
                                      start=(kt == 0), stop=(kt == nkt - 1)), r=rv + [rpt], w=[po[m][1]])
                    if diff:
                        for m in range(2):
                            pt_, rpt = pts[m]
                            at_, rat_ = acc[m]
                            if kt == 0:
                                P.op("dve", CP(at_[:, :N], pt_[:, :N]), r=[rpt], w=[rat_])
                            else:
                                P.op("dve", TT(at_[:, :N], at_[:, :N], pt_[:, :N], ALU.add), r=[rpt, rat_], w=[rat_])
                    if kt + 2 < nkt:
                        issue_S(kt + 2)
                outs = []
                for m in range(2):
                    c0, dvp = cd[m]
                    ot, rot = osb.next()
                    P.op("dve", CP(ot[0:dvp, :N], po[m][0][0:dvp, :N]), r=[po[m][1]], w=[rot])
                    if diff:
                        at_, rat_ = acc[m]
                        pe_, rpe = psE.next()
                        P.op("pe", MM(pe_[0:1, :N], ones_f[:, 0:1], at_[:, :N]), r=[rat_, R_ones], w=[rpe])
                        lt, rlt = lsb.next()
                        P.op("dve", CP(lt[0:1, :N], pe_[0:1, :N]), r=[rpe], w=[rlt])
                        outs.append((ot, rot, lt, rlt))
                    else:
                        outs.append((ot, rot, None, None))
                return outs

            def bcast_row(row_ap, rrow, base, nparts, N):
                pe_, rpe = psE.next()
                P.op("pe", MM(pe_[0:nparts, :N], ones_f[base:base + 1, 0:nparts], row_ap), r=[rrow, R_ones], w=[rpe])
                bt, rbt = bcs.next()
                P.op("act", ACT(bt[0:nparts, :N], pe_[0:nparts, :N], AF.Identity), r=[rpe], w=[rbt])
                return bt, rbt

            def epilogue_gqa(ot, rot, N, dst_ap, rdst):
                P.op("dve", RCP(ot[64:65, :N], ot[64:65, :N]), r=[rot], w=[rot])
                bt, rbt = bcast_row(ot[64:65, :N], rot, 64, 64, N)
                at, rat = aout.next()
                P.op("dve", TT(at[0:64, :N], ot[0:64, :N], bt[0:64, :N], ALU.mult), r=[rot, rbt], w=[rat])
                P.dma("sp", DMA(dst_ap, at[0:64, :N]), r=[rat], w=[rdst])

            def epilogue_diff(o1, ro1, l1, rl1, o2, ro2, l2, rl2, N, dst_ap, rdst):
                P.op("dve", RCP(l1[0:1, :N], l1[0:1, :N]), r=[rl1], w=[rl1])
                P.op("dve", RCP(l2[0:1, :N], l2[0:1, :N]), r=[rl2], w=[rl2])
                P.op("dve", TS(l2[0:1, :N], l2[0:1, :N], nlam, None, ALU.mult), r=[rl2, R_lam], w=[rl2])
                b1, rb1 = bcast_row(l1[0:1, :N], rl1, 0, 128, N)
                w1, rw1 = ework.next()
                P.op("dve", TT(w1[:, :N], o1[:, :N], b1[:, :N], ALU.mult), r=[ro1, rb1], w=[rw1])
                b2, rb2 = bcast_row(l2[0:1, :N], rl2, 0, 128, N)
                w2, rw2 = ework.next()
                P.op("dve", TT(w2[:, :N], o2[:, :N], b2[:, :N], ALU.mult), r=[ro2, rb2], w=[rw2])
                P.op("dve", TT(w1[:, :N], w1[:, :N], w2[:, :N], ALU.add), r=[rw1, rw2], w=[rw1])
                sq, rsq = esq.next()
                P.op("act", ACT(sq[:, :N], w1[:, :N], AF.Square), r=[rw1], w=[rsq])
                pe_, rpe = psE.next()
                P.op("pe", MM(pe_[:, :N], ones_bf[:], sq[:, :N]), r=[rsq, R_ones], w=[rpe])
                w3, rw3 = ework.next()
                P.op("act", ACT(w3[:, :N], pe_[:, :N], AF.Sqrt, bias=epsT[:, 0:1], scale=1.0 / 128), r=[rpe, R_eps], w=[rw3])
                P.op("dve", RCP(w3[:, :N], w3[:, :N]), r=[rw3], w=[rw3])
                P.op("dve", TT(w1[:, :N], w1[:, :N], w3[:, :N], ALU.mult), r=[rw1, rw3], w=[rw1])
                at, rat = aout.next()
                P.op("dve", TS(at[:, :N], w1[:, :N], subln[:, i:i + 1], 1.0 - lam_init, ALU.mult, ALU.mult),
                     r=[rw1, R_cs[6]], w=[rat])
                P.dma("sp", DMA(dst_ap, at[:, :N]), r=[rat], w=[rdst])

            def run_group(g, slot, lat_chunks=True):
                qtiles = [g] if g < 4 else [4, 5, 6, 7]
                chunks = []
                if with_ctx_out:
                    chunks.append(("ctx", 0))
                chunks += [("lat", b) for b in range(NB)]
                for qi in qtiles:
                    for (kind, b) in chunks:
                        qt, rq = qring.next()
                        if kind == "ctx":
                            N, nkt = CTX, CTX // 128
                            P.dma("sp", DMA(qt[:, :N], Qc.ap()[qi]), r=[R_Qc[qi]], w=[rq])
                        else:
                            N, nkt = 512, NKT
                            P.dma("sp", DMA(qt[:, :N], Qs.ap()[qi, :, b * 512:(b + 1) * 512]), r=[R_Qs[qi][b]], w=[rq])
                        res = unit_pair(g, slot, qt, rq, N, nkt)
                        if g < 4:
                            if kind == "ctx":
                                dst, rdst = ATc.ap()[g], R_ATc[g]
                            else:
                                dst, rdst = AT.ap()[g, :, b * 512:(b + 1) * 512], R_AT[g][b]
                            epilogue_diff(res[0][0], res[0][1], res[0][2], res[0][3],
                                          res[1][0], res[1][1], res[1][2], res[1][3], N, dst, rdst)
                        else:
                            j = qi - 4
                            for m in range(2):
                                hidx = 4 + j + 4 * m
                                if kind == "ctx":
                                    dst, rdst = ATc.ap()[hidx, 0:64, :], R_ATc[hidx]
                                else:
                                    dst, rdst = AT.ap()[hidx, 0:64, b * 512:(b + 1) * 512], R_AT[hidx][b]
                                epilogue_gqa(res[m][0], res[m][1], N, dst, rdst)

            load_kv(0, 0)
            for g in range(5):
                if g + 1 < 5:
                    load_kv(g + 1, (g + 1) % 2)
                run_group(g, g % 2)
            P.barrier()

        with ExitStack() as ph:
            def sbp(name, shape, dt):
                return ph.enter_context(nc.sbuf_tensor(un(name), list(shape), dt))

            def psp(name, shape, dt=F32):
                return ph.enter_context(nc.psum_tensor(un(name), list(shape), dt))

            wod = sbp("wod_s", [128, 4, D], BF16)
            wog = sbp("wog_s", [64, 8, D], BF16)
            R_wod, R_wog = Res(), Res()
            P.dma("sp", DMA(wod[:], wod_b.ap()[i].rearrange("(k p) n -> p k n", p=128)), r=[R_w[("wod", i)]], w=[R_wod])
            P.dma("sp", DMA(wog[:], wog_b.ap()[i]), r=[R_w[("wog", i)]], w=[R_wog])
            atr = Ring([sbp(f"atb{j}", [128, 12, 512], BF16) for j in range(2)])
            xr = Ring([sbp(f"a4x{j}", [128, NCH, 512], F32) for j in range(2)])
            py = Ring([psp(f"py{j}", [128, 512]) for j in range(3)])

            def outproj(N, at_src, rat_list, x_fm, rx_dram, col0, stream):
                at_, rat_ = atr.next()
                P.dma("sp", DMA(at_[:, 0:4, :N], at_src[0:4].rearrange("a p t -> p a t")), r=rat_list[0:4], w=[rat_])
                P.dma("sp", DMA(at_[0:64, 4:12, :N], at_src[4:12, 0:64, :].rearrange("a p t -> p a t")), r=rat_list[4:12], w=[rat_])
                xt, rx = xr.next()
                P.dma("sp", DMA(xt[:, :, :N], x_fm[:, :, col0:col0 + N]), r=[rx_dram], w=[rx])
                for fo in range(NCH):
                    pt_, rp = py.next()
                    for a in range(4):
                        P.op("pe", MM(pt_[:, :N], wod[:, a, fo * 128:(fo + 1) * 128], at_[:, a, :N],
                                      start=(a == 0), stop=False), r=[R_wod, rat_], w=[rp])
                    for a in range(8):
                        P.op("pe", MM(pt_[:, :N], wog[:, a, fo * 128:(fo + 1) * 128], at_[0:64, 4 + a, :N],
                                      start=False, stop=(a == 7)), r=[R_wog, rat_], w=[rp])
                    P.op("dve", STT(xt[:, fo, :N], pt_[:, :N], modv[:, l, 2, stream, fo:fo + 1], xt[:, fo, :N], ALU.mult, ALU.add),
                         r=[rp, rx, R_const], w=[rx])
                P.dma("sp", DMA(x_fm[:, :, col0:col0 + N], xt[:, :, :N]), r=[rx], w=[rx_dram])

            if with_ctx_out:
                outproj(CTX, ATc.ap(), R_ATc, cw_fm, R_cw, 0, 1)
            for b in range(NB):
                outproj(512, AT.ap()[:, :, b * 512:(b + 1) * 512], [R_AT[a][b] for a in range(12)], xw_fm, R_xw[b], 1 + b * 512, 0)
            P.barrier()

    def sgu_layer(l):
        i = l // 2
        xw_fm, R_xw = xw_fms[cur["i"]], R_xws[cur["i"]]
        with_ctx = l < 2
        with ExitStack() as ph:
            def sbp(name, shape, dt):
                return ph.enter_context(nc.sbuf_tensor(un(name), list(shape), dt))

            def psp(name, shape, dt=F32):
                return ph.enter_context(nc.psum_tensor(un(name), list(shape), dt))

            env = make_env(ph, f"sg_{l}")
            wu = sbp("swu_s", [128, NCH, D], BF16)
            wvv = sbp("swv_s", [128, NCH, D], BF16)
            wo = sbp("swo_s", [128, NCH, D], BF16)
            wsT = sbp("swsT_s", [128, 4, 128], BF16)
            R_wu, R_wvv, R_wo, R_wsT = Res(), Res(), Res(), Res()
            P.dma("sp", DMA(wu[:], swu_b.ap()[i].rearrange("(k p) n -> p k n", p=128)), r=[R_w[("swu", i)]], w=[R_wu])
            P.dma("sp", DMA(wvv[:], swv_b.ap()[i].rearrange("(k p) n -> p k n", p=128)), r=[R_w[("swv", i)]], w=[R_wvv])
            P.dma("sp", DMA(wo[:], swo_b.ap()[i].rearrange("(k p) n -> p k n", p=128)), r=[R_w[("swo", i)]], w=[R_wo])
            P.dma("sp", DMA(wsT[:], swsT_b.ap()[i]), r=[R_w[("swsT", i)]], w=[R_wsT])
            xr = Ring([sbp(f"sgx{j}", [128, NCH, 512], F32) for j in range(2)])
            hr = Ring([sbp(f"sgh{j}", [128, NCH, 512], BF16) for j in range(2)])
            uT = Ring([sbp(f"sgu{j}", [128, NCH, 512], BF16) for j in range(1)])
            suT = Ring([sbp(f"sgsu{j}", [128, NCH, 512], BF16) for j in range(1)])
            vg = Ring([sbp(f"sgv{j}", [128, D], F32) for j in range(2)])
            vn = Ring([sbp(f"sgvn{j}", [128, D], BF16) for j in range(2)])
            vjunk = Ring([sbp("sgjunk", [128, D], BF16)])
            vss = Ring([sbp(f"sgss{j}", [128, 2], F32) for j in range(2)])
            mt = Ring([sbp(f"sgm{j}", [128, 128], F32) for j in range(3)])
            pU = Ring([psp(f"pU{j}", [128, 512]) for j in range(2)])
            pVv = Ring([psp(f"pVv{j}", [128, D]) for j in range(1)])
            pM = Ring([psp(f"pM{j}", [128, 512]) for j in range(2)])
            pY = Ring([psp(f"pY{j}", [128, 512]) for j in range(1)])

            def sgu_block(N, x_fm, rx_dram, col0, stream):
                xt, rx = xr.next()
                P.dma("sp", DMA(xt[:, :, :N], x_fm[:, :, col0:col0 + N]), r=[rx_dram], w=[rx])
                ht, rh = hr.next()
                norm_mod(env, xt, rx, N, modv[:, l, 0, stream, :], modv[:, l, 1, stream, :], ht, rh)
                ut, rut = uT.next()
                for fc in range(NCH):
                    pu, rpu = pU.next()
                    for k in range(NCH):
                        P.op("pe", MM(pu[:, :N], wu[:, k, fc * 128:(fc + 1) * 128], ht[:, k, :N],
                                      start=(k == 0), stop=(k == NCH - 1)), r=[R_wu, rh], w=[rpu])
                    P.op("act", ACT(ut[:, fc, :N], pu[:, :N], AF.Gelu), r=[rpu], w=[rut])
                sut, rsut = suT.next()
                for tt_ in range(N // 128):
                    pv, rpv = pVv.next()
                    for half in range(2):
                        for k in range(NCH):
                            P.op("pe", MM(pv[:, half * 512:(half + 1) * 512], ht[:, k, tt_ * 128:(tt_ + 1) * 128],
                                          wvv[:, k, half * 512:(half + 1) * 512], start=(k == 0), stop=(k == NCH - 1)),
                                 r=[R_wvv, rh], w=[rpv])
                    vt, rvt = vg.next()
                    P.op("act", ACT(vt[:], pv[:], AF.Gelu), r=[rpv], w=[rvt])
                    sst, rss = vss.next()
                    jk, rjk = vjunk.next()
                    P.op("dve", lambda e, jk=jk, vt=vt, sst=sst: e.tensor_tensor(jk[:], vt[:], vt[:], ALU.mult), r=[rvt], w=[rjk])
                    P.op("dve", lambda e, jk=jk, sst=sst: e.reduce_sum(sst[:, 0:1], jk[:], mybir.AxisListType.X), r=[rjk], w=[rss])
                    P.op("act", ACT(sst[:, 1:2], sst[:, 0:1], AF.Sqrt, bias=epsT[:, 0:1], scale=1.0 / D), r=[rss, R_eps], w=[rss])
                    P.op("dve", RCP(sst[:, 1:2], sst[:, 1:2]), r=[rss], w=[rss])
                    vnt, rvn = vn.next()
                    P.op("dve", TS(vnt[:], vt[:], sst[:, 1:2], None, ALU.mult), r=[rvt, rss], w=[rvn])
                    pm_, rpm = pM.next()
                    for gi in range(4):
                        for cc_ in range(2):
                            fc = gi * 2 + cc_
                            sl = pm_[:, (fc % 4) * 128:(fc % 4 + 1) * 128]
                            P.op("pe", MM(sl, vnt[:, fc * 128:(fc + 1) * 128], wsT[:, gi, :]), r=[rvn, R_wsT], w=[rpm])
                            mtt, rmt = mt.next()
                            P.op("dve", STT(mtt[:], sl, svn[:, i, fc:fc + 1], sbsb[:, i, gi, :], ALU.mult, ALU.add),
                                 r=[rpm, R_cs[8], R_cs[9]], w=[rmt])
                            P.op("pool", TT(sut[:, fc, tt_ * 128:(tt_ + 1) * 128], mtt[:], ut[:, fc, tt_ * 128:(tt_ + 1) * 128], ALU.mult),
                                 r=[rmt, rut], w=[rsut])
                            if fc == 3:
                                pm_, rpm = pM.next()
                for fo in range(NCH):
                    pt_, rp = pY.next()
                    for k in range(NCH):
                        P.op("pe", MM(pt_[:, :N], wo[:, k, fo * 128:(fo + 1) * 128], sut[:, k, :N],
                                      start=(k == 0), stop=(k == NCH - 1)), r=[R_wo, rsut], w=[rp])
                    P.op("dve", STT(xt[:, fo, :N], pt_[:, :N], modv[:, l, 2, stream, fo:fo + 1], xt[:, fo, :N], ALU.mult, ALU.add),
                         r=[rp, rx, R_const], w=[rx])
                P.dma("sp", DMA(x_fm[:, :, col0:col0 + N], xt[:, :, :N]), r=[rx], w=[rx_dram])

            if with_ctx:
                sgu_block(CTX, cw_fm, R_cw, 0, 1)
            for b in range(NB):
                sgu_block(512, xw_fm, R_xw[b], 1 + b * 512, 0)
            P.barrier()

    def halo_exchange(l):
        xw_fm, R_xw, R_xh = xw_fms[cur["i"]], R_xws[cur["i"]], R_xhs[cur["i"]]
        R_hl, R_ha = Res(), Res()
        with ExitStack() as ph:
            hs = ph.enter_context(nc.sbuf_tensor(un("hs"), [8, D], F32))
            ho = ph.enter_context(nc.sbuf_tensor(un("ho"), [128, 2, NCH, 1], F32))
            hp = ph.enter_context(nc.psum_tensor(un("hp"), [128, 2, NCH], F32))
            hb = ph.enter_context(nc.sbuf_tensor(un("hb"), [128, 2, NCH, 1], F32))
            rhs_, rho, rhp, rhb = Res(), Res(), Res(), Res()
            P.dma("sp", DMA(hb[:, 0, :, :], xw_fm[:, :, 1:2], allow_slow_non_contiguous=True), r=[R_xw[0]], w=[rhb])
            P.dma("sp", DMA(hb[:, 1, :, :], xw_fm[:, :, TPC:TPC + 1], allow_slow_non_contiguous=True), r=[R_xw[NB - 1]], w=[rhb])
            for s_ in range(2):
                P.dma("sp", DMA(HLl.ap()[s_].rearrange("(p c) -> p c", p=128), hb[:, s_, :, 0], allow_slow_non_contiguous=True), r=[rhb], w=[R_hl])
            P.cc(lambda e: e.collective_compute("AllGather", ALU.bypass, replica_groups=[[0, 1, 2, 3], [4, 5, 6, 7]],
                                                ins=[HLl.ap().opt()], outs=[HLa.ap().opt()]), r=[R_hl], w=[R_ha])
            P.dma("sp", DMA(hs[:], HLa.ap()), r=[R_ha], w=[rhs_])
            for s_ in range(2):
                for c in range(NCH):
                    P.op("pe", MM(hp[:, s_, c:c + 1], hs[:, c:D:NCH], sel[:, s_:s_ + 1]), r=[rhs_, R_cs[12]], w=[rhp])
            P.op("dve", CP(ho[:, :, :, 0], hp[:]), r=[rhp], w=[rho])
            P.dma("sp", DMA(xw_fm[:, :, 0:1], ho[:, 0, :, :], allow_slow_non_contiguous=True), r=[rho], w=[R_xh])
            P.dma("sp", DMA(xw_fm[:, :, TPC + 1:TPC + 2], ho[:, 1, :, :], allow_slow_non_contiguous=True), r=[rho], w=[R_xh])
            P.barrier()

    def ffn_layer(l):
        with_ctx = l < 2
        ci = cur["i"]
        xin_fm, R_xin, R_xh = xw_fms[ci], R_xws[ci], R_xhs[ci]
        xout_fm, R_xout = xw_fms[1 - ci], R_xws[1 - ci]
        with ExitStack() as ph:
            def sbp(name, shape, dt):
                return ph.enter_context(nc.sbuf_tensor(un(name), list(shape), dt))

            def psp(name, shape, dt=F32):
                return ph.enter_context(nc.psum_tensor(un(name), list(shape), dt))

            env = make_env(ph, f"ff_{l}")
            xt = sbp("ffx", [128, NCH, 1024], F32)
            xh = sbp("ffxh", [128, NCH, 2], F32)
            ht = sbp("ffh", [128, NCH, 1024], BF16)
            hh2 = sbp("ffhh", [128, NCH, 2], BF16)
            hzt = sbp("ffhz", [128, 88], F32)
            rx, rh, rxh, rhh, rhz = Res(), Res(), Res(), Res(), Res()
            wupr = Ring([sbp(f"wup{j}", [128, NCH, 512], BF16) for j in range(2)])
            wdnr = Ring([sbp(f"wdn{j}", [128, NPAIR, 128], BF16) for j in range(2)])
            aT = sbp("ffa", [128, NPAIR, 1024], BF16)
            ra = Res()
            zr = Ring([sbp(f"ffz{j}", [128, 1026], F32) for j in range(3)])
            tr_ = Ring([sbp(f"fft{j}", [128, 1024], F32) for j in range(3)])
            sg = Ring([sbp(f"ffs{j}", [128, 1024], F32) for j in range(2)])
            pH = Ring([psp("pH0", [128, 512])])
            pZ = Ring([psp(f"pZ{j}", [128, 512]) for j in range(5)])
            pYr = Ring([psp(f"pYf{j}", [128, 512]) for j in range(1)])

            def ffn_block(N, src_fm, r_src, dst_fm, r_dst, colL, colM, stream, zero_halo, first_blk=False, last_blk=False):
                nh = (N + 511) // 512
                hw = [min(512, N - hh * 512) for hh in range(nh)]
                P.dma("sp", DMA(xt[:, :, 0:N], src_fm[:, :, colM:colM + N]), r=r_src, w=[rx])
                A2, B2 = modv[:, l, 3, stream, :], modv[:, l, 4, stream, :]
                for hh in range(nh):
                    c0 = hh * 512
                    norm_mod(env, xt[:, :, c0:c0 + hw[hh]], rx, hw[hh], A2, B2, ht[:, :, c0:c0 + hw[hh]], rh)
                if zero_halo:
                    P.op("pool", MS(hh2[:], 0.0), w=[rhh])
                else:
                    P.dma("sp", DMA(xh[:, :, 0:1], src_fm[:, :, colL:colL + 1], allow_slow_non_contiguous=True), r=r_src + [R_xh], w=[rxh])
                    P.dma("sp", DMA(xh[:, :, 1:2], src_fm[:, :, colM + N:colM + N + 1], allow_slow_non_contiguous=True), r=r_src + [R_xh], w=[rxh])
                    norm_mod(env, xh, rxh, 2, A2, B2, hh2, rhh)
                    for c in range(NCH):
                        if first_blk:
                            P.op("dve", TS(hh2[:, c, 0:1], hh2[:, c, 0:1], flags[:, 0:1], None, ALU.mult), r=[rhh, R_cs[11]], w=[rhh])
                        if last_blk:
                            P.op("dve", TS(hh2[:, c, 1:2], hh2[:, c, 1:2], flags[:, 1:2], None, ALU.mult), r=[rhh, R_cs[11]], w=[rhh])
                wts = {}

                def get_w(j):
                    jj = j // 2
                    if jj not in wts:
                        wt, rw = wupr.next()
                        P.dma("sp", DMA(wt[:], wup_b.ap()[l].rearrange("(k p) n -> p k n", p=128)[:, :, jj * 512:(jj + 1) * 512]),
                              r=[R_w[("wup", l)]], w=[rw])
                        wts[jj] = (wt, rw)
                    wt, rw = wts[jj]
                    return wt, rw, (j % 2) * 256

                ph_, rph = pH.next()
                for j in range(NPAIR):
                    wt, rw, wc = get_w(j)
                    for s_ in range(2):
                        ti = j * 2 + s_
                        for k in range(NCH):
                            P.op("pe", MM(ph_[:, ti * 2:ti * 2 + 2], wt[:, k, wc + s_ * 128:wc + (s_ + 1) * 128], hh2[:, k, :],
                                          start=(k == 0), stop=(k == NCH - 1)), r=[rw, rhh], w=[rph])
                wts.clear()
                P.op("act", ACT(hzt[:, 0:88], ph_[:, 0:88], AF.Identity), r=[rph], w=[rhz])
                for j in range(NPAIR):
                    wt, rw, wc = get_w(j)
                    outs = []
                    for s_ in range(2):
                        zt, rz = zr.next()
                        for hh in range(nh):
                            pz, rpz = pZ.next()
                            c0 = hh * 512
                            for k in range(NCH):
                                P.op("pe", MM(pz[:, :hw[hh]], wt[:, k, wc + s_ * 128:wc + (s_ + 1) * 128], ht[:, k, c0:c0 + hw[hh]],
                                              start=(k == 0), stop=(k == NCH - 1)), r=[rw, rh], w=[rpz])
                            P.op("act", ACT(zt[:, 1 + c0:1 + c0 + hw[hh]], pz[:, :hw[hh]], AF.Identity), r=[rpz], w=[rz])
                        ti = j * 2 + s_
                        P.op("pool", CP(zt[:, 0:1], hzt[:, ti * 2:ti * 2 + 1]), r=[rhz], w=[rz])
                        P.op("pool", CP(zt[:, N + 1:N + 2], hzt[:, ti * 2 + 1:ti * 2 + 2]), r=[rhz], w=[rz])
                        tt, rt = tr_.next()
                        P.op("dve", TS(tt[:, :N], zt[:, 1:N + 1], convp[:, l, ti, 1:2], convp[:, l, ti, 3:4], ALU.mult, ALU.add),
                             r=[rz, R_cs[10]], w=[rt])
                        P.op("dve", STT(tt[:, :N], zt[:, 0:N], convp[:, l, ti, 0:1], tt[:, :N], ALU.mult, ALU.add),
                             r=[rz, rt, R_cs[10]], w=[rt])
                        P.op("dve", STT(tt[:, :N], zt[:, 2:N + 2], convp[:, l, ti, 2:3], tt[:, :N], ALU.mult, ALU.add),
                             r=[rz, rt, R_cs[10]], w=[rt])
                        outs.append((tt, rt))
                    st, rs = sg.next()
                    P.op("act", ACT(st[:, :N], outs[0][0][:, :N], AF.Silu), r=[outs[0][1]], w=[rs])
                    P.op("pool", TT(aT[:, j, :N], st[:, :N], outs[1][0][:, :N], ALU.mult), r=[rs, outs[1][1]], w=[ra])
                for fo in range(NCH):
                    wd, rwd = wdnr.next()
                    P.dma("sp", DMA(wd[:], wdn_b.ap()[l].rearrange("(j p) n -> p j n", p=128)[:, :, fo * 128:(fo + 1) * 128]),
                          r=[R_w[("wdn", l)]], w=[rwd])
                    for hh in range(nh):
                        c0 = hh * 512
                        py_, rpy = pYr.next()
                        for j in range(NPAIR):
                            P.op("pe", MM(py_[:, :hw[hh]], wd[:, j, :], aT[:, j, c0:c0 + hw[hh]],
                                          start=(j == 0), stop=(j == NPAIR - 1)), r=[rwd, ra], w=[rpy])
                        P.op("dve", STT(xt[:, fo, c0:c0 + hw[hh]], py_[:, :hw[hh]], modv[:, l, 5, stream, fo:fo + 1],
                                        xt[:, fo, c0:c0 + hw[hh]], ALU.mult, ALU.add), r=[rpy, rx, R_const], w=[rx])
                P.dma("sp", DMA(dst_fm[:, :, colM:colM + N], xt[:, :, 0:N]), r=[rx], w=r_dst)

            if with_ctx:
                ffn_block(CTX, cw_fm, [R_cw], cw_fm, [R_cw], 0, 0, 1, True)
            NBLK = max(1, TPC // 1024)
            BN = TPC // NBLK
            for b in range(NBLK):
                ks = [k for k in range(NB) if not ((k + 1) * 512 <= b * BN - 1 or k * 512 >= (b + 1) * BN + 1)]
                km = [k for k in range(NB) if b * BN <= k * 512 < (b + 1) * BN]
                ffn_block(BN, xin_fm, [R_xin[k] for k in ks], xout_fm, [R_xout[k] for k in km], b * BN, 1 + b * BN, 0, False,
                          first_blk=(b == 0), last_blk=(b == NBLK - 1))
            P.barrier()
        cur["i"] = 1 - ci

    def final_norm():
        with ExitStack() as ph:
            def sbp(name, shape, dt):
                return ph.enter_context(nc.sbuf_tensor(un(name), list(shape), dt))
            env = make_env(ph, "fin")
            xr = Ring([sbp(f"fnx{j}", [128, NCH, 512], F32) for j in range(2)])
            yr = Ring([sbp(f"fny{j}", [128, NCH, 512], F32) for j in range(2)])
            y_fm = yT_out.ap().rearrange("(c p) t -> p c t", p=128)
            xw_fm, R_xw = xw_fms[cur["i"]], R_xws[cur["i"]]
            for b in range(NB):
                xt, rx = xr.next()
                P.dma("sp", DMA(xt[:], xw_fm[:, :, 1 + b * 512:1 + (b + 1) * 512]), r=[R_xw[b]], w=[rx])
                yt, ry = yr.next()
                norm_mod(env, xt, rx, 512, finn, None, yt, ry)
                P.dma("sp", DMA(y_fm[:, :, b * 512:(b + 1) * 512], yt[:]), r=[ry], w=[Res()])

    import os
    dbg = os.environ.get("DBG_XMID")
    for l in range(NL):
        if l % 2 == 0:
            attention_layer(l)
        else:
            sgu_layer(l)
        if dbg:
            break
        halo_exchange(l)
        ffn_layer(l)
    if dbg:
        for b in range(NB):
            P.dma("sp", DMA(yT_out.ap()[:, b * 512:(b + 1) * 512], xw0.ap()[:, 1 + b * 512:1 + (b + 1) * 512]),
                  r=[R_xws[0][b]], w=[Res()])
    else:
        final_norm()
    P.finish()
    P.emit()
    es.close()
    return nc


def _fm(v):
    v = np.asarray(v, np.float32)
    return np.ascontiguousarray(v.reshape(-1, 128).T)


def prep_inputs(inp, TPC):
    f32 = lambda a: np.ascontiguousarray(np.asarray(a, dtype=np.float32))
    x = f32(inp["x"]); ctx = f32(inp["ctx"]); c = f32(inp["c"]); c_ctx = f32(inp["c_ctx"])
    S = x.shape[1]
    assert S == 4 * TPC
    shared = {}
    shared["ada_w"] = f32(inp["ada_w"])
    shared["ada_b"] = np.ascontiguousarray(np.stack([_fm(inp["ada_b"][l]) for l in range(DEPTH)], 1))
    shared["mixn"] = np.ascontiguousarray(np.stack([_fm(inp["mix_norm"][l]) for l in range(DEPTH)], 1))
    shared["ffnn"] = np.ascontiguousarray(np.stack([_fm(inp["ffn_norm"][l]) for l in range(DEPTH)], 1))
    shared["finn"] = _fm(inp["final_norm"])
    w_in = f32(inp["attn_w_in"])
    de = np.concatenate([np.arange(0, 64, 2), np.arange(1, 64, 2)])
    sw = np.concatenate([np.arange(1, 64, 2), np.arange(0, 64, 2)])
    off = {"q1": 0, "q2": 256, "k1": 512, "k2": 768, "va": 1024, "qb": 1536, "kb": 2048, "vb": 2176}

    def blk(name, h, perm):
        return off[name] + h * 64 + perm

    def tiles(perm):
        cols = []
        for h in range(4):
            cols += [blk("q1", h, perm), blk("q2", h, perm)]
        for h in range(4):
            cols += [blk("k1", h, perm), blk("k2", h, perm)]
        for j in range(4):
            cols += [blk("qb", j, perm), blk("qb", 4 + j, perm)]
        cols += [blk("kb", 0, perm), blk("kb", 1, perm)]
        return np.concatenate(cols)

    cn, cs = tiles(de), tiles(sw)
    shared["wqk"] = np.ascontiguousarray(w_in[:, :, cn])
    shared["wqs"] = np.ascontiguousarray(w_in[:, :, cs])
    vcols = np.concatenate([np.arange(1024, 1536), np.arange(2176, 2304)])
    shared["wv"] = np.ascontiguousarray(w_in[:, :, vcols])
    qn = f32(inp["gqa_q_norm"]); kn = f32(inp["gqa_k_norm"])
    gg = np.zeros((128, 2, 5, 2), np.float32)
    for i in range(2):
        for t in range(5):
            g = qn[i] if t < 4 else kn[i]
            gg[:, i, t, 0] = np.concatenate([g[de], g[de]])
            gg[:, i, t, 1] = np.concatenate([g[sw], g[sw]])
    shared["ggain"] = gg
    w_out = f32(inp["attn_w_out"])
    shared["wod"] = np.ascontiguousarray(w_out[:, 0:512, :])
    shared["wog"] = np.ascontiguousarray(w_out[:, 512:1024, :].reshape(2, 8, 64, D).transpose(0, 2, 1, 3))
    lq = np.stack([f32(inp["diff_lq1"]), f32(inp["diff_lk1"]), f32(inp["diff_lq2"]), f32(inp["diff_lk2"])], 1)
    shared["lqk"] = np.ascontiguousarray(lq[None])
    shared["subln"] = np.ascontiguousarray(f32(inp["diff_subln"]).T)
    swin = f32(inp["sgu_w_in"])
    shared["swu"] = np.ascontiguousarray(swin[:, :, 0:D])
    shared["swv"] = np.ascontiguousarray(swin[:, :, D:2 * D])
    shared["svn"] = np.ascontiguousarray(np.stack([_fm(inp["sgu_v_norm"][i]) for i in range(2)], 1))
    shared["swsT"] = np.ascontiguousarray(f32(inp["sgu_w_s"]).transpose(0, 3, 1, 2))
    shared["sbsb"] = np.ascontiguousarray(np.broadcast_to(f32(inp["sgu_b_s"])[None], (128, 2, 4, 128)))
    shared["swo"] = f32(inp["sgu_w_out"])
    wup = f32(inp["ffn_w_up"])
    shared["wup"] = np.ascontiguousarray(
        np.stack([wup[:, :, 0:FF].reshape(DEPTH, D, NPAIR, 128), wup[:, :, FF:2 * FF].reshape(DEPTH, D, NPAIR, 128)], 3)
        .reshape(DEPTH, D, NPAIR * 256))
    cwt = f32(inp["ffn_conv_w"]); cb = f32(inp["ffn_conv_b"])
    cp = np.zeros((128, DEPTH, 44, 4), np.float32)
    for l in range(DEPTH):
        for j in range(NPAIR):
            for s_ in range(2):
                f0 = s_ * FF + j * 128
                for k in range(3):
                    cp[:, l, j * 2 + s_, k] = cwt[l, k, f0:f0 + 128]
                cp[:, l, j * 2 + s_, 3] = cb[l, f0:f0 + 128]
    shared["convp"] = cp
    shared["wdn"] = f32(inp["ffn_w_down"])
    inv = (10000.0 ** (-np.arange(16, dtype=np.float32) / 16)).astype(np.float32)
    maps = []
    for core in range(8):
        b, r = core // 4, core % 4
        m = dict(shared)
        t0 = r * TPC
        m["xT"] = np.ascontiguousarray(x[b, t0:t0 + TPC, :].T)
        m["cT"] = np.ascontiguousarray(ctx[b].T)
        cv = np.zeros((128, NCH, 2), np.float32)
        cv[:, :, 0] = _fm(c[b])
        cv[:, :, 1] = _fm(c_ctx)
        m["cvec"] = cv
        t = np.arange(t0, t0 + TPC)
        row = (t // 64).astype(np.float32)
        col = (t % 64).astype(np.float32)
        ang = np.concatenate([row[None, :] * inv[:, None], col[None, :] * inv[:, None]], 0).astype(np.float32)
        cs_, sn_ = np.cos(ang).astype(np.float32), np.sin(ang).astype(np.float32)
        C64 = np.concatenate([cs_, cs_], 0)
        S64 = np.concatenate([-sn_, sn_], 0)
        m["ropeC"] = np.ascontiguousarray(np.concatenate([C64, C64], 0))
        m["ropeS"] = np.ascontiguousarray(np.concatenate([S64, S64], 0))
        fl = np.zeros((128, 2), np.float32)
        fl[:, 0] = 1.0 if r > 0 else 0.0
        fl[:, 1] = 1.0 if r < 3 else 0.0
        m["flags"] = fl
        sl = np.zeros((8, 2), np.float32)
        if r > 0:
            sl[2 * (r - 1) + 1, 0] = 1.0
        if r < 3:
            sl[2 * (r + 1), 1] = 1.0
        m["sel"] = sl
        maps.append(m)
    return maps


_NC_CACHE = {}


def run(inp, TPC, NL=DEPTH):
    key = (TPC, NL)
    if key not in _NC_CACHE:
        _NC_CACHE[key] = build(TPC, NL)
    nc = _NC_CACHE[key]
    maps = prep_inputs(inp, TPC)
    res = run_bass_kernel_spmd(nc, maps, core_ids=list(range(8)))
    B = 2
    out = np.zeros((B, 4 * TPC, D), np.float32)
    for core in range(8):
        b, r = core // 4, core % 4
        out[b, r * TPC:(r + 1) * TPC, :] = np.asarray(res.results[core]["yT"], np.float32).T
    return out


def kernel(**inputs):
    TPC = np.asarray(inputs["x"]).shape[1] // 4
    return run(inputs, TPC, DEPTH)
```

```python
import math
from contextlib import ExitStack

import numpy as np
import concourse.bass as bass
import concourse.mybir as mybir
from concourse.bass_utils import run_bass_kernel_spmd

F32 = mybir.dt.float32
BF16 = mybir.dt.bfloat16
AF = mybir.ActivationFunctionType
ALU = mybir.AluOpType

D = 1024
NCH = 8
CTX = 256
FF = 2816
NPAIR = 22
EPS = 1e-6
DEPTH = 4
ENGS = ("pe", "act", "dve", "pool", "sp")


class Res:
    __slots__ = ("w", "rc", "rd")

    def __init__(self):
        self.w = None
        self.rc = {}
        self.rd = []


class Op:
    __slots__ = ("eng", "kind", "fn", "deps", "idx", "needed", "count", "sem", "target")


class Prog:
    NS = 8

    def __init__(self, nc, es):
        self.nc = nc
        self.ops = {e: [] for e in ENGS}
        self.bar = {e: [] for e in ENGS}
        self.dma_n = {q: 0 for q in ("sp", "act", "pool")}
        self.dsem = {q: [es.enter_context(nc.semaphore(f"d_{q}_{i}")) for i in range(self.NS)]
                     for q in ("sp", "act", "pool")}
        self.csem = {e: es.enter_context(nc.semaphore(f"c_{e}")) for e in ("pe", "act", "dve", "pool")}
        self.es = es
        self.live_dma = []
        self.ncc = 0

    def _add(self, eng, kind, fn, r, w):
        o = Op()
        o.eng, o.kind, o.fn, o.needed, o.count, o.sem, o.target = eng, kind, fn, False, 0, None, 0
        o.idx = len(self.ops[eng])
        deps = {}

        def dep(x, why):
            if x is None or x is o:
                return
            k = id(x)
            if k in deps:
                if why == "raw":
                    deps[k] = (x, why)
                return
            deps[k] = (x, why)

        for x in r:
            dep(x.w, "raw")
        for x in w:
            dep(x.w, "waw")
            for rd in x.rc.values():
                dep(rd, "war")
            for rd in x.rd:
                dep(rd, "war")
        for x in self.bar[eng]:
            dep(x, "raw")
        self.bar[eng] = []
        o.deps = list(deps.values())
        for (x, why) in o.deps:
            if x.kind == "c":
                if x.eng != eng or kind != "c":
                    x.needed = True
                elif eng != "pe" and why == "raw" and (o.idx - x.idx) <= 2:
                    x.needed = True
        self.ops[eng].append(o)
        for x in r:
            if kind == "c":
                x.rc[eng] = o
            else:
                x.rd.append(o)
        for x in w:
            x.w = o
            x.rc = {}
            x.rd = []
        return o

    def op(self, eng, fn, r=(), w=()):
        return self._add(eng, "c", fn, r, w)

    def dma(self, q, fn, r=(), w=()):
        o = self._add(q, "d", fn, r, w)
        n = self.dma_n[q]
        self.dma_n[q] = n + 1
        o.sem = self.dsem[q][n % self.NS]
        o.target = 16 * (n // self.NS + 1)
        self.live_dma.append(o)
        return o

    def cc(self, fn, r=(), w=()):
        import os
        if os.environ.get("NO_CC"):
            return self._add("pool", "c", lambda e: e.engine_nop(), r, w)
        if not hasattr(self, "cc_chain"):
            self.cc_chain = Res()
        o = self._add("pool", "cc", fn, r, list(w) + [self.cc_chain])
        o.sem = self.es.enter_context(self.nc.semaphore(f"cc_{self.ncc}"))
        self.ncc += 1
        o.target = 1
        self.live_dma.append(o)
        return o

    def barrier(self):
        last = {}
        for e in ("pe", "act", "dve", "pool"):
            for o in reversed(self.ops[e]):
                if o.kind == "c":
                    last[e] = o
                    break
        for e in ENGS:
            self.bar[e] = [o for (k, o) in last.items() if k != e] + list(self.live_dma)
        self.live_dma = []

    def finish(self):
        self.barrier()
        self._add("sp", "nop", None, (), ())

    def emit(self):
        nc = self.nc
        for e in ("pe", "act", "dve", "pool"):
            c = 0
            for o in self.ops[e]:
                if o.kind == "c" and o.needed:
                    c += 1
                o.count = c if o.kind == "c" else 0
        prog = self

        def run(E, eng):
            seen = {}

            def wait(sem, val):
                k = id(sem)
                if seen.get(k, 0) >= val:
                    return
                eng.wait_ge(sem, val)
                seen[k] = val

            for o in prog.ops[E]:
                if o.kind == "d" and o.target > 16:
                    wait(o.sem, o.target - 16)
                for (x, why) in o.deps:
                    if x.kind == "c":
                        if x.eng == E and o.kind == "c":
                            if E == "pe" or why != "raw" or (o.idx - x.idx) > 2:
                                continue
                        wait(prog.csem[x.eng], x.count)
                    elif x.kind in ("d", "cc"):
                        wait(x.sem, x.target)
                if o.fn is None:
                    continue
                ins = o.fn(eng)
                if o.kind == "c":
                    if o.needed:
                        ins.then_inc(prog.csem[E], 1)
                elif o.kind == "d":
                    ins.then_inc(o.sem, 16)
                elif o.kind == "cc":
                    ins.then_inc(o.sem)

        with nc.Block() as block:
            @block.tensor
            def _(e):
                run("pe", e)

            @block.scalar
            def _(e):
                run("act", e)

            @block.vector
            def _(e):
                run("dve", e)

            @block.gpsimd
            def _(e):
                run("pool", e)

            @block.sync
            def _(e):
                run("sp", e)


def MM(out, lhsT, rhs, start=True, stop=True):
    return lambda e: e.matmul(out, lhsT, rhs, start=start, stop=stop)


def ACT(out, in_, func, bias=0.0, scale=1.0):
    return lambda e: e.activation(out, in_, func, bias=bias, scale=scale)


def TT(out, a, b, op):
    return lambda e: e.tensor_tensor(out, a, b, op)


def TS(out, a, s1, s2, op0, op1=None):
    if op1 is None:
        return lambda e: e.tensor_scalar(out, a, s1, None, op0)
    return lambda e: e.tensor_scalar(out, a, s1, s2, op0, op1)


def STT(out, a, s, b, op0, op1):
    return lambda e: e.scalar_tensor_tensor(out, a, s, b, op0, op1)


def CP(out, a):
    return lambda e: e.tensor_copy(out, a)


def RCP(out, a):
    return lambda e: e.reciprocal(out, a)


def MS(ap, v):
    return lambda e: e.memset(ap, v)


def DMA(out, in_, **kw):
    return lambda e: e.dma_start(out=out, in_=in_, **kw)


class Ring:
    def __init__(self, aps):
        self.aps = aps
        self.res = [Res() for _ in aps]
        self.i = 0

    def next(self):
        k = self.i % len(self.aps)
        self.i += 1
        return self.aps[k], self.res[k]


def build(TPC, NL=DEPTH):
    S = 4 * TPC
    NB = TPC // 512
    NKT = (CTX + S) // 128
    NLT = TPC // 128
    nc = bass.Bass("TRN2", target_bir_lowering=False)
    es = ExitStack()

    def din(name, shape, dt=F32):
        return nc.dram_tensor(name, list(shape), dt, kind="ExternalInput")

    def dscr(name, shape, dt):
        return nc.dram_tensor(name, list(shape), dt)

    xT_in = din("xT", [D, TPC])
    cT_in = din("cT", [D, CTX])
    cvec_in = din("cvec", [128, NCH, 2])
    adaw_in = din("ada_w", [DEPTH, D, 6 * D])
    adab_in = din("ada_b", [128, DEPTH, 48])
    mixn_in = din("mixn", [128, DEPTH, NCH])
    ffnn_in = din("ffnn", [128, DEPTH, NCH])
    finn_in = din("finn", [128, NCH])
    wqk_in = din("wqk", [2, D, 1664])
    wqs_in = din("wqs", [2, D, 1664])
    wv_in = din("wv", [2, D, 640])
    ggain_in = din("ggain", [128, 2, 5, 2])
    wod_in = din("wod", [2, 512, D])
    wog_in = din("wog", [2, 64, 8, D])
    lqk_in = din("lqk", [1, 2, 4, 64])
    subln_in = din("subln", [128, 2])
    ropeC_in = din("ropeC", [128, TPC])
    ropeS_in = din("ropeS", [128, TPC])
    swu_in = din("swu", [2, D, D])
    swv_in = din("swv", [2, D, D])
    svn_in = din("svn", [128, 2, NCH])
    swsT_in = din("swsT", [2, 128, 4, 128])
    sbsb_in = din("sbsb", [128, 2, 4, 128])
    swo_in = din("swo", [2, D, D])
    wup_in = din("wup", [DEPTH, D, NPAIR * 256])
    convp_in = din("convp", [128, DEPTH, 44, 4])
    wdn_in = din("wdn", [DEPTH, FF, D])
    flags_in = din("flags", [128, 2])
    sel_in = din("sel", [8, 2])
    yT_out = nc.dram_tensor("yT", [D, TPC], F32, kind="ExternalOutput")

    xw0 = dscr("xw0", [D, TPC + 2], F32)
    xw1 = dscr("xw1", [D, TPC + 2], F32)
    cw = dscr("cw", [D, CTX], F32)
    wqk_b = dscr("wqk_b", [2, D, 1664], BF16)
    wqs_b = dscr("wqs_b", [2, D, 1664], BF16)
    wv_b = dscr("wv_b", [2, D, 640], BF16)
    wod_b = dscr("wod_b", [2, 512, D], BF16)
    wog_b = dscr("wog_b", [2, 64, 8, D], BF16)
    swu_b = dscr("swu_b", [2, D, D], BF16)
    swv_b = dscr("swv_b", [2, D, D], BF16)
    swsT_b = dscr("swsT_b", [2, 128, 4, 128], BF16)
    swo_b = dscr("swo_b", [2, D, D], BF16)
    wup_b = dscr("wup_b", [DEPTH, D, NPAIR * 256], BF16)
    wdn_b = dscr("wdn_b", [DEPTH, FF, D], BF16)
    Qs = dscr("Qs", [8, 128, TPC], BF16)
    Qc = dscr("Qc", [8, 128, CTX], BF16)
    Klp = [dscr(f"Kl{p}", [64, TPC], BF16) for p in range(10)]
    Vlp = [dscr(f"Vl{p}", [64, TPC], BF16) for p in range(10)]
    Kap = [dscr(f"Ka{p}", [4 * 64, TPC], BF16) for p in range(10)]
    Vap = [dscr(f"Va{p}", [4 * 64, TPC], BF16) for p in range(10)]
    Kc = dscr("Kc", [640, CTX], BF16)
    Vc = dscr("Vc", [640, CTX], BF16)
    AT = dscr("AT", [12, 128, TPC], BF16)
    ATc = dscr("ATc", [12, 128, CTX], BF16)
    HLl = dscr("HLl", [2, D], F32)
    HLa = dscr("HLa", [8, D], F32)

    P = Prog(nc, es)

    R_xws = [[Res() for _ in range(NB)] for _ in range(2)]
    R_xhs = [Res(), Res()]
    cur = {"i": 0}
    R_cw = Res()
    R_w = {}

    uid = [0]

    def un(name):
        uid[0] += 1
        return f"{name}_u{uid[0]}"

    def sb(name, shape, dt):
        return es.enter_context(nc.sbuf_tensor(un(name), list(shape), dt))

    ones_bf = sb("ones_bf", [128, 128], BF16)
    bd_bf = sb("bd_bf", [128, 128], BF16)
    ones_f = sb("ones_f", [128, 128], F32)
    cvec = sb("cvec_s", [128, NCH, 2], F32)
    modraw = sb("modraw", [128, DEPTH, 48, 2], F32)
    adab = sb("adab_s", [128, DEPTH, 48], F32)
    mixn = sb("mixn_s", [128, DEPTH, NCH], F32)
    ffnn = sb("ffnn_s", [128, DEPTH, NCH], F32)
    finn = sb("finn_s", [128, NCH], F32)
    modv = sb("modv", [128, DEPTH, 6, 2, NCH], F32)
    ggain = sb("ggain_s", [128, 2, 5, 2], F32)
    subln = sb("subln_s", [128, 2], F32)
    lqk = sb("lqk_s", [1, 2, 4, 64], F32)
    lamt = sb("lamt", [1, 2, 8], F32)
    svn = sb("svn_s", [128, 2, NCH], F32)
    sbsb = sb("sbsb_s", [128, 2, 4, 128], F32)
    convp = sb("convp_s", [128, DEPTH, 44, 4], F32)
    flags = sb("flags_s", [128, 2], F32)
    sel = sb("sel_s", [8, 2], F32)
    R_const = Res()

    consts = [(cvec, cvec_in), (adab, adab_in), (mixn, mixn_in), (ffnn, ffnn_in), (finn, finn_in),
              (ggain, ggain_in), (subln, subln_in), (lqk, lqk_in), (svn, svn_in), (sbsb, sbsb_in),
              (convp, convp_in), (flags, flags_in), (sel, sel_in)]
    R_cs = []
    for (t, src) in consts:
        r_ = Res()
        R_cs.append(r_)
        P.dma("sp", DMA(t[:], src.ap()), w=[r_])
    R_ones = Res()
    P.op("pool", MS(ones_bf[:], 1.0), w=[R_ones])
    P.op("pool", MS(ones_f[:], 1.0), w=[R_ones])
    P.op("pool", MS(bd_bf[:], 0.0), w=[R_ones])
    P.op("pool", MS(bd_bf[0:64, 0:64], 1.0), w=[R_ones])
    P.op("pool", MS(bd_bf[64:128, 64:128], 1.0), w=[R_ones])

    def conv_w(key, src_ap, dst_ap, n_el):
        r_ = Res()
        R_w[key] = r_
        rows = n_el // 1024
        s2 = src_ap
        d2 = dst_ap
        step = 8192
        for r0 in range(0, rows, step):
            r1 = min(rows, r0 + step)
            P.dma("pool", DMA(d2[r0:r1, :], s2[r0:r1, :]), w=[r_])

    def flat2(h, idx, n_el):
        ap = h.ap()[idx]
        names = " ".join(f"d{i}" for i in range(len(ap.shape)))
        ap = ap.rearrange(f"{names} -> ({names})")
        return ap.rearrange("(r c) -> r c", c=1024)

    def conv_mixer_weights(l):
        if l >= NL:
            return
        i = l // 2
        if l % 2 == 0:
            for (nm, src, dst, n) in (("wqk", wqk_in, wqk_b, D * 1664), ("wqs", wqs_in, wqs_b, D * 1664),
                                      ("wv", wv_in, wv_b, D * 640), ("wod", wod_in, wod_b, 512 * D),
                                      ("wog", wog_in, wog_b, 64 * 8 * D)):
                conv_w((nm, i), flat2(src, i, n), flat2(dst, i, n), n)
        else:
            for (nm, src, dst, n) in (("swu", swu_in, swu_b, D * D), ("swv", swv_in, swv_b, D * D),
                                      ("swsT", swsT_in, swsT_b, 128 * 4 * 128), ("swo", swo_in, swo_b, D * D)):
                conv_w((nm, i), flat2(src, i, n), flat2(dst, i, n), n)

    def conv_ffn_weights(l):
        if l >= NL:
            return
        conv_w(("wup", l), flat2(wup_in, l, D * NPAIR * 256), flat2(wup_b, l, D * NPAIR * 256), D * NPAIR * 256)
        conv_w(("wdn", l), flat2(wdn_in, l, FF * D), flat2(wdn_b, l, FF * D), FF * D)

    conv_mixer_weights(0)

    xw_fms = [xw0.ap().rearrange("(c p) t -> p c t", p=128), xw1.ap().rearrange("(c p) t -> p c t", p=128)]
    cw_fm = cw.ap().rearrange("(c p) t -> p c t", p=128)
    for b in range(NB):
        P.dma("sp", DMA(xw0.ap()[:, 1 + b * 512:1 + (b + 1) * 512], xT_in.ap()[:, b * 512:(b + 1) * 512]),
              w=[R_xws[0][b]])
    P.dma("sp", DMA(cw.ap(), cT_in.ap()), w=[R_cw])

    with ExitStack() as ph:
        def sbp(name, shape, dt):
            return ph.enter_context(nc.sbuf_tensor(un(name), list(shape), dt))

        def psp(name, shape, dt=F32):
            return ph.enter_context(nc.psum_tensor(un(name), list(shape), dt))

        scv = sbp("scv", [128, NCH, 2], F32)
        R_scv = Res()
        P.op("act", ACT(scv[:], cvec[:], AF.Silu), r=[R_cs[0]], w=[R_scv])
        wst = Ring([sbp(f"adaw{i}", [128, NCH, 768], F32) for i in range(2)])
        pm = Ring([psp(f"pm{i}", [128, 512]) for i in range(2)])
        for l in range(NL):
            for cg in range(8):
                wt, wr = wst.next()
                P.dma("sp", DMA(wt[:], adaw_in.ap()[l].rearrange("(k p) n -> p k n", p=128)[:, :, cg * 768:(cg + 1) * 768]),
                      w=[wr])
                pt, pr = pm.next()
                for j in range(6):
                    for k in range(NCH):
                        P.op("pe", MM(pt[:, 2 * j:2 * j + 2], wt[:, k, j * 128:(j + 1) * 128], scv[:, k, :],
                                      start=(k == 0), stop=(k == NCH - 1)), r=[wr, R_scv], w=[pr])
                for s_ in range(2):
                    P.op("dve", TT(modraw[:, l, cg * 6:(cg + 1) * 6, s_], pt[:, s_:12:2],
                                   adab[:, l, cg * 6:(cg + 1) * 6], ALU.add), r=[pr, R_cs[1]], w=[R_const])
        for l in range(NL):
            for s_ in range(2):
                mr = lambda m: modraw[:, l, m * 8:(m + 1) * 8, s_]
                P.op("dve", STT(modv[:, l, 0, s_, :], mr(1), 1.0, mixn[:, l, :], ALU.add, ALU.mult),
                     r=[R_const, R_cs[2]], w=[R_const])
                P.op("dve", CP(modv[:, l, 1, s_, :], mr(0)), r=[R_const], w=[R_const])
                P.op("dve", CP(modv[:, l, 2, s_, :], mr(2)), r=[R_const], w=[R_const])
                P.op("dve", STT(modv[:, l, 3, s_, :], mr(4), 1.0, ffnn[:, l, :], ALU.add, ALU.mult),
                     r=[R_const, R_cs[3]], w=[R_const])
                P.op("dve", CP(modv[:, l, 4, s_, :], mr(3)), r=[R_const], w=[R_const])
                P.op("dve", CP(modv[:, l, 5, s_, :], mr(5)), r=[R_const], w=[R_const])
        P.barrier()

    def norm_mod(env, xt, rx, N, A, Bv, h, rh, flag=None):
        pt, pr = env["ps"].next()
        for c in range(NCH):
            sq, rs = env["sq"].next()
            P.op("act", ACT(sq[:, :N], xt[:, c, :N], AF.Square), r=[rx], w=[rs])
            P.op("pe", MM(pt[:, :N], ones_bf[:], sq[:, :N], start=(c == 0), stop=(c == NCH - 1)),
                 r=[rs, R_ones], w=[pr])
        rt, rr = env["rstd"].next()
        P.op("act", ACT(rt[:, :N], pt[:, :N], AF.Sqrt, bias=env["eps"][:, 0:1], scale=1.0 / D), r=[pr, env["reps"]], w=[rr])
        P.op("dve", RCP(rt[:, :N], rt[:, :N]), r=[rr], w=[rr])
        for c in range(NCH):
            tt, tr = env["tmp"].next()
            P.op("dve", TT(tt[:, :N], xt[:, c, :N], rt[:, :N], ALU.mult), r=[rx, rr], w=[tr])
            if Bv is not None:
                P.op("act", ACT(h[:, c, :N], tt[:, :N], AF.Identity, bias=Bv[:, c:c + 1], scale=A[:, c:c + 1]),
                     r=[tr, R_const], w=[rh])
            else:
                P.op("act", ACT(h[:, c, :N], tt[:, :N], AF.Identity, scale=A[:, c:c + 1]), r=[tr, R_const], w=[rh])
            if flag is not None:
                P.op("dve", TS(h[:, c, :N], h[:, c, :N], flag, None, ALU.mult), r=[rh, R_cs[11]], w=[rh])

    epsT = sb("epsT", [128, 1], F32)
    R_eps = Res()
    P.op("pool", MS(epsT[:], EPS), w=[R_eps])

    def make_env(ph, tag, ps_n=1):
        def sbp(name, shape, dt):
            return ph.enter_context(nc.sbuf_tensor(un(name), list(shape), dt))
        return {
            "sq": Ring([sbp(f"{tag}_sq{i}", [128, 512], BF16) for i in range(3)]),
            "tmp": Ring([sbp(f"{tag}_tmp{i}", [128, 512], F32) for i in range(3)]),
            "rstd": Ring([sbp(f"{tag}_rstd{i}", [128, 512], F32) for i in range(2)]),
            "ps": Ring([ph.enter_context(nc.psum_tensor(un(f"{tag}_nps{i}"), [128, 512], F32)) for i in range(ps_n)]),
            "eps": epsT, "reps": R_eps,
        }

    def attention_layer(l):
        i = l // 2
        xw_fm, R_xw = xw_fms[cur["i"]], R_xws[cur["i"]]
        with_ctx_out = l < 2
        lam_init = 0.8 - 0.6 * math.exp(-0.3 * l)
        R_Qs = [[Res() for _ in range(NB)] for _ in range(8)]
        R_Qc = [Res() for _ in range(8)]
        R_Kl = [Res() for _ in range(10)]
        R_Vl = [Res() for _ in range(10)]
        R_Ka = [Res() for _ in range(10)]
        R_Va = [Res() for _ in range(10)]
        R_Kc, R_Vc = Res(), Res()
        R_AT = [[Res() for _ in range(NB)] for _ in range(12)]
        R_ATc = [Res() for _ in range(12)]

        R_lam = Res()
        with ExitStack() as ph:
            prod = ph.enter_context(nc.sbuf_tensor(un("lprod"), [1, 2, 64], F32))
            rp = Res()
            P.op("dve", TT(prod[:, 0, :], lqk[:, i, 0, :], lqk[:, i, 1, :], ALU.mult), r=[R_cs[7]], w=[rp])
            P.op("dve", TT(prod[:, 1, :], lqk[:, i, 2, :], lqk[:, i, 3, :], ALU.mult), r=[R_cs[7]], w=[rp])
            P.op("dve", lambda e: e.reduce_sum(lamt[:, i, 0:2], prod[:], mybir.AxisListType.X), r=[rp], w=[R_lam])
            P.op("act", ACT(lamt[:, i, 2:4], lamt[:, i, 0:2], AF.Exp), r=[R_lam], w=[R_lam])
            P.op("dve", TT(lamt[:, i, 4:5], lamt[:, i, 3:4], lamt[:, i, 2:3], ALU.subtract), r=[R_lam], w=[R_lam])
            P.op("dve", TS(lamt[:, i, 5:6], lamt[:, i, 4:5], -lam_init, None, ALU.add), r=[R_lam], w=[R_lam])
            P.barrier()
        nlam = lamt[0:1, i, 5:6]

        with ExitStack() as ph:
            def sbp(name, shape, dt):
                return ph.enter_context(nc.sbuf_tensor(un(name), list(shape), dt))

            def psp(name, shape, dt=F32):
                return ph.enter_context(nc.psum_tensor(un(name), list(shape), dt))

            env = make_env(ph, f"a1_{l}")
            wqk = sbp("wqk_s", [128, NCH, 1664], BF16)
            wqs = sbp("wqs_s", [128, NCH, 1664], BF16)
            wv = sbp("wv_s", [128, NCH, 640], BF16)
            R_wq, R_wqs_, R_wv_ = Res(), Res(), Res()
            P.dma("sp", DMA(wqk[:], wqk_b.ap()[i].rearrange("(k p) n -> p k n", p=128)), r=[R_w[("wqk", i)]], w=[R_wq])
            P.dma("sp", DMA(wqs[:], wqs_b.ap()[i].rearrange("(k p) n -> p k n", p=128)), r=[R_w[("wqs", i)]], w=[R_wqs_])
            P.dma("sp", DMA(wv[:], wv_b.ap()[i].rearrange("(k p) n -> p k n", p=128)), r=[R_w[("wv", i)]], w=[R_wv_])
            xring = Ring([sbp(f"a1x{j}", [128, NCH, 512], F32) for j in range(2)])
            hring = Ring([sbp(f"a1h{j}", [128, NCH, 512], BF16) for j in range(2)])
            ropeC = Ring([sbp(f"rC{j}", [128, 512], F32) for j in range(2)])
            ropeS = Ring([sbp(f"rS{j}", [128, 512], F32) for j in range(2)])
            pA = Ring([psp(f"pA{j}", [128, 512]) for j in range(2)])
            pB = Ring([psp(f"pB{j}", [128, 512]) for j in range(2)])
            pG = Ring([psp("pG0", [128, 512])])
            pV = Ring([psp(f"pV{j}", [128, 640]) for j in range(1)])
            t1r = Ring([sbp(f"t1_{j}", [128, 512], F32) for j in range(2)])
            t2r = Ring([sbp(f"t2_{j}", [128, 512], F32) for j in range(2)])
            gsq = Ring([sbp(f"gsq{j}", [128, 512], BF16) for j in range(2)])
            grs = Ring([sbp(f"grs{j}", [128, 512], F32) for j in range(2)])
            outr = Ring([sbp(f"qko{j}", [128, 512], BF16) for j in range(3)])
            vrow = Ring([sbp(f"vrow{j}", [128, 640], BF16) for j in range(2)])

            def project(N, src_fm, rsrc, col0, stream, b_idx):
                lat = stream == 0
                xt, rx = xring.next()
                P.dma("sp", DMA(xt[:, :, :N], src_fm[:, :, col0:col0 + N]), r=[rsrc], w=[rx])
                ht, rh = hring.next()
                norm_mod(env, xt, rx, N, modv[:, l, 0, stream, :], modv[:, l, 1, stream, :], ht, rh)
                if lat:
                    rc, rrc = ropeC.next()
                    rs_, rrs = ropeS.next()
                    P.dma("sp", DMA(rc[:], ropeC_in.ap()[:, b_idx * 512:(b_idx + 1) * 512]), w=[rrc])
                    P.dma("sp", DMA(rs_[:], ropeS_in.ap()[:, b_idx * 512:(b_idx + 1) * 512]), w=[rrs])
                for t in range(13):
                    is_q = t < 4 or 8 <= t < 12
                    if is_q and (not lat) and (not with_ctx_out):
                        continue
                    gq = t >= 8
                    pa, rpa = pA.next()
                    for k in range(NCH):
                        P.op("pe", MM(pa[:, :N], wqk[:, k, t * 128:(t + 1) * 128], ht[:, k, :N],
                                      start=(k == 0), stop=(k == NCH - 1)), r=[R_wq, rh], w=[rpa])
                    if lat:
                        pb, rpb = pB.next()
                        for k in range(NCH):
                            P.op("pe", MM(pb[:, :N], wqs[:, k, t * 128:(t + 1) * 128], ht[:, k, :N],
                                          start=(k == 0), stop=(k == NCH - 1)), r=[R_wqs_, rh], w=[rpb])
                    if gq:
                        g_idx = t - 8
                        sqt, rsq = gsq.next()
                        P.op("act", ACT(sqt[:, :N], pa[:, :N], AF.Square), r=[rpa], w=[rsq])
                        pg, rpg = pG.next()
                        P.op("pe", MM(pg[:, :N], bd_bf[:], sqt[:, :N]), r=[rsq, R_ones], w=[rpg])
                        rst, rrst = grs.next()
                        P.op("act", ACT(rst[:, :N], pg[:, :N], AF.Sqrt, bias=epsT[:, 0:1], scale=1.0 / 64), r=[rpg, R_eps], w=[rrst])
                        P.op("dve", RCP(rst[:, :N], rst[:, :N]), r=[rrst], w=[rrst])
                    ot, rot = outr.next()
                    if lat:
                        t1, rt1 = t1r.next()
                        t2, rt2 = t2r.next()
                        if gq:
                            P.op("dve", STT(t1[:, :N], pa[:, :N], ggain[:, i, g_idx, 0:1], rc[:, :N], ALU.mult, ALU.mult),
                                 r=[rpa, rrc, R_cs[5]], w=[rt1])
                            P.op("dve", STT(t2[:, :N], pb[:, :N], ggain[:, i, g_idx, 1:2], rs_[:, :N], ALU.mult, ALU.mult),
                                 r=[rpb, rrs, R_cs[5]], w=[rt2])
                            P.op("pool", TT(t1[:, :N], t1[:, :N], t2[:, :N], ALU.add), r=[rt1, rt2], w=[rt1])
                            P.op("dve", TT(ot[:, :N], t1[:, :N], rst[:, :N], ALU.mult), r=[rt1, rrst], w=[rot])
                        else:
                            P.op("dve", TT(t1[:, :N], pa[:, :N], rc[:, :N], ALU.mult), r=[rpa, rrc], w=[rt1])
                            P.op("dve", TT(t2[:, :N], pb[:, :N], rs_[:, :N], ALU.mult), r=[rpb, rrs], w=[rt2])
                            P.op("pool", TT(ot[:, :N], t1[:, :N], t2[:, :N], ALU.add), r=[rt1, rt2], w=[rot])
                    else:
                        if gq:
                            P.op("dve", STT(ot[:, :N], pa[:, :N], ggain[:, i, g_idx, 0:1], rst[:, :N], ALU.mult, ALU.mult),
                                 r=[rpa, rrst, R_cs[5]], w=[rot])
                        else:
                            P.op("act", ACT(ot[:, :N], pa[:, :N], AF.Identity), r=[rpa], w=[rot])
                    if is_q:
                        qi = t if t < 4 else t - 4
                        if lat:
                            P.dma("sp", DMA(Qs.ap()[qi, :, col0 - 1:col0 - 1 + N], ot[:, :N]), r=[rot], w=[R_Qs[qi][b_idx]])
                        else:
                            P.dma("sp", DMA(Qc.ap()[qi], ot[:, :N]), r=[rot], w=[R_Qc[qi]])
                    else:
                        ki = t - 4 if t < 8 else 4
                        if lat:
                            for hf in range(2):
                                P.dma("sp", DMA(Klp[2 * ki + hf].ap()[:, col0 - 1:col0 - 1 + N], ot[hf * 64:(hf + 1) * 64, :N]),
                                      r=[rot], w=[R_Kl[2 * ki + hf]])
                        else:
                            P.dma("sp", DMA(Kc.ap()[ki * 128:(ki + 1) * 128, :], ot[:, :N]), r=[rot], w=[R_Kc])
                for tt_ in range(N // 128):
                    pv, rpv = pV.next()
                    for k in range(NCH):
                        P.op("pe", MM(pv[:, 0:512], ht[:, k, tt_ * 128:(tt_ + 1) * 128], wv[:, k, 0:512],
                                      start=(k == 0), stop=(k == NCH - 1)), r=[R_wv_, rh], w=[rpv])
                    for k in range(NCH):
                        P.op("pe", MM(pv[:, 512:640], ht[:, k, tt_ * 128:(tt_ + 1) * 128], wv[:, k, 512:640],
                                      start=(k == 0), stop=(k == NCH - 1)), r=[R_wv_, rh], w=[rpv])
                    vr, rvr = vrow.next()
                    P.op("act", ACT(vr[:], pv[:], AF.Identity), r=[rpv], w=[rvr])
                    if lat:
                        tile_idx = (col0 - 1) // 128 + tt_
                        for gi_ in range(5):
                            for hf in range(2):
                                P.dma("sp", DMA(Vlp[2 * gi_ + hf].ap()[:, tile_idx * 128:(tile_idx + 1) * 128],
                                                vr[hf * 64:(hf + 1) * 64, gi_ * 128:(gi_ + 1) * 128]), r=[rvr], w=[R_Vl[2 * gi_ + hf]])
                    else:
                        dst = Vc.ap().rearrange("(g p) (t c) -> p g t c", p=128, c=128)[:, :, tt_, :]
                        P.dma("sp", DMA(dst, vr[:].rearrange("p (g c) -> p g c", c=128)), r=[rvr], w=[R_Vc])

            project(CTX, cw_fm, R_cw, 0, 1, 0)
            for b in range(NB):
                project(512, xw_fm, R_xw[b], 1 + b * 512, 0, b)
            P.barrier()

        groups = [[0, 1, 2, 3], [4, 5, 6, 7]]
        for p_ in range(10):
            P.cc(lambda e, p_=p_: e.collective_compute("AllGather", ALU.bypass, replica_groups=groups,
                                                       ins=[Klp[p_].ap().opt()], outs=[Kap[p_].ap().opt()]),
                 r=[R_Kl[p_]], w=[R_Ka[p_]])
            P.cc(lambda e, p_=p_: e.collective_compute("AllGather", ALU.bypass, replica_groups=groups,
                                                       ins=[Vlp[p_].ap().opt()], outs=[Vap[p_].ap().opt()]),
                 r=[R_Vl[p_]], w=[R_Va[p_]])

        with ExitStack() as ph:
            def sbp(name, shape, dt):
                return ph.enter_context(nc.sbuf_tensor(un(name), list(shape), dt))

            def psp(name, shape, dt=F32):
                return ph.enter_context(nc.psum_tensor(un(name), list(shape), dt))

            KB = [sbp(f"KB{j}", [128, CTX + S], BF16) for j in range(2)]
            VB = [sbp(f"VB{j}", [128, NKT * 130], BF16) for j in range(2)]
            R_KB = [[Res() for _ in range(9)] for _ in range(2)]
            R_VB = [[Res() for _ in range(9)] for _ in range(2)]
            qring = Ring([sbp(f"qt{j}", [128, 512], BF16) for j in range(2)])
            pring = Ring([sbp(f"pt{j}", [128, 512], BF16) for j in range(8)])
            accr = Ring([sbp(f"acc{j}", [128, 512], F32) for j in range(4)])
            psS = Ring([psp(f"psS{j}", [128, 512]) for j in range(4)])
            psO = Ring([psp(f"psO{j}", [128, 512]) for j in range(3)])
            psE = Ring([psp("psE0", [128, 512])])
            osb = Ring([sbp(f"osb{j}", [128, 512], F32) for j in range(4)])
            lsb = Ring([sbp(f"lsb{j}", [128, 512], F32) for j in range(2)])
            bcs = Ring([sbp(f"bcs{j}", [128, 512], F32) for j in range(2)])
            ework = Ring([sbp(f"ew{j}", [128, 512], F32) for j in range(3)])
            esq = Ring([sbp("esq0", [128, 512], BF16)])
            aout = Ring([sbp(f"aout{j}", [128, 512], BF16) for j in range(2)])

            def load_kv(g, slot):
                kb, vb = KB[slot], VB[slot]
                rk, rv = R_KB[slot], R_VB[slot]
                P.dma("sp", DMA(kb[:, 0:CTX], Kc.ap()[g * 128:(g + 1) * 128, :]), r=[R_Kc], w=[rk[0]])
                for r_ in range(4):
                    for hf in range(2):
                        P.dma("sp", DMA(kb[hf * 64:(hf + 1) * 64, CTX + r_ * TPC:CTX + (r_ + 1) * TPC],
                                        Kap[2 * g + hf].ap()[r_ * 64:(r_ + 1) * 64, :]), r=[R_Ka[2 * g + hf]], w=[rk[1 + 2 * r_ + hf]])
                if g < 4:
                    v3 = vb[:, 0:NKT * 128].rearrange("p (t c) -> p t c", c=128)
                    P.dma("sp", DMA(v3[:, 0:2, :], Vc.ap()[g * 128:(g + 1) * 128, :].rearrange("p (t c) -> p t c", c=128)),
                          r=[R_Vc], w=[rv[0]])
                    for r_ in range(4):
                        for hf in range(2):
                            P.dma("sp", DMA(v3[hf * 64:(hf + 1) * 64, 2 + r_ * NLT:2 + (r_ + 1) * NLT, :],
                                            Vap[2 * g + hf].ap()[r_ * 64:(r_ + 1) * 64, :].rearrange("p (t c) -> p t c", c=128)),
                                  r=[R_Va[2 * g + hf]], w=[rv[1 + 2 * r_ + hf]])
                else:
                    v3 = vb[:].rearrange("p (t c) -> p t c", c=130)
                    for hh in range(2):
                        P.dma("sp", DMA(v3[:, 0:2, hh * 65:hh * 65 + 64],
                                        Vc.ap()[g * 128:(g + 1) * 128, :].rearrange("p (t c) -> p t c", c=128)[:, :, hh * 64:(hh + 1) * 64]),
                              r=[R_Vc], w=[rv[0]])
                        for r_ in range(4):
                            for hf in range(2):
                                P.dma("sp", DMA(v3[hf * 64:(hf + 1) * 64, 2 + r_ * NLT:2 + (r_ + 1) * NLT, hh * 65:hh * 65 + 64],
                                                Vap[2 * g + hf].ap()[r_ * 64:(r_ + 1) * 64, :].rearrange("p (t c) -> p t c", c=128)[:, :, hh * 64:(hh + 1) * 64]),
                                      r=[R_Va[2 * g + hf]], w=[rv[1 + 2 * r_ + hf]])
                    for hh in range(2):
                        P.op("pool", MS(v3[:, :, hh * 65 + 64:hh * 65 + 65], 1.0), r=[], w=rv)

            def unit_pair(g, slot, qt, rq, N, nkt):
                kb, vb = KB[slot], VB[slot]
                rk, rv = R_KB[slot], R_VB[slot]
                diff = g < 4
                if diff:
                    v3 = vb[:, 0:NKT * 128].rearrange("p (t c) -> p t c", c=128)
                    cd = [(0, 128), (0, 128)]
                else:
                    v3 = vb[:].rearrange("p (t c) -> p t c", c=130)
                    cd = [(0, 65), (65, 65)]
                po = [psO.next() for _ in range(2)]
                acc = [accr.next() for _ in range(2)] if diff else None
                ps_list = {}

                def issue_S(kt):
                    for m in range(2):
                        st, rst_ = psS.next()
                        ps_list[(kt, m)] = (st, rst_)
                        P.op("pe", MM(st[:, :N], kb[m * 64:(m + 1) * 64, kt * 128:(kt + 1) * 128],
                                      qt[m * 64:(m + 1) * 64, :N]), r=rk + [rq], w=[rst_])

                for kt in range(min(2, nkt)):
                    issue_S(kt)
                for kt in range(nkt):
                    pts = []
                    for m in range(2):
                        st, rst_ = ps_list.pop((kt, m))
                        pt_, rpt = pring.next()
                        P.op("act", ACT(pt_[:, :N], st[:, :N], AF.Exp, scale=0.125), r=[rst_], w=[rpt])
                        pts.append((pt_, rpt))
                    for m in range(2):
                        pt_, rpt = pts[m]
                        c0, dvp = cd[m]
                        P.op("pe", MM(po[m][0][0:dvp, :N], v3[:, kt, c0:c0 + dvp], pt_[:, :N],
                                      start=(kt == 0), stop=(kt == nkt - 1)), r=rv + [rpt], w=[po[m][1]])
                    if diff:
                        for m in range(2):
                            pt_, rpt = pts[m]
                            at_, rat_ = acc[m]
                            if kt == 0:
                                P.op("dve", CP(at_[:, :N], pt_[:, :N]), r=[rpt], w=[rat_])
                            else:
                                P.op("dve", TT(at_[:, :N], at_[:, :N], pt_[:, :N], ALU.add), r=[rpt, rat_], w=[rat_])
                    if kt + 2 < nkt:
                        issue_S(kt + 2)
                outs = []
                for m in range(2):
                    c0, dvp = cd[m]
                    ot, rot = osb.next()
                    P.op("dve", CP(ot[0:dvp, :N], po[m][0][0:dvp, :N]), r=[po[m][1]], w=[rot])
                    if diff:
                        at_, rat_ = acc[m]
                        pe_, rpe = psE.next()
                        P.op("pe", MM(pe_[0:1, :N], ones_f[:, 0:1], at_[:, :N]), r=[rat_, R_ones], w=[rpe])
                        lt, rlt = lsb.next()
                        P.op("dve", CP(lt[0:1, :N], pe_[0:1, :N]), r=[rpe], w=[rlt])
                        outs.append((ot, rot, lt, rlt))
                    else:
                        outs.append((ot, rot, None, None))
                return outs

            def bcast_row(row_ap, rrow, base, nparts, N):
                pe_, rpe = psE.next()
                P.op("pe", MM(pe_[0:nparts, :N], ones_f[base:base + 1, 0:nparts], row_ap), r=[rrow, R_ones], w=[rpe])
                bt, rbt = bcs.next()
                P.op("act", ACT(bt[0:nparts, :N], pe_[0:nparts, :N], AF.Identity), r=[rpe], w=[rbt])
                return bt, rbt

            def epilogue_gqa(ot, rot, N, dst_ap, rdst):
                P.op("dve", RCP(ot[64:65, :N], ot[64:65, :N]), r=[rot], w=[rot])
                bt, rbt = bcast_row(ot[64:65, :N], rot, 64, 64, N)
                at, rat = aout.next()
                P.op("dve", TT(at[0:64, :N], ot[0:64, :N], bt[0:64, :N], ALU.mult), r=[rot, rbt], w=[rat])
                P.dma("sp", DMA(dst_ap, at[0:64, :N]), r=[rat], w=[rdst])

            def epilogue_diff(o1, ro1, l1, rl1, o2, ro2, l2, rl2, N, dst_ap, rdst):
                P.op("dve", RCP(l1[0:1, :N], l1[0:1, :N]), r=[rl1], w=[rl1])
                P.op("dve", RCP(l2[0:1, :N], l2[0:1, :N]), r=[rl2], w=[rl2])
                P.op("dve", TS(l2[0:1, :N], l2[0:1, :N], nlam, None, ALU.mult), r=[rl2, R_lam], w=[rl2])
                b1, rb1 = bcast_row(l1[0:1, :N], rl1, 0, 128, N)
                w1, rw1 = ework.next()
                P.op("dve", TT(w1[:, :N], o1[:, :N], b1[:, :N], ALU.mult), r=[ro1, rb1], w=[rw1])
                b2, rb2 = bcast_row(l2[0:1, :N], rl2, 0, 128, N)
                w2, rw2 = ework.next()
                P.op("dve", TT(w2[:, :N], o2[:, :N], b2[:, :N], ALU.mult), r=[ro2, rb2], w=[rw2])
                P.op("dve", TT(w1[:, :N], w1[:, :N], w2[:, :N], ALU.add), r=[rw1, rw2], w=[rw1])
                sq, rsq = esq.next()
                P.op("act", ACT(sq[:, :N], w1[:, :N], AF.Square), r=[rw1], w=[rsq])
                pe_, rpe = psE.next()
                P.op("pe", MM(pe_[:, :N], ones_bf[:], sq[:, :N]), r=[rsq, R_ones], w=[rpe])
                w3, rw3 = ework.next()
                P.op("act", ACT(w3[:, :N], pe_[:, :N], AF.Sqrt, bias=epsT[:, 0:1], scale=1.0 / 128), r=[rpe, R_eps], w=[rw3])
                P.op("dve", RCP(w3[:, :N], w3[:, :N]), r=[rw3], w=[rw3])
                P.op("dve", TT(w1[:, :N], w1[:, :N], w3[:, :N], ALU.mult), r=[rw1, rw3], w=[rw1])
                at, rat = aout.next()
                P.op("dve", TS(at[:, :N], w1[:, :N], subln[:, i:i + 1], 1.0 - lam_init, ALU.mult, ALU.mult),
                     r=[rw1, R_cs[6]], w=[rat])
                P.dma("sp", DMA(dst_ap, at[:, :N]), r=[rat], w=[rdst])

            def run_group(g, slot, lat_chunks=True):
                qtiles = [g] if g < 4 else [4, 5, 6, 7]
                chunks = []
                if with_ctx_out:
                    chunks.append(("ctx", 0))
                chunks += [("lat", b) for b in range(NB)]
                for qi in qtiles:
                    for (kind, b) in chunks:
                        qt, rq = qring.next()
                        if kind == "ctx":
                            N, nkt = CTX, CTX // 128
                            P.dma("sp", DMA(qt[:, :N], Qc.ap()[qi]), r=[R_Qc[qi]], w=[rq])
                        else:
                            N, nkt = 512, NKT
                            P.dma("sp", DMA(qt[:, :N], Qs.ap()[qi, :, b * 512:(b + 1) * 512]), r=[R_Qs[qi][b]], w=[rq])
                        res = unit_pair(g, slot, qt, rq, N, nkt)
                        if g < 4:
                            if kind == "ctx":
                                dst, rdst = ATc.ap()[g], R_ATc[g]
                            else:
                                dst, rdst = AT.ap()[g, :, b * 512:(b + 1) * 512], R_AT[g][b]
                            epilogue_diff(res[0][0], res[0][1], res[0][2], res[0][3],
                                          res[1][0], res[1][1], res[1][2], res[1][3], N, dst, rdst)
                        else:
                            j = qi - 4
                            for m in range(2):
                                hidx = 4 + j + 4 * m
                                if kind == "ctx":
                                    dst, rdst = ATc.ap()[hidx, 0:64, :], R_ATc[hidx]
                                else:
                                    dst, rdst = AT.ap()[hidx, 0:64, b * 512:(b + 1) * 512], R_AT[hidx][b]
                                epilogue_gqa(res[m][0], res[m][1], N, dst, rdst)

            if l == 0:
                conv_ffn_weights(0)
                conv_mixer_weights(1)
                conv_ffn_weights(1)
                conv_mixer_weights(2)
            else:
                conv_ffn_weights(l)
                conv_mixer_weights(l + 1)
                conv_ffn_weights(l + 1)
            load_kv(0, 0)
            for g in range(5):
                if g + 1 < 5:
                    load_kv(g + 1, (g + 1) % 2)
                run_group(g, g % 2)
            P.barrier()

        with ExitStack() as ph:
            def sbp(name, shape, dt):
                return ph.enter_context(nc.sbuf_tensor(un(name), list(shape), dt))

            def psp(name, shape, dt=F32):
                return ph.enter_context(nc.psum_tensor(un(name), list(shape), dt))

            wod = sbp("wod_s", [128, 4, D], BF16)
            wog = sbp("wog_s", [64, 8, D], BF16)
            R_wod, R_wog = Res(), Res()
            P.dma("sp", DMA(wod[:], wod_b.ap()[i].rearrange("(k p) n -> p k n", p=128)), r=[R_w[("wod", i)]], w=[R_wod])
            P.dma("sp", DMA(wog[:], wog_b.ap()[i]), r=[R_w[("wog", i)]], w=[R_wog])
            atr = Ring([sbp(f"atb{j}", [128, 12, 512], BF16) for j in range(2)])
            xr = Ring([sbp(f"a4x{j}", [128, NCH, 512], F32) for j in range(2)])
            py = Ring([psp(f"py{j}", [128, 512]) for j in range(3)])

            def outproj(N, at_src, rat_list, x_fm, rx_dram, col0, stream):
                at_, rat_ = atr.next()
                P.dma("sp", DMA(at_[:, 0:4, :N], at_src[0:4].rearrange("a p t -> p a t")), r=rat_list[0:4], w=[rat_])
                P.dma("sp", DMA(at_[0:64, 4:12, :N], at_src[4:12, 0:64, :].rearrange("a p t -> p a t")), r=rat_list[4:12], w=[rat_])
                xt, rx = xr.next()
                P.dma("sp", DMA(xt[:, :, :N], x_fm[:, :, col0:col0 + N]), r=[rx_dram], w=[rx])
                for fo in range(NCH):
                    pt_, rp = py.next()
                    for a in range(4):
                        P.op("pe", MM(pt_[:, :N], wod[:, a, fo * 128:(fo + 1) * 128], at_[:, a, :N],
                                      start=(a == 0), stop=False), r=[R_wod, rat_], w=[rp])
                    for a in range(8):
                        P.op("pe", MM(pt_[:, :N], wog[:, a, fo * 128:(fo + 1) * 128], at_[0:64, 4 + a, :N],
                                      start=False, stop=(a == 7)), r=[R_wog, rat_], w=[rp])
                    P.op("dve", STT(xt[:, fo, :N], pt_[:, :N], modv[:, l, 2, stream, fo:fo + 1], xt[:, fo, :N], ALU.mult, ALU.add),
                         r=[rp, rx, R_const], w=[rx])
                P.dma("sp", DMA(x_fm[:, :, col0:col0 + N], xt[:, :, :N]), r=[rx], w=[rx_dram])

            if with_ctx_out:
                outproj(CTX, ATc.ap(), R_ATc, cw_fm, R_cw, 0, 1)
            for b in range(NB):
                outproj(512, AT.ap()[:, :, b * 512:(b + 1) * 512], [R_AT[a][b] for a in range(12)], xw_fm, R_xw[b], 1 + b * 512, 0)
            P.barrier()

    def sgu_layer(l):
        i = l // 2
        xw_fm, R_xw = xw_fms[cur["i"]], R_xws[cur["i"]]
        with_ctx = l < 2
        with ExitStack() as ph:
            def sbp(name, shape, dt):
                return ph.enter_context(nc.sbuf_tensor(un(name), list(shape), dt))

            def psp(name, shape, dt=F32):
                return ph.enter_context(nc.psum_tensor(un(name), list(shape), dt))

            env = make_env(ph, f"sg_{l}")
            wu = sbp("swu_s", [128, NCH, D], BF16)
            wvv = sbp("swv_s", [128, NCH, D], BF16)
            wo = sbp("swo_s", [128, NCH, D], BF16)
            wsT = sbp("swsT_s", [128, 4, 128], BF16)
            R_wu, R_wvv, R_wo, R_wsT = Res(), Res(), Res(), Res()
            P.dma("sp", DMA(wu[:], swu_b.ap()[i].rearrange("(k p) n -> p k n", p=128)), r=[R_w[("swu", i)]], w=[R_wu])
            P.dma("sp", DMA(wvv[:], swv_b.ap()[i].rearrange("(k p) n -> p k n", p=128)), r=[R_w[("swv", i)]], w=[R_wvv])
            P.dma("sp", DMA(wo[:], swo_b.ap()[i].rearrange("(k p) n -> p k n", p=128)), r=[R_w[("swo", i)]], w=[R_wo])
            P.dma("sp", DMA(wsT[:], swsT_b.ap()[i]), r=[R_w[("swsT", i)]], w=[R_wsT])
            xr = Ring([sbp(f"sgx{j}", [128, NCH, 512], F32) for j in range(2)])
            hr = Ring([sbp(f"sgh{j}", [128, NCH, 512], BF16) for j in range(2)])
            uT = Ring([sbp(f"sgu{j}", [128, NCH, 512], BF16) for j in range(1)])
            suT = Ring([sbp(f"sgsu{j}", [128, NCH, 512], BF16) for j in range(1)])
            vg = Ring([sbp(f"sgv{j}", [128, D], F32) for j in range(2)])
            vn = Ring([sbp(f"sgvn{j}", [128, D], BF16) for j in range(2)])
            vjunk = Ring([sbp("sgjunk", [128, D], BF16)])
            vss = Ring([sbp(f"sgss{j}", [128, 2], F32) for j in range(2)])
            mt = Ring([sbp(f"sgm{j}", [128, 128], F32) for j in range(3)])
            pU = Ring([psp(f"pU{j}", [128, 512]) for j in range(2)])
            pVv = Ring([psp(f"pVv{j}", [128, D]) for j in range(1)])
            pM = Ring([psp(f"pM{j}", [128, 512]) for j in range(2)])
            pY = Ring([psp(f"pY{j}", [128, 512]) for j in range(1)])

            def sgu_block(N, x_fm, rx_dram, col0, stream):
                xt, rx = xr.next()
                P.dma("sp", DMA(xt[:, :, :N], x_fm[:, :, col0:col0 + N]), r=[rx_dram], w=[rx])
                ht, rh = hr.next()
                norm_mod(env, xt, rx, N, modv[:, l, 0, stream, :], modv[:, l, 1, stream, :], ht, rh)
                ut, rut = uT.next()
                for fc in range(NCH):
                    pu, rpu = pU.next()
                    for k in range(NCH):
                        P.op("pe", MM(pu[:, :N], wu[:, k, fc * 128:(fc + 1) * 128], ht[:, k, :N],
                                      start=(k == 0), stop=(k == NCH - 1)), r=[R_wu, rh], w=[rpu])
                    P.op("act", ACT(ut[:, fc, :N], pu[:, :N], AF.Gelu), r=[rpu], w=[rut])
                sut, rsut = suT.next()
                for tt_ in range(N // 128):
                    pv, rpv = pVv.next()
                    for half in range(2):
                        for k in range(NCH):
                            P.op("pe", MM(pv[:, half * 512:(half + 1) * 512], ht[:, k, tt_ * 128:(tt_ + 1) * 128],
                                          wvv[:, k, half * 512:(half + 1) * 512], start=(k == 0), stop=(k == NCH - 1)),
                                 r=[R_wvv, rh], w=[rpv])
                    vt, rvt = vg.next()
                    P.op("act", ACT(vt[:], pv[:], AF.Gelu), r=[rpv], w=[rvt])
                    sst, rss = vss.next()
                    jk, rjk = vjunk.next()
                    P.op("dve", lambda e, jk=jk, vt=vt, sst=sst: e.tensor_tensor(jk[:], vt[:], vt[:], ALU.mult), r=[rvt], w=[rjk])
                    P.op("dve", lambda e, jk=jk, sst=sst: e.reduce_sum(sst[:, 0:1], jk[:], mybir.AxisListType.X), r=[rjk], w=[rss])
                    P.op("act", ACT(sst[:, 1:2], sst[:, 0:1], AF.Sqrt, bias=epsT[:, 0:1], scale=1.0 / D), r=[rss, R_eps], w=[rss])
                    P.op("dve", RCP(sst[:, 1:2], sst[:, 1:2]), r=[rss], w=[rss])
                    vnt, rvn = vn.next()
                    P.op("dve", TS(vnt[:], vt[:], sst[:, 1:2], None, ALU.mult), r=[rvt, rss], w=[rvn])
                    pm_, rpm = pM.next()
                    for gi in range(4):
                        for cc_ in range(2):
                            fc = gi * 2 + cc_
                            sl = pm_[:, (fc % 4) * 128:(fc % 4 + 1) * 128]
                            P.op("pe", MM(sl, vnt[:, fc * 128:(fc + 1) * 128], wsT[:, gi, :]), r=[rvn, R_wsT], w=[rpm])
                            mtt, rmt = mt.next()
                            P.op("dve", STT(mtt[:], sl, svn[:, i, fc:fc + 1], sbsb[:, i, gi, :], ALU.mult, ALU.add),
                                 r=[rpm, R_cs[8], R_cs[9]], w=[rmt])
                            P.op("pool", TT(sut[:, fc, tt_ * 128:(tt_ + 1) * 128], mtt[:], ut[:, fc, tt_ * 128:(tt_ + 1) * 128], ALU.mult),
                                 r=[rmt, rut], w=[rsut])
                            if fc == 3:
                                pm_, rpm = pM.next()
                for fo in range(NCH):
                    pt_, rp = pY.next()
                    for k in range(NCH):
                        P.op("pe", MM(pt_[:, :N], wo[:, k, fo * 128:(fo + 1) * 128], sut[:, k, :N],
                                      start=(k == 0), stop=(k == NCH - 1)), r=[R_wo, rsut], w=[rp])
                    P.op("dve", STT(xt[:, fo, :N], pt_[:, :N], modv[:, l, 2, stream, fo:fo + 1], xt[:, fo, :N], ALU.mult, ALU.add),
                         r=[rp, rx, R_const], w=[rx])
                P.dma("sp", DMA(x_fm[:, :, col0:col0 + N], xt[:, :, :N]), r=[rx], w=[rx_dram])

            if with_ctx:
                sgu_block(CTX, cw_fm, R_cw, 0, 1)
            for b in range(NB):
                sgu_block(512, xw_fm, R_xw[b], 1 + b * 512, 0)
            P.barrier()

    def halo_exchange(l):
        xw_fm, R_xw, R_xh = xw_fms[cur["i"]], R_xws[cur["i"]], R_xhs[cur["i"]]
        R_hl, R_ha = Res(), Res()
        with ExitStack() as ph:
            hs = ph.enter_context(nc.sbuf_tensor(un("hs"), [8, D], F32))
            ho = ph.enter_context(nc.sbuf_tensor(un("ho"), [128, 2, NCH, 1], F32))
            hp = ph.enter_context(nc.psum_tensor(un("hp"), [128, 2, NCH], F32))
            hb = ph.enter_context(nc.sbuf_tensor(un("hb"), [128, 2, NCH, 1], F32))
            rhs_, rho, rhp, rhb = Res(), Res(), Res(), Res()
            P.dma("sp", DMA(hb[:, 0, :, :], xw_fm[:, :, 1:2], allow_slow_non_contiguous=True), r=[R_xw[0]], w=[rhb])
            P.dma("sp", DMA(hb[:, 1, :, :], xw_fm[:, :, TPC:TPC + 1], allow_slow_non_contiguous=True), r=[R_xw[NB - 1]], w=[rhb])
            for s_ in range(2):
                P.dma("sp", DMA(HLl.ap()[s_].rearrange("(p c) -> p c", p=128), hb[:, s_, :, 0], allow_slow_non_contiguous=True), r=[rhb], w=[R_hl])
            P.cc(lambda e: e.collective_compute("AllGather", ALU.bypass, replica_groups=[[0, 1, 2, 3], [4, 5, 6, 7]],
                                                ins=[HLl.ap().opt()], outs=[HLa.ap().opt()]), r=[R_hl], w=[R_ha])
            P.dma("sp", DMA(hs[:], HLa.ap()), r=[R_ha], w=[rhs_])
            for s_ in range(2):
                for c in range(NCH):
                    P.op("pe", MM(hp[:, s_, c:c + 1], hs[:, c:D:NCH], sel[:, s_:s_ + 1]), r=[rhs_, R_cs[12]], w=[rhp])
            P.op("dve", CP(ho[:, :, :, 0], hp[:]), r=[rhp], w=[rho])
            P.dma("sp", DMA(xw_fm[:, :, 0:1], ho[:, 0, :, :], allow_slow_non_contiguous=True), r=[rho], w=[R_xh])
            P.dma("sp", DMA(xw_fm[:, :, TPC + 1:TPC + 2], ho[:, 1, :, :], allow_slow_non_contiguous=True), r=[rho], w=[R_xh])
            P.barrier()

    def ffn_layer(l):
        with_ctx = l < 2
        ci = cur["i"]
        xin_fm, R_xin, R_xh = xw_fms[ci], R_xws[ci], R_xhs[ci]
        xout_fm, R_xout = xw_fms[1 - ci], R_xws[1 - ci]
        with ExitStack() as ph:
            def sbp(name, shape, dt):
                return ph.enter_context(nc.sbuf_tensor(un(name), list(shape), dt))

            def psp(name, shape, dt=F32):
                return ph.enter_context(nc.psum_tensor(un(name), list(shape), dt))

            env = make_env(ph, f"ff_{l}")
            xt = sbp("ffx", [128, NCH, 1024], F32)
            xh = sbp("ffxh", [128, NCH, 2], F32)
            ht = sbp("ffh", [128, NCH, 1024], BF16)
            hh2 = sbp("ffhh", [128, NCH, 2], BF16)
            hzt = sbp("ffhz", [128, 88], F32)
            rx, rh, rxh, rhh, rhz = Res(), Res(), Res(), Res(), Res()
            wupr = Ring([sbp(f"wup{j}", [128, NCH, 512], BF16) for j in range(2)])
            wdnr = Ring([sbp(f"wdn{j}", [128, NPAIR, 128], BF16) for j in range(2)])
            aT = sbp("ffa", [128, NPAIR, 1024], BF16)
            ra = Res()
            zr = Ring([sbp(f"ffz{j}", [128, 1026], F32) for j in range(3)])
            tr_ = Ring([sbp(f"fft{j}", [128, 1024], F32) for j in range(3)])
            sg = Ring([sbp(f"ffs{j}", [128, 1024], F32) for j in range(2)])
            pH = Ring([psp("pH0", [128, 512])])
            pZ = Ring([psp(f"pZ{j}", [128, 512]) for j in range(5)])
            pYr = Ring([psp(f"pYf{j}", [128, 512]) for j in range(1)])

            def ffn_block(N, src_fm, r_src, dst_fm, r_dst, colL, colM, stream, zero_halo, first_blk=False, last_blk=False):
                nh = (N + 511) // 512
                hw = [min(512, N - hh * 512) for hh in range(nh)]
                P.dma("sp", DMA(xt[:, :, 0:N], src_fm[:, :, colM:colM + N]), r=r_src, w=[rx])
                A2, B2 = modv[:, l, 3, stream, :], modv[:, l, 4, stream, :]
                for hh in range(nh):
                    c0 = hh * 512
                    norm_mod(env, xt[:, :, c0:c0 + hw[hh]], rx, hw[hh], A2, B2, ht[:, :, c0:c0 + hw[hh]], rh)
                if zero_halo:
                    P.op("pool", MS(hh2[:], 0.0), w=[rhh])
                else:
                    P.dma("sp", DMA(xh[:, :, 0:1], src_fm[:, :, colL:colL + 1], allow_slow_non_contiguous=True), r=r_src + [R_xh], w=[rxh])
                    P.dma("sp", DMA(xh[:, :, 1:2], src_fm[:, :, colM + N:colM + N + 1], allow_slow_non_contiguous=True), r=r_src + [R_xh], w=[rxh])
                    norm_mod(env, xh, rxh, 2, A2, B2, hh2, rhh)
                    for c in range(NCH):
                        if first_blk:
                            P.op("dve", TS(hh2[:, c, 0:1], hh2[:, c, 0:1], flags[:, 0:1], None, ALU.mult), r=[rhh, R_cs[11]], w=[rhh])
                        if last_blk:
                            P.op("dve", TS(hh2[:, c, 1:2], hh2[:, c, 1:2], flags[:, 1:2], None, ALU.mult), r=[rhh, R_cs[11]], w=[rhh])
                wts = {}

                def get_w(j):
                    jj = j // 2
                    if jj not in wts:
                        wt, rw = wupr.next()
                        P.dma("sp", DMA(wt[:], wup_b.ap()[l].rearrange("(k p) n -> p k n", p=128)[:, :, jj * 512:(jj + 1) * 512]),
                              r=[R_w[("wup", l)]], w=[rw])
                        wts[jj] = (wt, rw)
                    wt, rw = wts[jj]
                    return wt, rw, (j % 2) * 256

                ph_, rph = pH.next()
                for j in range(NPAIR):
                    wt, rw, wc = get_w(j)
                    for s_ in range(2):
                        ti = j * 2 + s_
                        for k in range(NCH):
                            P.op("pe", MM(ph_[:, ti * 2:ti * 2 + 2], wt[:, k, wc + s_ * 128:wc + (s_ + 1) * 128], hh2[:, k, :],
                                          start=(k == 0), stop=(k == NCH - 1)), r=[rw, rhh], w=[rph])
                    P.op("act", ACT(hzt[:, j * 4:j * 4 + 4], ph_[:, j * 4:j * 4 + 4], AF.Identity), r=[rph], w=[rhz])
                    outs = []
                    for s_ in range(2):
                        zt, rz = zr.next()
                        for hh in range(nh):
                            pz, rpz = pZ.next()
                            c0 = hh * 512
                            for k in range(NCH):
                                P.op("pe", MM(pz[:, :hw[hh]], wt[:, k, wc + s_ * 128:wc + (s_ + 1) * 128], ht[:, k, c0:c0 + hw[hh]],
                                              start=(k == 0), stop=(k == NCH - 1)), r=[rw, rh], w=[rpz])
                            P.op("act", ACT(zt[:, 1 + c0:1 + c0 + hw[hh]], pz[:, :hw[hh]], AF.Identity), r=[rpz], w=[rz])
                        ti = j * 2 + s_
                        P.op("pool", CP(zt[:, 0:1], hzt[:, ti * 2:ti * 2 + 1]), r=[rhz], w=[rz])
                        P.op("pool", CP(zt[:, N + 1:N + 2], hzt[:, ti * 2 + 1:ti * 2 + 2]), r=[rhz], w=[rz])
                        tt, rt = tr_.next()
                        P.op("dve", TS(tt[:, :N], zt[:, 1:N + 1], convp[:, l, ti, 1:2], convp[:, l, ti, 3:4], ALU.mult, ALU.add),
                             r=[rz, R_cs[10]], w=[rt])
                        P.op("dve", STT(tt[:, :N], zt[:, 0:N], convp[:, l, ti, 0:1], tt[:, :N], ALU.mult, ALU.add),
                             r=[rz, rt, R_cs[10]], w=[rt])
                        P.op("dve", STT(tt[:, :N], zt[:, 2:N + 2], convp[:, l, ti, 2:3], tt[:, :N], ALU.mult, ALU.add),
                             r=[rz, rt, R_cs[10]], w=[rt])
                        outs.append((tt, rt))
                    st, rs = sg.next()
                    P.op("act", ACT(st[:, :N], outs[0][0][:, :N], AF.Silu), r=[outs[0][1]], w=[rs])
                    P.op("pool", TT(aT[:, j, :N], st[:, :N], outs[1][0][:, :N], ALU.mult), r=[rs, outs[1][1]], w=[ra])
                for fo in range(NCH):
                    wd, rwd = wdnr.next()
                    P.dma("sp", DMA(wd[:], wdn_b.ap()[l].rearrange("(j p) n -> p j n", p=128)[:, :, fo * 128:(fo + 1) * 128]),
                          r=[R_w[("wdn", l)]], w=[rwd])
                    for hh in range(nh):
                        c0 = hh * 512
                        py_, rpy = pYr.next()
                        for j in range(NPAIR):
                            P.op("pe", MM(py_[:, :hw[hh]], wd[:, j, :], aT[:, j, c0:c0 + hw[hh]],
                                          start=(j == 0), stop=(j == NPAIR - 1)), r=[rwd, ra], w=[rpy])
                        P.op("dve", STT(xt[:, fo, c0:c0 + hw[hh]], py_[:, :hw[hh]], modv[:, l, 5, stream, fo:fo + 1],
                                        xt[:, fo, c0:c0 + hw[hh]], ALU.mult, ALU.add), r=[rpy, rx, R_const], w=[rx])
                P.dma("sp", DMA(dst_fm[:, :, colM:colM + N], xt[:, :, 0:N]), r=[rx], w=r_dst)

            if with_ctx:
                ffn_block(CTX, cw_fm, [R_cw], cw_fm, [R_cw], 0, 0, 1, True)
            NBLK = max(1, TPC // 1024)
            BN = TPC // NBLK
            for b in range(NBLK):
                ks = [k for k in range(NB) if not ((k + 1) * 512 <= b * BN - 1 or k * 512 >= (b + 1) * BN + 1)]
                km = [k for k in range(NB) if b * BN <= k * 512 < (b + 1) * BN]
                ffn_block(BN, xin_fm, [R_xin[k] for k in ks], xout_fm, [R_xout[k] for k in km], b * BN, 1 + b * BN, 0, False,
                          first_blk=(b == 0), last_blk=(b == NBLK - 1))
            P.barrier()
        cur["i"] = 1 - ci

    def final_norm():
        with ExitStack() as ph:
            def sbp(name, shape, dt):
                return ph.enter_context(nc.sbuf_tensor(un(name), list(shape), dt))
            env = make_env(ph, "fin")
            xr = Ring([sbp(f"fnx{j}", [128, NCH, 512], F32) for j in range(2)])
            yr = Ring([sbp(f"fny{j}", [128, NCH, 512], F32) for j in range(2)])
            y_fm = yT_out.ap().rearrange("(c p) t -> p c t", p=128)
            xw_fm, R_xw = xw_fms[cur["i"]], R_xws[cur["i"]]
            for b in range(NB):
                xt, rx = xr.next()
                P.dma("sp", DMA(xt[:], xw_fm[:, :, 1 + b * 512:1 + (b + 1) * 512]), r=[R_xw[b]], w=[rx])
                yt, ry = yr.next()
                norm_mod(env, xt, rx, 512, finn, None, yt, ry)
                P.dma("sp", DMA(y_fm[:, :, b * 512:(b + 1) * 512], yt[:]), r=[ry], w=[Res()])

    import os
    dbg = os.environ.get("DBG_XMID")
    for l in range(NL):
        if l % 2 == 0:
            attention_layer(l)
        else:
            sgu_layer(l)
        if dbg:
            break
        halo_exchange(l)
        ffn_layer(l)
    if dbg:
        for b in range(NB):
            P.dma("sp", DMA(yT_out.ap()[:, b * 512:(b + 1) * 512], xw0.ap()[:, 1 + b * 512:1 + (b + 1) * 512]),
                  r=[R_xws[0][b]], w=[Res()])
    else:
        final_norm()
    P.finish()
    P.emit()
    es.close()
    return nc


def _fm(v):
    v = np.asarray(v, np.float32)
    return np.ascontiguousarray(v.reshape(-1, 128).T)


def prep_inputs(inp, TPC):
    f32 = lambda a: np.ascontiguousarray(np.asarray(a, dtype=np.float32))
    x = f32(inp["x"]); ctx = f32(inp["ctx"]); c = f32(inp["c"]); c_ctx = f32(inp["c_ctx"])
    S = x.shape[1]
    assert S == 4 * TPC
    shared = {}
    shared["ada_w"] = f32(inp["ada_w"])
    shared["ada_b"] = np.ascontiguousarray(np.stack([_fm(inp["ada_b"][l]) for l in range(DEPTH)], 1))
    shared["mixn"] = np.ascontiguousarray(np.stack([_fm(inp["mix_norm"][l]) for l in range(DEPTH)], 1))
    shared["ffnn"] = np.ascontiguousarray(np.stack([_fm(inp["ffn_norm"][l]) for l in range(DEPTH)], 1))
    shared["finn"] = _fm(inp["final_norm"])
    w_in = f32(inp["attn_w_in"])
    de = np.concatenate([np.arange(0, 64, 2), np.arange(1, 64, 2)])
    sw = np.concatenate([np.arange(1, 64, 2), np.arange(0, 64, 2)])
    off = {"q1": 0, "q2": 256, "k1": 512, "k2": 768, "va": 1024, "qb": 1536, "kb": 2048, "vb": 2176}

    def blk(name, h, perm):
        return off[name] + h * 64 + perm

    def tiles(perm):
        cols = []
        for h in range(4):
            cols += [blk("q1", h, perm), blk("q2", h, perm)]
        for h in range(4):
            cols += [blk("k1", h, perm), blk("k2", h, perm)]
        for j in range(4):
            cols += [blk("qb", j, perm), blk("qb", 4 + j, perm)]
        cols += [blk("kb", 0, perm), blk("kb", 1, perm)]
        return np.concatenate(cols)

    cn, cs = tiles(de), tiles(sw)
    shared["wqk"] = np.ascontiguousarray(w_in[:, :, cn])
    shared["wqs"] = np.ascontiguousarray(w_in[:, :, cs])
    vcols = np.concatenate([np.arange(1024, 1536), np.arange(2176, 2304)])
    shared["wv"] = np.ascontiguousarray(w_in[:, :, vcols])
    qn = f32(inp["gqa_q_norm"]); kn = f32(inp["gqa_k_norm"])
    gg = np.zeros((128, 2, 5, 2), np.float32)
    for i in range(2):
        for t in range(5):
            g = qn[i] if t < 4 else kn[i]
            gg[:, i, t, 0] = np.concatenate([g[de], g[de]])
            gg[:, i, t, 1] = np.concatenate([g[sw], g[sw]])
    shared["ggain"] = gg
    w_out = f32(inp["attn_w_out"])
    shared["wod"] = np.ascontiguousarray(w_out[:, 0:512, :])
    shared["wog"] = np.ascontiguousarray(w_out[:, 512:1024, :].reshape(2, 8, 64, D).transpose(0, 2, 1, 3))
    lq = np.stack([f32(inp["diff_lq1"]), f32(inp["diff_lk1"]), f32(inp["diff_lq2"]), f32(inp["diff_lk2"])], 1)
    shared["lqk"] = np.ascontiguousarray(lq[None])
    shared["subln"] = np.ascontiguousarray(f32(inp["diff_subln"]).T)
    swin = f32(inp["sgu_w_in"])
    shared["swu"] = np.ascontiguousarray(swin[:, :, 0:D])
    shared["swv"] = np.ascontiguousarray(swin[:, :, D:2 * D])
    shared["svn"] = np.ascontiguousarray(np.stack([_fm(inp["sgu_v_norm"][i]) for i in range(2)], 1))
    shared["swsT"] = np.ascontiguousarray(f32(inp["sgu_w_s"]).transpose(0, 3, 1, 2))
    shared["sbsb"] = np.ascontiguousarray(np.broadcast_to(f32(inp["sgu_b_s"])[None], (128, 2, 4, 128)))
    shared["swo"] = f32(inp["sgu_w_out"])
    wup = f32(inp["ffn_w_up"])
    shared["wup"] = np.ascontiguousarray(
        np.stack([wup[:, :, 0:FF].reshape(DEPTH, D, NPAIR, 128), wup[:, :, FF:2 * FF].reshape(DEPTH, D, NPAIR, 128)], 3)
        .reshape(DEPTH, D, NPAIR * 256))
    cwt = f32(inp["ffn_conv_w"]); cb = f32(inp["ffn_conv_b"])
    cp = np.zeros((128, DEPTH, 44, 4), np.float32)
    for l in range(DEPTH):
        for j in range(NPAIR):
            for s_ in range(2):
                f0 = s_ * FF + j * 128
                for k in range(3):
                    cp[:, l, j * 2 + s_, k] = cwt[l, k, f0:f0 + 128]
                cp[:, l, j * 2 + s_, 3] = cb[l, f0:f0 + 128]
    shared["convp"] = cp
    shared["wdn"] = f32(inp["ffn_w_down"])
    inv = (10000.0 ** (-np.arange(16, dtype=np.float32) / 16)).astype(np.float32)
    maps = []
    for core in range(8):
        b, r = core // 4, core % 4
        m = dict(shared)
        t0 = r * TPC
        m["xT"] = np.ascontiguousarray(x[b, t0:t0 + TPC, :].T)
        m["cT"] = np.ascontiguousarray(ctx[b].T)
        cv = np.zeros((128, NCH, 2), np.float32)
        cv[:, :, 0] = _fm(c[b])
        cv[:, :, 1] = _fm(c_ctx)
        m["cvec"] = cv
        t = np.arange(t0, t0 + TPC)
        row = (t // 64).astype(np.float32)
        col = (t % 64).astype(np.float32)
        ang = np.concatenate([row[None, :] * inv[:, None], col[None, :] * inv[:, None]], 0).astype(np.float32)
        cs_, sn_ = np.cos(ang).astype(np.float32), np.sin(ang).astype(np.float32)
        C64 = np.concatenate([cs_, cs_], 0)
        S64 = np.concatenate([-sn_, sn_], 0)
        m["ropeC"] = np.ascontiguousarray(np.concatenate([C64, C64], 0))
        m["ropeS"] = np.ascontiguousarray(np.concatenate([S64, S64], 0))
        fl = np.zeros((128, 2), np.float32)
        fl[:, 0] = 1.0 if r > 0 else 0.0
        fl[:, 1] = 1.0 if r < 3 else 0.0
        m["flags"] = fl
        sl = np.zeros((8, 2), np.float32)
        if r > 0:
            sl[2 * (r - 1) + 1, 0] = 1.0
        if r < 3:
            sl[2 * (r + 1), 1] = 1.0
        m["sel"] = sl
        maps.append(m)
    return maps


_NC_CACHE = {}


def run(inp, TPC, NL=DEPTH):
    key = (TPC, NL)
    if key not in _NC_CACHE:
        _NC_CACHE[key] = build(TPC, NL)
    nc = _NC_CACHE[key]
    maps = prep_inputs(inp, TPC)
    res = run_bass_kernel_spmd(nc, maps, core_ids=list(range(8)))
    B = 2
    out = np.zeros((B, 4 * TPC, D), np.float32)
    for core in range(8):
        b, r = core // 4, core % 4
        out[b, r * TPC:(r + 1) * TPC, :] = np.asarray(res.results[core]["yT"], np.float32).T
    return out


def kernel(**inputs):
    TPC = np.asarray(inputs["x"]).shape[1] // 4
    return run(inputs, TPC, DEPTH)
```

```python
import math
from contextlib import ExitStack

import numpy as np
import concourse.bass as bass
import concourse.mybir as mybir
from concourse.bass_utils import run_bass_kernel_spmd

F32 = mybir.dt.float32
BF16 = mybir.dt.bfloat16
AF = mybir.ActivationFunctionType
ALU = mybir.AluOpType

D = 1024
NCH = 8
CTX = 256
FF = 2816
NPAIR = 22
EPS = 1e-6
DEPTH = 4
ENGS = ("pe", "act", "dve", "pool", "sp")


class Res:
    __slots__ = ("w", "rc", "rd")

    def __init__(self):
        self.w = None
        self.rc = {}
        self.rd = []


class Op:
    __slots__ = ("eng", "kind", "fn", "deps", "idx", "needed", "count", "sem", "target")


class Prog:
    NS = 8

    def __init__(self, nc, es):
        self.nc = nc
        self.ops = {e: [] for e in ENGS}
        self.bar = {e: [] for e in ENGS}
        self.dma_n = {q: 0 for q in ("sp", "act", "pool")}
        self.dsem = {q: [es.enter_context(nc.semaphore(f"d_{q}_{i}")) for i in range(self.NS)]
                     for q in ("sp", "act", "pool")}
        self.csem = {e: es.enter_context(nc.semaphore(f"c_{e}")) for e in ("pe", "act", "dve", "pool")}
        self.es = es
        self.live_dma = []
        self.ncc = 0

    def _add(self, eng, kind, fn, r, w):
        o = Op()
        o.eng, o.kind, o.fn, o.needed, o.count, o.sem, o.target = eng, kind, fn, False, 0, None, 0
        o.idx = len(self.ops[eng])
        deps = {}

        def dep(x, why):
            if x is None or x is o:
                return
            k = id(x)
            if k in deps:
                if why == "raw":
                    deps[k] = (x, why)
                return
            deps[k] = (x, why)

        for x in r:
            dep(x.w, "raw")
        for x in w:
            dep(x.w, "waw")
            for rd in x.rc.values():
                dep(rd, "war")
            for rd in x.rd:
                dep(rd, "war")
        for x in self.bar[eng]:
            dep(x, "raw")
        self.bar[eng] = []
        o.deps = list(deps.values())
        for (x, why) in o.deps:
            if x.kind == "c":
                if x.eng != eng or kind != "c":
                    x.needed = True
                elif eng != "pe" and why == "raw" and (o.idx - x.idx) <= 2:
                    x.needed = True
        self.ops[eng].append(o)
        for x in r:
            if kind == "c":
                x.rc[eng] = o
            else:
                x.rd.append(o)
        for x in w:
            x.w = o
            x.rc = {}
            x.rd = []
        return o

    def op(self, eng, fn, r=(), w=()):
        return self._add(eng, "c", fn, r, w)

    def dma(self, q, fn, r=(), w=()):
        o = self._add(q, "d", fn, r, w)
        n = self.dma_n[q]
        self.dma_n[q] = n + 1
        o.sem = self.dsem[q][n % self.NS]
        o.target = 16 * (n // self.NS + 1)
        self.live_dma.append(o)
        return o

    def cc(self, fn, r=(), w=()):
        import os
        if os.environ.get("NO_CC"):
            return self._add("pool", "c", lambda e: e.engine_nop(), r, w)
        if not hasattr(self, "cc_chain"):
            self.cc_chain = Res()
        o = self._add("pool", "cc", fn, r, list(w) + [self.cc_chain])
        o.sem = self.es.enter_context(self.nc.semaphore(f"cc_{self.ncc}"))
        self.ncc += 1
        o.target = 1
        self.live_dma.append(o)
        return o

    def barrier(self):
        last = {}
        for e in ("pe", "act", "dve", "pool"):
            for o in reversed(self.ops[e]):
                if o.kind == "c":
                    last[e] = o
                    break
        for e in ENGS:
            self.bar[e] = [o for (k, o) in last.items() if k != e] + list(self.live_dma)
        self.live_dma = []

    def finish(self):
        self.barrier()
        self._add("sp", "nop", None, (), ())

    def emit(self):
        nc = self.nc
        for e in ("pe", "act", "dve", "pool"):
            c = 0
            for o in self.ops[e]:
                if o.kind == "c" and o.needed:
                    c += 1
                o.count = c if o.kind == "c" else 0
        prog = self

        def run(E, eng):
            seen = {}

            def wait(sem, val):
                k = id(sem)
                if seen.get(k, 0) >= val:
                    return
                eng.wait_ge(sem, val)
                seen[k] = val

            for o in prog.ops[E]:
                if o.kind == "d" and o.target > 16:
                    wait(o.sem, o.target - 16)
                for (x, why) in o.deps:
                    if x.kind == "c":
                        if x.eng == E and o.kind == "c":
                            if E == "pe" or why != "raw" or (o.idx - x.idx) > 2:
                                continue
                        wait(prog.csem[x.eng], x.count)
                    elif x.kind in ("d", "cc"):
                        wait(x.sem, x.target)
                if o.fn is None:
                    continue
                ins = o.fn(eng)
                if o.kind == "c":
                    if o.needed:
                        ins.then_inc(prog.csem[E], 1)
                elif o.kind == "d":
                    ins.then_inc(o.sem, 16)
                elif o.kind == "cc":
                    ins.then_inc(o.sem)

        with nc.Block() as block:
            @block.tensor
            def _(e):
                run("pe", e)

            @block.scalar
            def _(e):
                run("act", e)

            @block.vector
            def _(e):
                run("dve", e)

            @block.gpsimd
            def _(e):
                run("pool", e)

            @block.sync
            def _(e):
                run("sp", e)


def MM(out, lhsT, rhs, start=True, stop=True):
    return lambda e: e.matmul(out, lhsT, rhs, start=start, stop=stop)


def ACT(out, in_, func, bias=0.0, scale=1.0):
    return lambda e: e.activation(out, in_, func, bias=bias, scale=scale)


def TT(out, a, b, op):
    return lambda e: e.tensor_tensor(out, a, b, op)


def TS(out, a, s1, s2, op0, op1=None):
    if op1 is None:
        return lambda e: e.tensor_scalar(out, a, s1, None, op0)
    return lambda e: e.tensor_scalar(out, a, s1, s2, op0, op1)


def STT(out, a, s, b, op0, op1):
    return lambda e: e.scalar_tensor_tensor(out, a, s, b, op0, op1)


def CP(out, a):
    return lambda e: e.tensor_copy(out, a)


def RCP(out, a):
    return lambda e: e.reciprocal(out, a)


def MS(ap, v):
    return lambda e: e.memset(ap, v)


def DMA(out, in_, **kw):
    return lambda e: e.dma_start(out=out, in_=in_, **kw)


class Ring:
    def __init__(self, aps):
        self.aps = aps
        self.res = [Res() for _ in aps]
        self.i = 0

    def next(self):
        k = self.i % len(self.aps)
        self.i += 1
        return self.aps[k], self.res[k]


def build(TPC, NL=DEPTH):
    S = 4 * TPC
    NB = TPC // 512
    NKT = (CTX + S) // 128
    NLT = TPC // 128
    nc = bass.Bass("TRN2", target_bir_lowering=False)
    es = ExitStack()

    def din(name, shape, dt=F32):
        return nc.dram_tensor(name, list(shape), dt, kind="ExternalInput")

    def dscr(name, shape, dt):
        return nc.dram_tensor(name, list(shape), dt)

    xT_in = din("xT", [D, TPC])
    cT_in = din("cT", [D, CTX])
    cvec_in = din("cvec", [128, NCH, 2])
    adaw_in = din("ada_w", [DEPTH, D, 6 * D])
    adab_in = din("ada_b", [128, DEPTH, 48])
    mixn_in = din("mixn", [128, DEPTH, NCH])
    ffnn_in = din("ffnn", [128, DEPTH, NCH])
    finn_in = din("finn", [128, NCH])
    wqk_in = din("wqk", [2, D, 1664])
    wqs_in = din("wqs", [2, D, 1664])
    wv_in = din("wv", [2, D, 640])
    ggain_in = din("ggain", [128, 2, 5, 2])
    wod_in = din("wod", [2, 512, D])
    wog_in = din("wog", [2, 64, 8, D])
    lqk_in = din("lqk", [1, 2, 4, 64])
    subln_in = din("subln", [128, 2])
    ropeC_in = din("ropeC", [128, TPC])
    ropeS_in = din("ropeS", [128, TPC])
    swu_in = din("swu", [2, D, D])
    swv_in = din("swv", [2, D, D])
    svn_in = din("svn", [128, 2, NCH])
    swsT_in = din("swsT", [2, 128, 4, 128])
    sbsb_in = din("sbsb", [128, 2, 4, 128])
    swo_in = din("swo", [2, D, D])
    wup_in = din("wup", [DEPTH, D, NPAIR * 256])
    convp_in = din("convp", [128, DEPTH, 44, 4])
    wdn_in = din("wdn", [DEPTH, FF, D])
    flags_in = din("flags", [128, 2])
    sel_in = din("sel", [8, 2])
    yT_out = nc.dram_tensor("yT", [D, TPC], F32, kind="ExternalOutput")

    xw0 = dscr("xw0", [D, TPC + 2], F32)
    xw1 = dscr("xw1", [D, TPC + 2], F32)
    cw = dscr("cw", [D, CTX], F32)
    wqk_b = dscr("wqk_b", [2, D, 1664], BF16)
    wqs_b = dscr("wqs_b", [2, D, 1664], BF16)
    wv_b = dscr("wv_b", [2, D, 640], BF16)
    wod_b = dscr("wod_b", [2, 512, D], BF16)
    wog_b = dscr("wog_b", [2, 64, 8, D], BF16)
    swu_b = dscr("swu_b", [2, D, D], BF16)
    swv_b = dscr("swv_b", [2, D, D], BF16)
    swsT_b = dscr("swsT_b", [2, 128, 4, 128], BF16)
    swo_b = dscr("swo_b", [2, D, D], BF16)
    wup_b = dscr("wup_b", [DEPTH, D, NPAIR * 256], BF16)
    wdn_b = dscr("wdn_b", [DEPTH, FF, D], BF16)
    Qs = dscr("Qs", [8, 128, TPC], BF16)
    Qc = dscr("Qc", [8, 128, CTX], BF16)
    Klp = [dscr(f"Kl{p}", [64, TPC], BF16) for p in range(10)]
    Vlp = [dscr(f"Vl{p}", [64, TPC], BF16) for p in range(10)]
    Kap = [dscr(f"Ka{p}", [4 * 64, TPC], BF16) for p in range(10)]
    Vap = [dscr(f"Va{p}", [4 * 64, TPC], BF16) for p in range(10)]
    Kc = dscr("Kc", [640, CTX], BF16)
    Vc = dscr("Vc", [640, CTX], BF16)
    AT = dscr("AT", [12, 128, TPC], BF16)
    ATc = dscr("ATc", [12, 128, CTX], BF16)
    HLl = dscr("HLl", [2, D], F32)
    HLa = dscr("HLa", [8, D], F32)

    P = Prog(nc, es)

    R_xws = [[Res() for _ in range(NB)] for _ in range(2)]
    R_xhs = [Res(), Res()]
    cur = {"i": 0}
    R_cw = Res()
    R_w = {}

    uid = [0]

    def un(name):
        uid[0] += 1
        return f"{name}_u{uid[0]}"

    def sb(name, shape, dt):
        return es.enter_context(nc.sbuf_tensor(un(name), list(shape), dt))

    ones_bf = sb("ones_bf", [128, 128], BF16)
    bd_bf = sb("bd_bf", [128, 128], BF16)
    ones_f = sb("ones_f", [128, 128], F32)
    cvec = sb("cvec_s", [128, NCH, 2], F32)
    modraw = sb("modraw", [128, DEPTH, 48, 2], F32)
    adab = sb("adab_s", [128, DEPTH, 48], F32)
    mixn = sb("mixn_s", [128, DEPTH, NCH], F32)
    ffnn = sb("ffnn_s", [128, DEPTH, NCH], F32)
    finn = sb("finn_s", [128, NCH], F32)
    modv = sb("modv", [128, DEPTH, 6, 2, NCH], F32)
    ggain = sb("ggain_s", [128, 2, 5, 2], F32)
    subln = sb("subln_s", [128, 2], F32)
    lqk = sb("lqk_s", [1, 2, 4, 64], F32)
    lamt = sb("lamt", [1, 2, 8], F32)
    svn = sb("svn_s", [128, 2, NCH], F32)
    sbsb = sb("sbsb_s", [128, 2, 4, 128], F32)
    convp = sb("convp_s", [128, DEPTH, 44, 4], F32)
    flags = sb("flags_s", [128, 2], F32)
    sel = sb("sel_s", [8, 2], F32)
    R_const = Res()

    consts = [(cvec, cvec_in), (adab, adab_in), (mixn, mixn_in), (ffnn, ffnn_in), (finn, finn_in),
              (ggain, ggain_in), (subln, subln_in), (lqk, lqk_in), (svn, svn_in), (sbsb, sbsb_in),
              (convp, convp_in), (flags, flags_in), (sel, sel_in)]
    R_cs = []
    for (t, src) in consts:
        r_ = Res()
        R_cs.append(r_)
        P.dma("sp", DMA(t[:], src.ap()), w=[r_])
    R_ones = Res()
    P.op("pool", MS(ones_bf[:], 1.0), w=[R_ones])
    P.op("pool", MS(ones_f[:], 1.0), w=[R_ones])
    P.op("pool", MS(bd_bf[:], 0.0), w=[R_ones])
    P.op("pool", MS(bd_bf[0:64, 0:64], 1.0), w=[R_ones])
    P.op("pool", MS(bd_bf[64:128, 64:128], 1.0), w=[R_ones])

    def conv_w(key, src_ap, dst_ap, n_el):
        r_ = Res()
        R_w[key] = r_
        rows = n_el // 1024
        s2 = src_ap
        d2 = dst_ap
        step = 8192
        for r0 in range(0, rows, step):
            r1 = min(rows, r0 + step)
            P.dma("pool", DMA(d2[r0:r1, :], s2[r0:r1, :]), w=[r_])

    def flat2(h, idx, n_el):
        ap = h.ap()[idx]
        names = " ".join(f"d{i}" for i in range(len(ap.shape)))
        ap = ap.rearrange(f"{names} -> ({names})")
        return ap.rearrange("(r c) -> r c", c=1024)

    def conv_mixer_weights(l):
        if l >= NL:
            return
        i = l // 2
        if l % 2 == 0:
            for (nm, src, dst, n) in (("wqk", wqk_in, wqk_b, D * 1664), ("wqs", wqs_in, wqs_b, D * 1664),
                                      ("wv", wv_in, wv_b, D * 640), ("wod", wod_in, wod_b, 512 * D),
                                      ("wog", wog_in, wog_b, 64 * 8 * D)):
                conv_w((nm, i), flat2(src, i, n), flat2(dst, i, n), n)
        else:
            for (nm, src, dst, n) in (("swu", swu_in, swu_b, D * D), ("swv", swv_in, swv_b, D * D),
                                      ("swsT", swsT_in, swsT_b, 128 * 4 * 128), ("swo", swo_in, swo_b, D * D)):
                conv_w((nm, i), flat2(src, i, n), flat2(dst, i, n), n)

    def conv_ffn_weights(l):
        if l >= NL:
            return
        conv_w(("wup", l), flat2(wup_in, l, D * NPAIR * 256), flat2(wup_b, l, D * NPAIR * 256), D * NPAIR * 256)
        conv_w(("wdn", l), flat2(wdn_in, l, FF * D), flat2(wdn_b, l, FF * D), FF * D)

    conv_mixer_weights(0)

    xw_fms = [xw0.ap().rearrange("(c p) t -> p c t", p=128), xw1.ap().rearrange("(c p) t -> p c t", p=128)]
    cw_fm = cw.ap().rearrange("(c p) t -> p c t", p=128)
    for b in range(NB):
        P.dma("sp", DMA(xw0.ap()[:, 1 + b * 512:1 + (b + 1) * 512], xT_in.ap()[:, b * 512:(b + 1) * 512]),
              w=[R_xws[0][b]])
    P.dma("sp", DMA(cw.ap(), cT_in.ap()), w=[R_cw])

    with ExitStack() as ph:
        def sbp(name, shape, dt):
            return ph.enter_context(nc.sbuf_tensor(un(name), list(shape), dt))

        def psp(name, shape, dt=F32):
            return ph.enter_context(nc.psum_tensor(un(name), list(shape), dt))

        scv = sbp("scv", [128, NCH, 2], F32)
        R_scv = Res()
        P.op("act", ACT(scv[:], cvec[:], AF.Silu), r=[R_cs[0]], w=[R_scv])
        wst = Ring([sbp(f"adaw{i}", [128, NCH, 768], F32) for i in range(2)])
        pm = Ring([psp(f"pm{i}", [128, 512]) for i in range(2)])
        for l in range(NL):
            for cg in range(8):
                wt, wr = wst.next()
                P.dma("sp", DMA(wt[:], adaw_in.ap()[l].rearrange("(k p) n -> p k n", p=128)[:, :, cg * 768:(cg + 1) * 768]),
                      w=[wr])
                pt, pr = pm.next()
                for j in range(6):
                    for k in range(NCH):
                        P.op("pe", MM(pt[:, 2 * j:2 * j + 2], wt[:, k, j * 128:(j + 1) * 128], scv[:, k, :],
                                      start=(k == 0), stop=(k == NCH - 1)), r=[wr, R_scv], w=[pr])
                for s_ in range(2):
                    P.op("dve", TT(modraw[:, l, cg * 6:(cg + 1) * 6, s_], pt[:, s_:12:2],
                                   adab[:, l, cg * 6:(cg + 1) * 6], ALU.add), r=[pr, R_cs[1]], w=[R_const])
        for l in range(NL):
            for s_ in range(2):
                mr = lambda m: modraw[:, l, m * 8:(m + 1) * 8, s_]
                P.op("dve", STT(modv[:, l, 0, s_, :], mr(1), 1.0, mixn[:, l, :], ALU.add, ALU.mult),
                     r=[R_const, R_cs[2]], w=[R_const])
                P.op("dve", CP(modv[:, l, 1, s_, :], mr(0)), r=[R_const], w=[R_const])
                P.op("dve", CP(modv[:, l, 2, s_, :], mr(2)), r=[R_const], w=[R_const])
                P.op("dve", STT(modv[:, l, 3, s_, :], mr(4), 1.0, ffnn[:, l, :], ALU.add, ALU.mult),
                     r=[R_const, R_cs[3]], w=[R_const])
                P.op("dve", CP(modv[:, l, 4, s_, :], mr(3)), r=[R_const], w=[R_const])
                P.op("dve", CP(modv[:, l, 5, s_, :], mr(5)), r=[R_const], w=[R_const])
        P.barrier()

    def norm_mod(env, xt, rx, N, A, Bv, h, rh, flag=None):
        pt, pr = env["ps"].next()
        for c in range(NCH):
            sq, rs = env["sq"].next()
            P.op("act", ACT(sq[:, :N], xt[:, c, :N], AF.Square), r=[rx], w=[rs])
            P.op("pe", MM(pt[:, :N], ones_bf[:], sq[:, :N], start=(c == 0), stop=(c == NCH - 1)),
                 r=[rs, R_ones], w=[pr])
        rt, rr = env["rstd"].next()
        P.op("act", ACT(rt[:, :N], pt[:, :N], AF.Sqrt, bias=env["eps"][:, 0:1], scale=1.0 / D), r=[pr, env["reps"]], w=[rr])
        P.op("dve", RCP(rt[:, :N], rt[:, :N]), r=[rr], w=[rr])
        for c in range(NCH):
            tt, tr = env["tmp"].next()
            P.op("dve", TT(tt[:, :N], xt[:, c, :N], rt[:, :N], ALU.mult), r=[rx, rr], w=[tr])
            if Bv is not None:
                P.op("act", ACT(h[:, c, :N], tt[:, :N], AF.Identity, bias=Bv[:, c:c + 1], scale=A[:, c:c + 1]),
                     r=[tr, R_const], w=[rh])
            else:
                P.op("act", ACT(h[:, c, :N], tt[:, :N], AF.Identity, scale=A[:, c:c + 1]), r=[tr, R_const], w=[rh])
            if flag is not None:
                P.op("dve", TS(h[:, c, :N], h[:, c, :N], flag, None, ALU.mult), r=[rh, R_cs[11]], w=[rh])

    epsT = sb("epsT", [128, 1], F32)
    R_eps = Res()
    P.op("pool", MS(epsT[:], EPS), w=[R_eps])

    def make_env(ph, tag, ps_n=1):
        def sbp(name, shape, dt):
            return ph.enter_context(nc.sbuf_tensor(un(name), list(shape), dt))
        return {
            "sq": Ring([sbp(f"{tag}_sq{i}", [128, 512], BF16) for i in range(3)]),
            "tmp": Ring([sbp(f"{tag}_tmp{i}", [128, 512], F32) for i in range(3)]),
            "rstd": Ring([sbp(f"{tag}_rstd{i}", [128, 512], F32) for i in range(2)]),
            "ps": Ring([ph.enter_context(nc.psum_tensor(un(f"{tag}_nps{i}"), [128, 512], F32)) for i in range(ps_n)]),
            "eps": epsT, "reps": R_eps,
        }

    def attention_layer(l):
        i = l // 2
        xw_fm, R_xw = xw_fms[cur["i"]], R_xws[cur["i"]]
        with_ctx_out = l < 2
        lam_init = 0.8 - 0.6 * math.exp(-0.3 * l)
        R_Qs = [[Res() for _ in range(NB)] for _ in range(8)]
        R_Qc = [Res() for _ in range(8)]
        R_Kl = [Res() for _ in range(10)]
        R_Vl = [Res() for _ in range(10)]
        R_Ka = [Res() for _ in range(10)]
        R_Va = [Res() for _ in range(10)]
        R_Kc, R_Vc = Res(), Res()
        R_AT = [[Res() for _ in range(NB)] for _ in range(12)]
        R_ATc = [Res() for _ in range(12)]

        R_lam = Res()
        with ExitStack() as ph:
            prod = ph.enter_context(nc.sbuf_tensor(un("lprod"), [1, 2, 64], F32))
            rp = Res()
            P.op("dve", TT(prod[:, 0, :], lqk[:, i, 0, :], lqk[:, i, 1, :], ALU.mult), r=[R_cs[7]], w=[rp])
            P.op("dve", TT(prod[:, 1, :], lqk[:, i, 2, :], lqk[:, i, 3, :], ALU.mult), r=[R_cs[7]], w=[rp])
            P.op("dve", lambda e: e.reduce_sum(lamt[:, i, 0:2], prod[:], mybir.AxisListType.X), r=[rp], w=[R_lam])
            P.op("act", ACT(lamt[:, i, 2:4], lamt[:, i, 0:2], AF.Exp), r=[R_lam], w=[R_lam])
            P.op("dve", TT(lamt[:, i, 4:5], lamt[:, i, 3:4], lamt[:, i, 2:3], ALU.subtract), r=[R_lam], w=[R_lam])
            P.op("dve", TS(lamt[:, i, 5:6], lamt[:, i, 4:5], -lam_init, None, ALU.add), r=[R_lam], w=[R_lam])
            P.barrier()
        nlam = lamt[0:1, i, 5:6]

        with ExitStack() as ph:
            def sbp(name, shape, dt):
                return ph.enter_context(nc.sbuf_tensor(un(name), list(shape), dt))

            def psp(name, shape, dt=F32):
                return ph.enter_context(nc.psum_tensor(un(name), list(shape), dt))

            env = make_env(ph, f"a1_{l}")
            wqk = sbp("wqk_s", [128, NCH, 1664], BF16)
            wqs = sbp("wqs_s", [128, NCH, 1664], BF16)
            wv = sbp("wv_s", [128, NCH, 640], BF16)
            R_wq, R_wqs_, R_wv_ = Res(), Res(), Res()
            P.dma("sp", DMA(wqk[:], wqk_b.ap()[i].rearrange("(k p) n -> p k n", p=128)), r=[R_w[("wqk", i)]], w=[R_wq])
            P.dma("sp", DMA(wqs[:], wqs_b.ap()[i].rearrange("(k p) n -> p k n", p=128)), r=[R_w[("wqs", i)]], w=[R_wqs_])
            P.dma("sp", DMA(wv[:], wv_b.ap()[i].rearrange("(k p) n -> p k n", p=128)), r=[R_w[("wv", i)]], w=[R_wv_])
            xring = Ring([sbp(f"a1x{j}", [128, NCH, 512], F32) for j in range(2)])
            hring = Ring([sbp(f"a1h{j}", [128, NCH, 512], BF16) for j in range(2)])
            ropeC = Ring([sbp(f"rC{j}", [128, 512], F32) for j in range(2)])
            ropeS = Ring([sbp(f"rS{j}", [128, 512], F32) for j in range(2)])
            pA = Ring([psp(f"pA{j}", [128, 512]) for j in range(2)])
            pB = Ring([psp(f"pB{j}", [128, 512]) for j in range(2)])
            pG = Ring([psp("pG0", [128, 512])])
            pV = Ring([psp(f"pV{j}", [128, 640]) for j in range(1)])
            t1r = Ring([sbp(f"t1_{j}", [128, 512], F32) for j in range(2)])
            t2r = Ring([sbp(f"t2_{j}", [128, 512], F32) for j in range(2)])
            gsq = Ring([sbp(f"gsq{j}", [128, 512], BF16) for j in range(2)])
            grs = Ring([sbp(f"grs{j}", [128, 512], F32) for j in range(2)])
            outr = Ring([sbp(f"qko{j}", [128, 512], BF16) for j in range(3)])
            vrow = Ring([sbp(f"vrow{j}", [128, 640], BF16) for j in range(2)])

            def project(N, src_fm, rsrc, col0, stream, b_idx):
                lat = stream == 0
                xt, rx = xring.next()
                P.dma("sp", DMA(xt[:, :, :N], src_fm[:, :, col0:col0 + N]), r=[rsrc], w=[rx])
                ht, rh = hring.next()
                norm_mod(env, xt, rx, N, modv[:, l, 0, stream, :], modv[:, l, 1, stream, :], ht, rh)
                if lat:
                    rc, rrc = ropeC.next()
                    rs_, rrs = ropeS.next()
                    P.dma("sp", DMA(rc[:], ropeC_in.ap()[:, b_idx * 512:(b_idx + 1) * 512]), w=[rrc])
                    P.dma("sp", DMA(rs_[:], ropeS_in.ap()[:, b_idx * 512:(b_idx + 1) * 512]), w=[rrs])
                for t in range(13):
                    is_q = t < 4 or 8 <= t < 12
                    if is_q and (not lat) and (not with_ctx_out):
                        continue
                    gq = t >= 8
                    pa, rpa = pA.next()
                    for k in range(NCH):
                        P.op("pe", MM(pa[:, :N], wqk[:, k, t * 128:(t + 1) * 128], ht[:, k, :N],
                                      start=(k == 0), stop=(k == NCH - 1)), r=[R_wq, rh], w=[rpa])
                    if lat:
                        pb, rpb = pB.next()
                        for k in range(NCH):
                            P.op("pe", MM(pb[:, :N], wqs[:, k, t * 128:(t + 1) * 128], ht[:, k, :N],
                                          start=(k == 0), stop=(k == NCH - 1)), r=[R_wqs_, rh], w=[rpb])
                    if gq:
                        g_idx = t - 8
                        sqt, rsq = gsq.next()
                        P.op("act", ACT(sqt[:, :N], pa[:, :N], AF.Square), r=[rpa], w=[rsq])
                        pg, rpg = pG.next()
                        P.op("pe", MM(pg[:, :N], bd_bf[:], sqt[:, :N]), r=[rsq, R_ones], w=[rpg])
                        rst, rrst = grs.next()
                        P.op("act", ACT(rst[:, :N], pg[:, :N], AF.Sqrt, bias=epsT[:, 0:1], scale=1.0 / 64), r=[rpg, R_eps], w=[rrst])
                        P.op("dve", RCP(rst[:, :N], rst[:, :N]), r=[rrst], w=[rrst])
                    ot, rot = outr.next()
                    if lat:
                        t1, rt1 = t1r.next()
                        t2, rt2 = t2r.next()
                        if gq:
                            P.op("dve", STT(t1[:, :N], pa[:, :N], ggain[:, i, g_idx, 0:1], rc[:, :N], ALU.mult, ALU.mult),
                                 r=[rpa, rrc, R_cs[5]], w=[rt1])
                            P.op("dve", STT(t2[:, :N], pb[:, :N], ggain[:, i, g_idx, 1:2], rs_[:, :N], ALU.mult, ALU.mult),
                                 r=[rpb, rrs, R_cs[5]], w=[rt2])
                            P.op("pool", TT(t1[:, :N], t1[:, :N], t2[:, :N], ALU.add), r=[rt1, rt2], w=[rt1])
                            P.op("dve", TT(ot[:, :N], t1[:, :N], rst[:, :N], ALU.mult), r=[rt1, rrst], w=[rot])
                        else:
                            P.op("dve", TT(t1[:, :N], pa[:, :N], rc[:, :N], ALU.mult), r=[rpa, rrc], w=[rt1])
                            P.op("dve", TT(t2[:, :N], pb[:, :N], rs_[:, :N], ALU.mult), r=[rpb, rrs], w=[rt2])
                            P.op("pool", TT(ot[:, :N], t1[:, :N], t2[:, :N], ALU.add), r=[rt1, rt2], w=[rot])
                    else:
                        if gq:
                            P.op("dve", STT(ot[:, :N], pa[:, :N], ggain[:, i, g_idx, 0:1], rst[:, :N], ALU.mult, ALU.mult),
                                 r=[rpa, rrst, R_cs[5]], w=[rot])
                        else:
                            P.op("act", ACT(ot[:, :N], pa[:, :N], AF.Identity), r=[rpa], w=[rot])
                    if is_q:
                        qi = t if t < 4 else t - 4
                        if lat:
                            P.dma("sp", DMA(Qs.ap()[qi, :, col0 - 1:col0 - 1 + N], ot[:, :N]), r=[rot], w=[R_Qs[qi][b_idx]])
                        else:
                            P.dma("sp", DMA(Qc.ap()[qi], ot[:, :N]), r=[rot], w=[R_Qc[qi]])
                    else:
                        ki = t - 4 if t < 8 else 4
                        if lat:
                            for hf in range(2):
                                P.dma("sp", DMA(Klp[2 * ki + hf].ap()[:, col0 - 1:col0 - 1 + N], ot[hf * 64:(hf + 1) * 64, :N]),
                                      r=[rot], w=[R_Kl[2 * ki + hf]])
                        else:
                            P.dma("sp", DMA(Kc.ap()[ki * 128:(ki + 1) * 128, :], ot[:, :N]), r=[rot], w=[R_Kc])
                for tt_ in range(N // 128):
                    pv, rpv = pV.next()
                    for k in range(NCH):
                        P.op("pe", MM(pv[:, 0:512], ht[:, k, tt_ * 128:(tt_ + 1) * 128], wv[:, k, 0:512],
                                      start=(k == 0), stop=(k == NCH - 1)), r=[R_wv_, rh], w=[rpv])
                    for k in range(NCH):
                        P.op("pe", MM(pv[:, 512:640], ht[:, k, tt_ * 128:(tt_ + 1) * 128], wv[:, k, 512:640],
                                      start=(k == 0), stop=(k == NCH - 1)), r=[R_wv_, rh], w=[rpv])
                    vr, rvr = vrow.next()
                    P.op("act", ACT(vr[:], pv[:], AF.Identity), r=[rpv], w=[rvr])
                    if lat:
                        tile_idx = (col0 - 1) // 128 + tt_
                        for gi_ in range(5):
                            for hf in range(2):
                                P.dma("sp", DMA(Vlp[2 * gi_ + hf].ap()[:, tile_idx * 128:(tile_idx + 1) * 128],
                                                vr[hf * 64:(hf + 1) * 64, gi_ * 128:(gi_ + 1) * 128]), r=[rvr], w=[R_Vl[2 * gi_ + hf]])
                    else:
                        dst = Vc.ap().rearrange("(g p) (t c) -> p g t c", p=128, c=128)[:, :, tt_, :]
                        P.dma("sp", DMA(dst, vr[:].rearrange("p (g c) -> p g c", c=128)), r=[rvr], w=[R_Vc])

            project(CTX, cw_fm, R_cw, 0, 1, 0)
            for b in range(NB):
                project(512, xw_fm, R_xw[b], 1 + b * 512, 0, b)
            P.barrier()

        groups = [[0, 1, 2, 3], [4, 5, 6, 7]]
        for p_ in range(10):
            P.cc(lambda e, p_=p_: e.collective_compute("AllGather", ALU.bypass, replica_groups=groups,
                                                       ins=[Klp[p_].ap().opt()], outs=[Kap[p_].ap().opt()]),
                 r=[R_Kl[p_]], w=[R_Ka[p_]])
            P.cc(lambda e, p_=p_: e.collective_compute("AllGather", ALU.bypass, replica_groups=groups,
                                                       ins=[Vlp[p_].ap().opt()], outs=[Vap[p_].ap().opt()]),
                 r=[R_Vl[p_]], w=[R_Va[p_]])

        with ExitStack() as ph:
            def sbp(name, shape, dt):
                return ph.enter_context(nc.sbuf_tensor(un(name), list(shape), dt))

            def psp(name, shape, dt=F32):
                return ph.enter_context(nc.psum_tensor(un(name), list(shape), dt))

            KB = [sbp(f"KB{j}", [128, CTX + S], BF16) for j in range(2)]
            VB = [sbp(f"VB{j}", [128, NKT * 130], BF16) for j in range(2)]
            R_KB = [[Res() for _ in range(9)] for _ in range(2)]
            R_VB = [[Res() for _ in range(9)] for _ in range(2)]
            qring = Ring([sbp(f"qt{j}", [128, 512], BF16) for j in range(2)])
            pring = Ring([sbp(f"pt{j}", [128, 512], BF16) for j in range(8)])
            accr = Ring([sbp(f"acc{j}", [128, 512], F32) for j in range(4)])
            psS = Ring([psp(f"psS{j}", [128, 512]) for j in range(4)])
            psO = Ring([psp(f"psO{j}", [128, 512]) for j in range(3)])
            psE = Ring([psp("psE0", [128, 512])])
            osb = Ring([sbp(f"osb{j}", [128, 512], F32) for j in range(4)])
            lsb = Ring([sbp(f"lsb{j}", [128, 512], F32) for j in range(2)])
            bcs = Ring([sbp(f"bcs{j}", [128, 512], F32) for j in range(2)])
            ework = Ring([sbp(f"ew{j}", [128, 512], F32) for j in range(3)])
            esq = Ring([sbp("esq0", [128, 512], BF16)])
            aout = Ring([sbp(f"aout{j}", [128, 512], BF16) for j in range(2)])

            def load_kv(g, slot):
                kb, vb = KB[slot], VB[slot]
                rk, rv = R_KB[slot], R_VB[slot]
                P.dma("sp", DMA(kb[:, 0:CTX], Kc.ap()[g * 128:(g + 1) * 128, :]), r=[R_Kc], w=[rk[0]])
                for r_ in range(4):
                    for hf in range(2):
                        P.dma("sp", DMA(kb[hf * 64:(hf + 1) * 64, CTX + r_ * TPC:CTX + (r_ + 1) * TPC],
                                        Kap[2 * g + hf].ap()[r_ * 64:(r_ + 1) * 64, :]), r=[R_Ka[2 * g + hf]], w=[rk[1 + 2 * r_ + hf]])
                if g < 4:
                    v3 = vb[:, 0:NKT * 128].rearrange("p (t c) -> p t c", c=128)
                    P.dma("sp", DMA(v3[:, 0:2, :], Vc.ap()[g * 128:(g + 1) * 128, :].rearrange("p (t c) -> p t c", c=128)),
                          r=[R_Vc], w=[rv[0]])
                    for r_ in range(4):
                        for hf in range(2):
                            P.dma("sp", DMA(v3[hf * 64:(hf + 1) * 64, 2 + r_ * NLT:2 + (r_ + 1) * NLT, :],
                                            Vap[2 * g + hf].ap()[r_ * 64:(r_ + 1) * 64, :].rearrange("p (t c) -> p t c", c=128)),
                                  r=[R_Va[2 * g + hf]], w=[rv[1 + 2 * r_ + hf]])
                else:
                    v3 = vb[:].rearrange("p (t c) -> p t c", c=130)
                    for hh in range(2):
                        P.dma("sp", DMA(v3[:, 0:2, hh * 65:hh * 65 + 64],
                                        Vc.ap()[g * 128:(g + 1) * 128, :].rearrange("p (t c) -> p t c", c=128)[:, :, hh * 64:(hh + 1) * 64]),
                              r=[R_Vc], w=[rv[0]])
                        for r_ in range(4):
                            for hf in range(2):
                                P.dma("sp", DMA(v3[hf * 64:(hf + 1) * 64, 2 + r_ * NLT:2 + (r_ + 1) * NLT, hh * 65:hh * 65 + 64],
                                                Vap[2 * g + hf].ap()[r_ * 64:(r_ + 1) * 64, :].rearrange("p (t c) -> p t c", c=128)[:, :, hh * 64:(hh + 1) * 64]),
                                      r=[R_Va[2 * g + hf]], w=[rv[1 + 2 * r_ + hf]])
                    for hh in range(2):
                        P.op("pool", MS(v3[:, :, hh * 65 + 64:hh * 65 + 65], 1.0), r=[], w=rv)

            def unit_pair(g, slot, qt, rq, N, nkt):
                kb, vb = KB[slot], VB[slot]
                rk, rv = R_KB[slot], R_VB[slot]
                diff = g < 4
                if diff:
                    v3 = vb[:, 0:NKT * 128].rearrange("p (t c) -> p t c", c=128)
                    cd = [(0, 128), (0, 128)]
                else:
                    v3 = vb[:].rearrange("p (t c) -> p t c", c=130)
                    cd = [(0, 65), (65, 65)]
                po = [psO.next() for _ in range(2)]
                acc = [accr.next() for _ in range(2)] if diff else None
                ps_list = {}

                def issue_S(kt):
                    for m in range(2):
                        st, rst_ = psS.next()
                        ps_list[(kt, m)] = (st, rst_)
                        P.op("pe", MM(st[:, :N], kb[m * 64:(m + 1) * 64, kt * 128:(kt + 1) * 128],
                                      qt[m * 64:(m + 1) * 64, :N]), r=rk + [rq], w=[rst_])

                for kt in range(min(2, nkt)):
                    issue_S(kt)
                for kt in range(nkt):
                    pts = []
                    for m in range(2):
                        st, rst_ = ps_list.pop((kt, m))
                        pt_, rpt = pring.next()
                        P.op("act", ACT(pt_[:, :N], st[:, :N], AF.Exp, scale=0.125), r=[rst_], w=[rpt])
                        pts.append((pt_, rpt))
                    for m in range(2):
                        pt_, rpt = pts[m]
                        c0, dvp = cd[m]
                        P.op("pe", MM(po[m][0][0:dvp, :N], v3[:, kt, c0:c0 + dvp], pt_[:, :N],
                                      start=(kt == 0), stop=(kt == nkt - 1)), r=rv + [rpt], w=[po[m][1]])
                    if diff:
                        for m in range(2):
                            pt_, rpt = pts[m]
                            at_, rat_ = acc[m]
                            if kt == 0:
                                P.op("dve", CP(at_[:, :N], pt_[:, :N]), r=[rpt], w=[rat_])
                            else:
                                P.op("dve", TT(at_[:, :N], at_[:, :N], pt_[:, :N], ALU.add), r=[rpt, rat_], w=[rat_])
                    if kt + 2 < nkt:
                        issue_S(kt + 2)
                outs = []
                for m in range(2):
                    c0, dvp = cd[m]
                    ot, rot = osb.next()
                    P.op("dve", CP(ot[0:dvp, :N], po[m][0][0:dvp, :N]), r=[po[m][1]], w=[rot])
                    if diff:
                        at_, rat_ = acc[m]
                        pe_, rpe = psE.next()
                        P.op("pe", MM(pe_[0:1, :N], ones_f[:, 0:1], at_[:, :N]), r=[rat_, R_ones], w=[rpe])
                        lt, rlt = lsb.next()
                        P.op("dve", CP(lt[0:1, :N], pe_[0:1, :N]), r=[rpe], w=[rlt])
                        outs.append((ot, rot, lt, rlt))
                    else:
                        outs.append((ot, rot, None, None))
                return outs

            def bcast_row(row_ap, rrow, base, nparts, N):
                pe_, rpe = psE.next()
                P.op("pe", MM(pe_[0:nparts, :N], ones_f[base:base + 1, 0:nparts], row_ap), r=[rrow, R_ones], w=[rpe])
                bt, rbt = bcs.next()
                P.op("act", ACT(bt[0:nparts, :N], pe_[0:nparts, :N], AF.Identity), r=[rpe], w=[rbt])
                return bt, rbt

            def epilogue_gqa(ot, rot, N, dst_ap, rdst):
                P.op("dve", RCP(ot[64:65, :N], ot[64:65, :N]), r=[rot], w=[rot])
                bt, rbt = bcast_row(ot[64:65, :N], rot, 64, 64, N)
                at, rat = aout.next()
                P.op("dve", TT(at[0:64, :N], ot[0:64, :N], bt[0:64, :N], ALU.mult), r=[rot, rbt], w=[rat])
                P.dma("sp", DMA(dst_ap, at[0:64, :N]), r=[rat], w=[rdst])

            def epilogue_diff(o1, ro1, l1, rl1, o2, ro2, l2, rl2, N, dst_ap, rdst):
                P.op("dve", RCP(l1[0:1, :N], l1[0:1, :N]), r=[rl1], w=[rl1])
                P.op("dve", RCP(l2[0:1, :N], l2[0:1, :N]), r=[rl2], w=[rl2])
                P.op("dve", TS(l2[0:1, :N], l2[0:1, :N], nlam, None, ALU.mult), r=[rl2, R_lam], w=[rl2])
                b1, rb1 = bcast_row(l1[0:1, :N], rl1, 0, 128, N)
                w1, rw1 = ework.next()
                P.op("dve", TT(w1[:, :N], o1[:, :N], b1[:, :N], ALU.mult), r=[ro1, rb1], w=[rw1])
                b2, rb2 = bcast_row(l2[0:1, :N], rl2, 0, 128, N)
                w2, rw2 = ework.next()
                P.op("dve", TT(w2[:, :N], o2[:, :N], b2[:, :N], ALU.mult), r=[ro2, rb2], w=[rw2])
                P.op("dve", TT(w1[:, :N], w1[:, :N], w2[:, :N], ALU.add), r=[rw1, rw2], w=[rw1])
                sq, rsq = esq.next()
                P.op("act", ACT(sq[:, :N], w1[:, :N], AF.Square), r=[rw1], w=[rsq])
                pe_, rpe = psE.next()
                P.op("pe", MM(pe_[:, :N], ones_bf[:], sq[:, :N]), r=[rsq, R_ones], w=[rpe])
                w3, rw3 = ework.next()
                P.op("act", ACT(w3[:, :N], pe_[:, :N], AF.Sqrt, bias=epsT[:, 0:1], scale=1.0 / 128), r=[rpe, R_eps], w=[rw3])
                P.op("dve", RCP(w3[:, :N], w3[:, :N]), r=[rw3], w=[rw3])
                P.op("dve", TT(w1[:, :N], w1[:, :N], w3[:, :N], ALU.mult), r=[rw1, rw3], w=[rw1])
                at, rat = aout.next()
                P.op("dve", TS(at[:, :N], w1[:, :N], subln[:, i:i + 1], 1.0 - lam_init, ALU.mult, ALU.mult),
                     r=[rw1, R_cs[6]], w=[rat])
                P.dma("sp", DMA(dst_ap, at[:, :N]), r=[rat], w=[rdst])

            def run_group(g, slot, lat_chunks=True):
                qtiles = [g] if g < 4 else [4, 5, 6, 7]
                chunks = []
                if with_ctx_out:
                    chunks.append(("ctx", 0))
                chunks += [("lat", b) for b in range(NB)]
                for qi in qtiles:
                    for (kind, b) in chunks:
                        qt, rq = qring.next()
                        if kind == "ctx":
                            N, nkt = CTX, CTX // 128
                            P.dma("sp", DMA(qt[:, :N], Qc.ap()[qi]), r=[R_Qc[qi]], w=[rq])
                        else:
                            N, nkt = 512, NKT
                            P.dma("sp", DMA(qt[:, :N], Qs.ap()[qi, :, b * 512:(b + 1) * 512]), r=[R_Qs[qi][b]], w=[rq])
                        res = unit_pair(g, slot, qt, rq, N, nkt)
                        if g < 4:
                            if kind == "ctx":
                                dst, rdst = ATc.ap()[g], R_ATc[g]
                            else:
                                dst, rdst = AT.ap()[g, :, b * 512:(b + 1) * 512], R_AT[g][b]
                            epilogue_diff(res[0][0], res[0][1], res[0][2], res[0][3],
                                          res[1][0], res[1][1], res[1][2], res[1][3], N, dst, rdst)
                        else:
                            j = qi - 4
                            for m in range(2):
                                hidx = 4 + j + 4 * m
                                if kind == "ctx":
                                    dst, rdst = ATc.ap()[hidx, 0:64, :], R_ATc[hidx]
                                else:
                                    dst, rdst = AT.ap()[hidx, 0:64, b * 512:(b + 1) * 512], R_AT[hidx][b]
                                epilogue_gqa(res[m][0], res[m][1], N, dst, rdst)

            if l == 0:
                conv_ffn_weights(0)
                conv_mixer_weights(1)
                conv_ffn_weights(1)
                conv_mixer_weights(2)
            else:
                conv_ffn_weights(l)
                conv_mixer_weights(l + 1)
                conv_ffn_weights(l + 1)
            load_kv(0, 0)
            for g in range(5):
                if g + 1 < 5:
                    load_kv(g + 1, (g + 1) % 2)
                run_group(g, g % 2)
            P.barrier()

        with ExitStack() as ph:
            def sbp(name, shape, dt):
                return ph.enter_context(nc.sbuf_tensor(un(name), list(shape), dt))

            def psp(name, shape, dt=F32):
                return ph.enter_context(nc.psum_tensor(un(name), list(shape), dt))

            wod = sbp("wod_s", [128, 4, D], BF16)
            wog = sbp("wog_s", [64, 8, D], BF16)
            R_wod, R_wog = Res(), Res()
            P.dma("sp", DMA(wod[:], wod_b.ap()[i].rearrange("(k p) n -> p k n", p=128)), r=[R_w[("wod", i)]], w=[R_wod])
            P.dma("sp", DMA(wog[:], wog_b.ap()[i]), r=[R_w[("wog", i)]], w=[R_wog])
            atr = Ring([sbp(f"atb{j}", [128, 12, 512], BF16) for j in range(2)])
            xr = Ring([sbp(f"a4x{j}", [128, NCH, 512], F32) for j in range(2)])
            py = Ring([psp(f"py{j}", [128, 512]) for j in range(3)])

            def outproj(N, at_src, rat_list, x_fm, rx_dram, col0, stream):
                at_, rat_ = atr.next()
                P.dma("sp", DMA(at_[:, 0:4, :N], at_src[0:4].rearrange("a p t -> p a t")), r=rat_list[0:4], w=[rat_])
                P.dma("sp", DMA(at_[0:64, 4:12, :N], at_src[4:12, 0:64, :].rearrange("a p t -> p a t")), r=rat_list[4:12], w=[rat_])
                xt, rx = xr.next()
                P.dma("sp", DMA(xt[:, :, :N], x_fm[:, :, col0:col0 + N]), r=[rx_dram], w=[rx])
                for fo in range(NCH):
                    pt_, rp = py.next()
                    for a in range(4):
                        P.op("pe", MM(pt_[:, :N], wod[:, a, fo * 128:(fo + 1) * 128], at_[:, a, :N],
                                      start=(a == 0), stop=False), r=[R_wod, rat_], w=[rp])
                    for a in range(8):
                        P.op("pe", MM(pt_[:, :N], wog[:, a, fo * 128:(fo + 1) * 128], at_[0:64, 4 + a, :N],
                                      start=False, stop=(a == 7)), r=[R_wog, rat_], w=[rp])
                    P.op("dve", STT(xt[:, fo, :N], pt_[:, :N], modv[:, l, 2, stream, fo:fo + 1], xt[:, fo, :N], ALU.mult, ALU.add),
                         r=[rp, rx, R_const], w=[rx])
                P.dma("sp", DMA(x_fm[:, :, col0:col0 + N], xt[:, :, :N]), r=[rx], w=[rx_dram])

            if with_ctx_out:
                outproj(CTX, ATc.ap(), R_ATc, cw_fm, R_cw, 0, 1)
            for b in range(NB):
                outproj(512, AT.ap()[:, :, b * 512:(b + 1) * 512], [R_AT[a][b] for a in range(12)], xw_fm, R_xw[b], 1 + b * 512, 0)
            P.barrier()

    def sgu_layer(l):
        i = l // 2
        xw_fm, R_xw = xw_fms[cur["i"]], R_xws[cur["i"]]
        with_ctx = l < 2
        with ExitStack() as ph:
            def sbp(name, shape, dt):
                return ph.enter_context(nc.sbuf_tensor(un(name), list(shape), dt))

            def psp(name, shape, dt=F32):
                return ph.enter_context(nc.psum_tensor(un(name), list(shape), dt))

            env = make_env(ph, f"sg_{l}")
            wu = sbp("swu_s", [128, NCH, D], BF16)
            wvv = sbp("swv_s", [128, NCH, D], BF16)
            wo = sbp("swo_s", [128, NCH, D], BF16)
            wsT = sbp("swsT_s", [128, 4, 128], BF16)
            R_wu, R_wvv, R_wo, R_wsT = Res(), Res(), Res(), Res()
            P.dma("sp", DMA(wu[:], swu_b.ap()[i].rearrange("(k p) n -> p k n", p=128)), r=[R_w[("swu", i)]], w=[R_wu])
            P.dma("sp", DMA(wvv[:], swv_b.ap()[i].rearrange("(k p) n -> p k n", p=128)), r=[R_w[("swv", i)]], w=[R_wvv])
            P.dma("sp", DMA(wo[:], swo_b.ap()[i].rearrange("(k p) n -> p k n", p=128)), r=[R_w[("swo", i)]], w=[R_wo])
            P.dma("sp", DMA(wsT[:], swsT_b.ap()[i]), r=[R_w[("swsT", i)]], w=[R_wsT])
            xr = Ring([sbp(f"sgx{j}", [128, NCH, 512], F32) for j in range(2)])
            hr = Ring([sbp(f"sgh{j}", [128, NCH, 512], BF16) for j in range(2)])
            uT = Ring([sbp(f"sgu{j}", [128, NCH, 512], BF16) for j in range(1)])
            suT = Ring([sbp(f"sgsu{j}", [128, NCH, 512], BF16) for j in range(1)])
            vg = Ring([sbp(f"sgv{j}", [128, D], F32) for j in range(2)])
            vn = Ring([sbp(f"sgvn{j}", [128, D], BF16) for j in range(2)])
            vjunk = Ring([sbp("sgjunk", [128, D], BF16)])
            vss = Ring([sbp(f"sgss{j}", [128, 2], F32) for j in range(2)])
            mt = Ring([sbp(f"sgm{j}", [128, 128], F32) for j in range(3)])
            pU = Ring([psp(f"pU{j}", [128, 512]) for j in range(2)])
            pVv = Ring([psp(f"pVv{j}", [128, D]) for j in range(1)])
            pM = Ring([psp(f"pM{j}", [128, 512]) for j in range(2)])
            pY = Ring([psp(f"pY{j}", [128, 512]) for j in range(1)])

            def sgu_block(N, x_fm, rx_dram, col0, stream):
                xt, rx = xr.next()
                P.dma("sp", DMA(xt[:, :, :N], x_fm[:, :, col0:col0 + N]), r=[rx_dram], w=[rx])
                ht, rh = hr.next()
                norm_mod(env, xt, rx, N, modv[:, l, 0, stream, :], modv[:, l, 1, stream, :], ht, rh)
                ut, rut = uT.next()
                for fc in range(NCH):
                    pu, rpu = pU.next()
                    for k in range(NCH):
                        P.op("pe", MM(pu[:, :N], wu[:, k, fc * 128:(fc + 1) * 128], ht[:, k, :N],
                                      start=(k == 0), stop=(k == NCH - 1)), r=[R_wu, rh], w=[rpu])
                    P.op("act", ACT(ut[:, fc, :N], pu[:, :N], AF.Gelu), r=[rpu], w=[rut])
                sut, rsut = suT.next()
                for tt_ in range(N // 128):
                    pv, rpv = pVv.next()
                    for half in range(2):
                        for k in range(NCH):
                            P.op("pe", MM(pv[:, half * 512:(half + 1) * 512], ht[:, k, tt_ * 128:(tt_ + 1) * 128],
                                          wvv[:, k, half * 512:(half + 1) * 512], start=(k == 0), stop=(k == NCH - 1)),
                                 r=[R_wvv, rh], w=[rpv])
                    vt, rvt = vg.next()
                    P.op("act", ACT(vt[:], pv[:], AF.Gelu), r=[rpv], w=[rvt])
                    sst, rss = vss.next()
                    jk, rjk = vjunk.next()
                    P.op("dve", lambda e, jk=jk, vt=vt, sst=sst: e.tensor_tensor(jk[:], vt[:], vt[:], ALU.mult), r=[rvt], w=[rjk])
                    P.op("dve", lambda e, jk=jk, sst=sst: e.reduce_sum(sst[:, 0:1], jk[:], mybir.AxisListType.X), r=[rjk], w=[rss])
                    P.op("act", ACT(sst[:, 1:2], sst[:, 0:1], AF.Sqrt, bias=epsT[:, 0:1], scale=1.0 / D), r=[rss, R_eps], w=[rss])
                    P.op("dve", RCP(sst[:, 1:2], sst[:, 1:2]), r=[rss], w=[rss])
                    vnt, rvn = vn.next()
                    P.op("dve", TS(vnt[:], vt[:], sst[:, 1:2], None, ALU.mult), r=[rvt, rss], w=[rvn])
                    pm_, rpm = pM.next()
                    for gi in range(4):
                        for cc_ in range(2):
                            fc = gi * 2 + cc_
                            sl = pm_[:, (fc % 4) * 128:(fc % 4 + 1) * 128]
                            P.op("pe", MM(sl, vnt[:, fc * 128:(fc + 1) * 128], wsT[:, gi, :]), r=[rvn, R_wsT], w=[rpm])
                            mtt, rmt = mt.next()
                            P.op("dve", STT(mtt[:], sl, svn[:, i, fc:fc + 1], sbsb[:, i, gi, :], ALU.mult, ALU.add),
                                 r=[rpm, R_cs[8], R_cs[9]], w=[rmt])
                            P.op("pool", TT(sut[:, fc, tt_ * 128:(tt_ + 1) * 128], mtt[:], ut[:, fc, tt_ * 128:(tt_ + 1) * 128], ALU.mult),
                                 r=[rmt, rut], w=[rsut])
                            if fc == 3:
                                pm_, rpm = pM.next()
                for fo in range(NCH):
                    pt_, rp = pY.next()
                    for k in range(NCH):
                        P.op("pe", MM(pt_[:, :N], wo[:, k, fo * 128:(fo + 1) * 128], sut[:, k, :N],
                                      start=(k == 0), stop=(k == NCH - 1)), r=[R_wo, rsut], w=[rp])
                    P.op("dve", STT(xt[:, fo, :N], pt_[:, :N], modv[:, l, 2, stream, fo:fo + 1], xt[:, fo, :N], ALU.mult, ALU.add),
                         r=[rp, rx, R_const], w=[rx])
                P.dma("sp", DMA(x_fm[:, :, col0:col0 + N], xt[:, :, :N]), r=[rx], w=[rx_dram])

            if with_ctx:
                sgu_block(CTX, cw_fm, R_cw, 0, 1)
            for b in range(NB):
                sgu_block(512, xw_fm, R_xw[b], 1 + b * 512, 0)
            P.barrier()

    def halo_exchange(l):
        xw_fm, R_xw, R_xh = xw_fms[cur["i"]], R_xws[cur["i"]], R_xhs[cur["i"]]
        R_hl, R_ha = Res(), Res()
        with ExitStack() as ph:
            hs = ph.enter_context(nc.sbuf_tensor(un("hs"), [8, D], F32))
            ho = ph.enter_context(nc.sbuf_tensor(un("ho"), [128, 2, NCH, 1], F32))
            hp = ph.enter_context(nc.psum_tensor(un("hp"), [128, 2, NCH], F32))
            hb = ph.enter_context(nc.sbuf_tensor(un("hb"), [128, 2, NCH, 1], F32))
            rhs_, rho, rhp, rhb = Res(), Res(), Res(), Res()
            P.dma("sp", DMA(hb[:, 0, :, :], xw_fm[:, :, 1:2], allow_slow_non_contiguous=True), r=[R_xw[0]], w=[rhb])
            P.dma("sp", DMA(hb[:, 1, :, :], xw_fm[:, :, TPC:TPC + 1], allow_slow_non_contiguous=True), r=[R_xw[NB - 1]], w=[rhb])
            for s_ in range(2):
                P.dma("sp", DMA(HLl.ap()[s_].rearrange("(p c) -> p c", p=128), hb[:, s_, :, 0], allow_slow_non_contiguous=True), r=[rhb], w=[R_hl])
            P.cc(lambda e: e.collective_compute("AllGather", ALU.bypass, replica_groups=[[0, 1, 2, 3], [4, 5, 6, 7]],
                                                ins=[HLl.ap().opt()], outs=[HLa.ap().opt()]), r=[R_hl], w=[R_ha])
            P.dma("sp", DMA(hs[:], HLa.ap()), r=[R_ha], w=[rhs_])
            for s_ in range(2):
                for c in range(NCH):
                    P.op("pe", MM(hp[:, s_, c:c + 1], hs[:, c:D:NCH], sel[:, s_:s_ + 1]), r=[rhs_, R_cs[12]], w=[rhp])
            P.op("dve", CP(ho[:, :, :, 0], hp[:]), r=[rhp], w=[rho])
            P.dma("sp", DMA(xw_fm[:, :, 0:1], ho[:, 0, :, :], allow_slow_non_contiguous=True), r=[rho], w=[R_xh])
            P.dma("sp", DMA(xw_fm[:, :, TPC + 1:TPC + 2], ho[:, 1, :, :], allow_slow_non_contiguous=True), r=[rho], w=[R_xh])
            P.barrier()

    def ffn_layer(l):
        with_ctx = l < 2
        ci = cur["i"]
        xin_fm, R_xin, R_xh = xw_fms[ci], R_xws[ci], R_xhs[ci]
        xout_fm, R_xout = xw_fms[1 - ci], R_xws[1 - ci]
        with ExitStack() as ph:
            def sbp(name, shape, dt):
                return ph.enter_context(nc.sbuf_tensor(un(name), list(shape), dt))

            def psp(name, shape, dt=F32):
                return ph.enter_context(nc.psum_tensor(un(name), list(shape), dt))

            env = make_env(ph, f"ff_{l}")
            xt = sbp("ffx", [128, NCH, 1024], F32)
            xh = sbp("ffxh", [128, NCH, 2], F32)
            ht = sbp("ffh", [128, NCH, 1024], BF16)
            hh2 = sbp("ffhh", [128, NCH, 2], BF16)
            hzt = sbp("ffhz", [128, 88], F32)
            rx, rh, rxh, rhh, rhz = Res(), Res(), Res(), Res(), Res()
            wupr = Ring([sbp(f"wup{j}", [128, NCH, 512], BF16) for j in range(2)])
            wdnr = Ring([sbp(f"wdn{j}", [128, NPAIR, 128], BF16) for j in range(2)])
            aT = sbp("ffa", [128, NPAIR, 1024], BF16)
            ra = Res()
            zr = Ring([sbp(f"ffz{j}", [128, 1026], F32) for j in range(4)])
            tr_ = Ring([sbp(f"fft{j}", [128, 1024], F32) for j in range(4)])
            sg = Ring([sbp(f"ffs{j}", [128, 1024], F32) for j in range(2)])
            pH = Ring([psp("pH0", [128, 512])])
            pZ = Ring([psp(f"pZ{j}", [128, 512]) for j in range(4)])
            pYr = Ring([psp(f"pYf{j}", [128, 512]) for j in range(2)])

            def ffn_block(N, src_fm, r_src, dst_fm, r_dst, colL, colM, stream, zero_halo, first_blk=False, last_blk=False):
                nh = (N + 511) // 512
                hw = [min(512, N - hh * 512) for hh in range(nh)]
                P.dma("sp", DMA(xt[:, :, 0:N], src_fm[:, :, colM:colM + N]), r=r_src, w=[rx])
                A2, B2 = modv[:, l, 3, stream, :], modv[:, l, 4, stream, :]
                for hh in range(nh):
                    c0 = hh * 512
                    norm_mod(env, xt[:, :, c0:c0 + hw[hh]], rx, hw[hh], A2, B2, ht[:, :, c0:c0 + hw[hh]], rh)
                if zero_halo:
                    P.op("pool", MS(hh2[:], 0.0), w=[rhh])
                else:
                    P.dma("sp", DMA(xh[:, :, 0:1], src_fm[:, :, colL:colL + 1], allow_slow_non_contiguous=True), r=r_src + [R_xh], w=[rxh])
                    P.dma("sp", DMA(xh[:, :, 1:2], src_fm[:, :, colM + N:colM + N + 1], allow_slow_non_contiguous=True), r=r_src + [R_xh], w=[rxh])
                    norm_mod(env, xh, rxh, 2, A2, B2, hh2, rhh)
                    for c in range(NCH):
                        if first_blk:
                            P.op("dve", TS(hh2[:, c, 0:1], hh2[:, c, 0:1], flags[:, 0:1], None, ALU.mult), r=[rhh, R_cs[11]], w=[rhh])
                        if last_blk:
                            P.op("dve", TS(hh2[:, c, 1:2], hh2[:, c, 1:2], flags[:, 1:2], None, ALU.mult), r=[rhh, R_cs[11]], w=[rhh])
                wts = {}

                def get_w(j):
                    jj = j // 2
                    if jj not in wts:
                        wt, rw = wupr.next()
                        P.dma("sp", DMA(wt[:], wup_b.ap()[l].rearrange("(k p) n -> p k n", p=128)[:, :, jj * 512:(jj + 1) * 512]),
                              r=[R_w[("wup", l)]], w=[rw])
                        wts[jj] = (wt, rw)
                    wt, rw = wts[jj]
                    return wt, rw, (j % 2) * 256

                def gate(j, outs):
                    st, rs = sg.next()
                    P.op("act", ACT(st[:, :N], outs[0][0][:, :N], AF.Silu), r=[outs[0][1]], w=[rs])
                    P.op("pool", TT(aT[:, j, :N], st[:, :N], outs[1][0][:, :N], ALU.mult), r=[rs, outs[1][1]], w=[ra])

                pending = None
                ph_, rph = pH.next()
                for j in range(NPAIR):
                    wt, rw, wc = get_w(j)
                    for s_ in range(2):
                        ti = j * 2 + s_
                        for k in range(NCH):
                            P.op("pe", MM(ph_[:, ti * 2:ti * 2 + 2], wt[:, k, wc + s_ * 128:wc + (s_ + 1) * 128], hh2[:, k, :],
                                          start=(k == 0), stop=(k == NCH - 1)), r=[rw, rhh], w=[rph])
                    P.op("act", ACT(hzt[:, j * 4:j * 4 + 4], ph_[:, j * 4:j * 4 + 4], AF.Identity), r=[rph], w=[rhz])
                    outs = []
                    for s_ in range(2):
                        zt, rz = zr.next()
                        for hh in range(nh):
                            pz, rpz = pZ.next()
                            c0 = hh * 512
                            for k in range(NCH):
                                P.op("pe", MM(pz[:, :hw[hh]], wt[:, k, wc + s_ * 128:wc + (s_ + 1) * 128], ht[:, k, c0:c0 + hw[hh]],
                                              start=(k == 0), stop=(k == NCH - 1)), r=[rw, rh], w=[rpz])
                            P.op("act", ACT(zt[:, 1 + c0:1 + c0 + hw[hh]], pz[:, :hw[hh]], AF.Identity), r=[rpz], w=[rz])
                        ti = j * 2 + s_
                        P.op("pool", CP(zt[:, 0:1], hzt[:, ti * 2:ti * 2 + 1]), r=[rhz], w=[rz])
                        P.op("pool", CP(zt[:, N + 1:N + 2], hzt[:, ti * 2 + 1:ti * 2 + 2]), r=[rhz], w=[rz])
                        tt, rt = tr_.next()
                        P.op("dve", TS(tt[:, :N], zt[:, 1:N + 1], convp[:, l, ti, 1:2], convp[:, l, ti, 3:4], ALU.mult, ALU.add),
                             r=[rz, R_cs[10]], w=[rt])
                        P.op("dve", STT(tt[:, :N], zt[:, 0:N], convp[:, l, ti, 0:1], tt[:, :N], ALU.mult, ALU.add),
                             r=[rz, rt, R_cs[10]], w=[rt])
                        P.op("dve", STT(tt[:, :N], zt[:, 2:N + 2], convp[:, l, ti, 2:3], tt[:, :N], ALU.mult, ALU.add),
                             r=[rz, rt, R_cs[10]], w=[rt])
                        outs.append((tt, rt))
                    if pending is not None:
                        gate(*pending)
                    pending = (j, outs)
                gate(*pending)
                for fo in range(NCH):
                    wd, rwd = wdnr.next()
                    P.dma("sp", DMA(wd[:], wdn_b.ap()[l].rearrange("(j p) n -> p j n", p=128)[:, :, fo * 128:(fo + 1) * 128]),
                          r=[R_w[("wdn", l)]], w=[rwd])
                    for hh in range(nh):
                        c0 = hh * 512
                        py_, rpy = pYr.next()
                        for j in range(NPAIR):
                            P.op("pe", MM(py_[:, :hw[hh]], wd[:, j, :], aT[:, j, c0:c0 + hw[hh]],
                                          start=(j == 0), stop=(j == NPAIR - 1)), r=[rwd, ra], w=[rpy])
                        P.op("dve", STT(xt[:, fo, c0:c0 + hw[hh]], py_[:, :hw[hh]], modv[:, l, 5, stream, fo:fo + 1],
                                        xt[:, fo, c0:c0 + hw[hh]], ALU.mult, ALU.add), r=[rpy, rx, R_const], w=[rx])
                P.dma("sp", DMA(dst_fm[:, :, colM:colM + N], xt[:, :, 0:N]), r=[rx], w=r_dst)

            if with_ctx:
                ffn_block(CTX, cw_fm, [R_cw], cw_fm, [R_cw], 0, 0, 1, True)
            NBLK = max(1, TPC // 1024)
            BN = TPC // NBLK
            for b in range(NBLK):
                ks = [k for k in range(NB) if not ((k + 1) * 512 <= b * BN - 1 or k * 512 >= (b + 1) * BN + 1)]
                km = [k for k in range(NB) if b * BN <= k * 512 < (b + 1) * BN]
                ffn_block(BN, xin_fm, [R_xin[k] for k in ks], xout_fm, [R_xout[k] for k in km], b * BN, 1 + b * BN, 0, False,
                          first_blk=(b == 0), last_blk=(b == NBLK - 1))
            P.barrier()
        cur["i"] = 1 - ci

    def final_norm():
        with ExitStack() as ph:
            def sbp(name, shape, dt):
                return ph.enter_context(nc.sbuf_tensor(un(name), list(shape), dt))
            env = make_env(ph, "fin")
            xr = Ring([sbp(f"fnx{j}", [128, NCH, 512], F32) for j in range(2)])
            yr = Ring([sbp(f"fny{j}", [128, NCH, 512], F32) for j in range(2)])
            y_fm = yT_out.ap().rearrange("(c p) t -> p c t", p=128)
            xw_fm, R_xw = xw_fms[cur["i"]], R_xws[cur["i"]]
            for b in range(NB):
                xt, rx = xr.next()
                P.dma("sp", DMA(xt[:], xw_fm[:, :, 1 + b * 512:1 + (b + 1) * 512]), r=[R_xw[b]], w=[rx])
                yt, ry = yr.next()
                norm_mod(env, xt, rx, 512, finn, None, yt, ry)
                P.dma("sp", DMA(y_fm[:, :, b * 512:(b + 1) * 512], yt[:]), r=[ry], w=[Res()])

    import os
    dbg = os.environ.get("DBG_XMID")
    for l in range(NL):
        if l % 2 == 0:
            attention_layer(l)
        else:
            sgu_layer(l)
        if dbg:
            break
        halo_exchange(l)
        ffn_layer(l)
    if dbg:
        for b in range(NB):
            P.dma("sp", DMA(yT_out.ap()[:, b * 512:(b + 1) * 512], xw0.ap()[:, 1 + b * 512:1 + (b + 1) * 512]),
                  r=[R_xws[0][b]], w=[Res()])
    else:
        final_norm()
    P.finish()
    P.emit()
    es.close()
    return nc


def _fm(v):
    v = np.asarray(v, np.float32)
    return np.ascontiguousarray(v.reshape(-1, 128).T)


def prep_inputs(inp, TPC):
    f32 = lambda a: np.ascontiguousarray(np.asarray(a, dtype=np.float32))
    x = f32(inp["x"]); ctx = f32(inp["ctx"]); c = f32(inp["c"]); c_ctx = f32(inp["c_ctx"])
    S = x.shape[1]
    assert S == 4 * TPC
    shared = {}
    shared["ada_w"] = f32(inp["ada_w"])
    shared["ada_b"] = np.ascontiguousarray(np.stack([_fm(inp["ada_b"][l]) for l in range(DEPTH)], 1))
    shared["mixn"] = np.ascontiguousarray(np.stack([_fm(inp["mix_norm"][l]) for l in range(DEPTH)], 1))
    shared["ffnn"] = np.ascontiguousarray(np.stack([_fm(inp["ffn_norm"][l]) for l in range(DEPTH)], 1))
    shared["finn"] = _fm(inp["final_norm"])
    w_in = f32(inp["attn_w_in"])
    de = np.concatenate([np.arange(0, 64, 2), np.arange(1, 64, 2)])
    sw = np.concatenate([np.arange(1, 64, 2), np.arange(0, 64, 2)])
    off = {"q1": 0, "q2": 256, "k1": 512, "k2": 768, "va": 1024, "qb": 1536, "kb": 2048, "vb": 2176}

    def blk(name, h, perm):
        return off[name] + h * 64 + perm

    def tiles(perm):
        cols = []
        for h in range(4):
            cols += [blk("q1", h, perm), blk("q2", h, perm)]
        for h in range(4):
            cols += [blk("k1", h, perm), blk("k2", h, perm)]
        for j in range(4):
            cols += [blk("qb", j, perm), blk("qb", 4 + j, perm)]
        cols += [blk("kb", 0, perm), blk("kb", 1, perm)]
        return np.concatenate(cols)

    cn, cs = tiles(de), tiles(sw)
    shared["wqk"] = np.ascontiguousarray(w_in[:, :, cn])
    shared["wqs"] = np.ascontiguousarray(w_in[:, :, cs])
    vcols = np.concatenate([np.arange(1024, 1536), np.arange(2176, 2304)])
    shared["wv"] = np.ascontiguousarray(w_in[:, :, vcols])
    qn = f32(inp["gqa_q_norm"]); kn = f32(inp["gqa_k_norm"])
    gg = np.zeros((128, 2, 5, 2), np.float32)
    for i in range(2):
        for t in range(5):
            g = qn[i] if t < 4 else kn[i]
            gg[:, i, t, 0] = np.concatenate([g[de], g[de]])
            gg[:, i, t, 1] = np.concatenate([g[sw], g[sw]])
    shared["ggain"] = gg
    w_out = f32(inp["attn_w_out"])
    shared["wod"] = np.ascontiguousarray(w_out[:, 0:512, :])
    shared["wog"] = np.ascontiguousarray(w_out[:, 512:1024, :].reshape(2, 8, 64, D).transpose(0, 2, 1, 3))
    lq = np.stack([f32(inp["diff_lq1"]), f32(inp["diff_lk1"]), f32(inp["diff_lq2"]), f32(inp["diff_lk2"])], 1)
    shared["lqk"] = np.ascontiguousarray(lq[None])
    shared["subln"] = np.ascontiguousarray(f32(inp["diff_subln"]).T)
    swin = f32(inp["sgu_w_in"])
    shared["swu"] = np.ascontiguousarray(swin[:, :, 0:D])
    shared["swv"] = np.ascontiguousarray(swin[:, :, D:2 * D])
    shared["svn"] = np.ascontiguousarray(np.stack([_fm(inp["sgu_v_norm"][i]) for i in range(2)], 1))
    shared["swsT"] = np.ascontiguousarray(f32(inp["sgu_w_s"]).transpose(0, 3, 1, 2))
    shared["sbsb"] = np.ascontiguousarray(np.broadcast_to(f32(inp["sgu_b_s"])[None], (128, 2, 4, 128)))
    shared["swo"] = f32(inp["sgu_w_out"])
    wup = f32(inp["ffn_w_up"])
    shared["wup"] = np.ascontiguousarray(
        np.stack([wup[:, :, 0:FF].reshape(DEPTH, D, NPAIR, 128), wup[:, :, FF:2 * FF].reshape(DEPTH, D, NPAIR, 128)], 3)
        .reshape(DEPTH, D, NPAIR * 256))
    cwt = f32(inp["ffn_conv_w"]); cb = f32(inp["ffn_conv_b"])
    cp = np.zeros((128, DEPTH, 44, 4), np.float32)
    for l in range(DEPTH):
        for j in range(NPAIR):
            for s_ in range(2):
                f0 = s_ * FF + j * 128
                for k in range(3):
                    cp[:, l, j * 2 + s_, k] = cwt[l, k, f0:f0 + 128]
                cp[:, l, j * 2 + s_, 3] = cb[l, f0:f0 + 128]
    shared["convp"] = cp
    shared["wdn"] = f32(inp["ffn_w_down"])
    inv = (10000.0 ** (-np.arange(16, dtype=np.float32) / 16)).astype(np.float32)
    maps = []
    for core in range(8):
        b, r = core // 4, core % 4
        m = dict(shared)
        t0 = r * TPC
        m["xT"] = np.ascontiguousarray(x[b, t0:t0 + TPC, :].T)
        m["cT"] = np.ascontiguousarray(ctx[b].T)
        cv = np.zeros((128, NCH, 2), np.float32)
        cv[:, :, 0] = _fm(c[b])
        cv[:, :, 1] = _fm(c_ctx)
        m["cvec"] = cv
        t = np.arange(t0, t0 + TPC)
        row = (t // 64).astype(np.float32)
        col = (t % 64).astype(np.float32)
        ang = np.concatenate([row[None, :] * inv[:, None], col[None, :] * inv[:, None]], 0).astype(np.float32)
        cs_, sn_ = np.cos(ang).astype(np.float32), np.sin(ang).astype(np.float32)
        C64 = np.concatenate([cs_, cs_], 0)
        S64 = np.concatenate([-sn_, sn_], 0)
        m["ropeC"] = np.ascontiguousarray(np.concatenate([C64, C64], 0))
        m["ropeS"] = np.ascontiguousarray(np.concatenate([S64, S64], 0))
        fl = np.zeros((128, 2), np.float32)
        fl[:, 0] = 1.0 if r > 0 else 0.0
        fl[:, 1] = 1.0 if r < 3 else 0.0
        m["flags"] = fl
        sl = np.zeros((8, 2), np.float32)
        if r > 0:
            sl[2 * (r - 1) + 1, 0] = 1.0
        if r < 3:
            sl[2 * (r + 1), 1] = 1.0
        m["sel"] = sl
        maps.append(m)
    return maps


_NC_CACHE = {}


def run(inp, TPC, NL=DEPTH):
    key = (TPC, NL)
    if key not in _NC_CACHE:
        _NC_CACHE[key] = build(TPC, NL)
    nc = _NC_CACHE[key]
    maps = prep_inputs(inp, TPC)
    res = run_bass_kernel_spmd(nc, maps, core_ids=list(range(8)))
    B = 2
    out = np.zeros((B, 4 * TPC, D), np.float32)
    for core in range(8):
        b, r = core // 4, core % 4
        out[b, r * TPC:(r + 1) * TPC, :] = np.asarray(res.results[core]["yT"], np.float32).T
    return out


def kernel(**inputs):
    TPC = np.asarray(inputs["x"]).shape[1] // 4
    return run(inputs, TPC, DEPTH)
```
